# Optimizing a Trainium2 kernel written in Bass

```python
import math
import jax, jax.numpy as jnp
from jax import lax
import numpy as np

D_MODEL = 1024
BATCH = 8
SEQ = 8192
DEPTH = 2

CTX_LEN = 256
GRID_W = 64
D_FF = 2816
N_MOD = 9
ROPE_BASE = 10000.0
Q_BLOCK = 128
NEG_INF = -1e30
EPS = 1e-6
SUBLN_EPS = 1e-5

MLA_HEADS = 4
MLA_Q_RANK = 256
MLA_KV_RANK = 128
MLA_NOPE = 64
MLA_ROPE = 32
MLA_V = 64
MLA_SCALE = (MLA_NOPE + MLA_ROPE) ** -0.5
DIFF_HEADS = 4
DIFF_HEAD = 32
DIFF_V = 2 * DIFF_HEAD
DIFF_SCALE = DIFF_HEAD ** -0.5
NA_HEADS = 4
NA_HEAD = 64
NA_ROWS = 8
NA_COLS = 16
NA_SCALE = NA_HEAD ** -0.5
GQA_HEADS = 4
GQA_KV_HEADS = 2
GQA_HEAD = 64
WINDOW = 128
GQA_SCALE = GQA_HEAD ** -0.5

N_BRANCH = 4
BRANCH_W = 256
DIFF_QK_W = DIFF_HEADS * 2 * DIFF_HEAD
DIFF_V_W = DIFF_HEADS * DIFF_V
NA_W = NA_HEADS * NA_HEAD
GQA_Q_W = GQA_HEADS * GQA_HEAD
GQA_KV_W = GQA_KV_HEADS * GQA_HEAD
IN_SIZES = (MLA_Q_RANK, MLA_KV_RANK, MLA_ROPE,
            DIFF_QK_W, DIFF_QK_W, DIFF_V_W,
            NA_W, NA_W, NA_W,
            GQA_Q_W, GQA_KV_W, GQA_KV_W)
IN_COLS = 2464

kernel_name = "hybrid_parallel_mla_diff_natten_swa_dit"


def rmsnorm(x, g, eps=EPS):
    x32 = x.astype(jnp.float32)
    y = x32 * lax.rsqrt(jnp.mean(x32 * x32, axis=-1, keepdims=True) + eps)
    return (y * g.astype(jnp.float32)).astype(x.dtype)


def modulate(h, g, shift, scale):
    return rmsnorm(h, g) * (1 + scale) + shift


def swiglu(u, w_gu, w_down):
    g, up = jnp.split(u @ w_gu, 2, axis=-1)
    return (jax.nn.silu(g) * up) @ w_down


def softmax_f32(s):
    return jax.nn.softmax(s.astype(jnp.float32), axis=-1)


def axial_angles(n_tok, dim):
    t = jnp.arange(n_tok)
    rows = (t // GRID_W).astype(jnp.float32)
    cols = (t % GRID_W).astype(jnp.float32)
    half = dim // 2
    freqs = jnp.power(ROPE_BASE, -jnp.arange(0, half, 2, dtype=jnp.float32) / half)
    return rows[:, None] * freqs, cols[:, None] * freqs


def rope_1d(x, ang):
    x1, x2 = jnp.split(x, 2, axis=-1)
    cos, sin = jnp.cos(ang), jnp.sin(ang)
    return jnp.concatenate([x1 * cos - x2 * sin, x2 * cos + x1 * sin], axis=-1)


def rope_2d(x, angles):
    ang_r, ang_c = angles
    shape = (ang_r.shape[0],) + (1,) * (x.ndim - 3) + (ang_r.shape[1],)
    x32 = x.astype(jnp.float32)
    d = x.shape[-1]
    out = jnp.concatenate([rope_1d(x32[..., : d // 2], ang_r.reshape(shape)),
                           rope_1d(x32[..., d // 2:], ang_c.reshape(shape))], axis=-1)
    return out.astype(x.dtype)


def attend(q, k, v, scale):
    s = jnp.einsum('bqhd,bkhd->bhqk', q, k) * scale
    p = softmax_f32(s).astype(v.dtype)
    return jnp.einsum('bhqk,bkhe->bqhe', p, v)


def diff_attend(q, k, v, lam, scale):
    s = jnp.einsum('bqhnd,bkhnd->bnhqk', q, k) * scale
    p = softmax_f32(s)
    pd = (p[:, 0] - lam * p[:, 1]).astype(v.dtype)
    return jnp.einsum('bhqk,bkhe->bqhe', pd, v)


def sweep_query_blocks(fn, q):
    B, S = q.shape[:2]
    nb = S // Q_BLOCK
    qb = jnp.moveaxis(q.reshape((B, nb, Q_BLOCK) + q.shape[2:]), 1, 0)
    out = lax.map(fn, qb)
    return jnp.moveaxis(out, 0, 1).reshape((B, S) + out.shape[3:])


def neighbourhood_attend(q, k, v, k_ctx, v_ctx, rpb):
    B, S, H, d = q.shape
    L = k_ctx.shape[1]
    rows = S // GRID_W
    kh = min(NA_ROWS, rows)
    n_cb = GRID_W // NA_COLS
    span = 2 * NA_COLS
    qcols = np.arange(GRID_W).reshape(n_cb, NA_COLS)
    col_start = np.clip(np.arange(n_cb) * NA_COLS - NA_COLS // 2, 0, GRID_W - span)
    kcols = col_start[:, None] + np.arange(span)
    cs = np.clip(qcols - NA_COLS // 2, 0, GRID_W - NA_COLS)
    kc = kcols[:, None, :]
    col_ok = (kc >= cs[..., None]) & (kc < cs[..., None] + NA_COLS)
    dc = kc - qcols[..., None] + (NA_COLS - 1)
    mask = jnp.asarray(np.broadcast_to(col_ok[:, :, None, :], (n_cb, NA_COLS, kh, span))
                       .reshape(n_cb, NA_COLS, kh * span))
    kg = k.reshape(B, rows, GRID_W, H, d)
    vg = v.reshape(B, rows, GRID_W, H, d)
    qg = jnp.moveaxis(q.reshape(B, rows, n_cb, NA_COLS, H, d), 1, 0)

    def row_fn(args):
        r, q_row = args
        rs = jnp.clip(r - kh // 2, 0, rows - kh)
        k_rows = lax.dynamic_slice_in_dim(kg, rs, kh, axis=1)[:, :, kcols]
        v_rows = lax.dynamic_slice_in_dim(vg, rs, kh, axis=1)[:, :, kcols]
        kb = jnp.moveaxis(k_rows, 2, 1).reshape(B, n_cb, kh * span, H, d)
        vb = jnp.moveaxis(v_rows, 2, 1).reshape(B, n_cb, kh * span, H, d)
        dr = rs + jnp.arange(kh) - r + (NA_ROWS - 1)
        bias = rpb[:, dr[None, None, :, None], dc[:, :, None, :]]
        bias = bias.reshape(H, n_cb, NA_COLS, kh * span).astype(jnp.float32)
        s_lat = jnp.einsum('bjqhd,bjkhd->bhjqk', q_row, kb).astype(jnp.float32) * NA_SCALE + bias[None]
        s_lat = jnp.where(mask[None, None], s_lat, NEG_INF)
        s_ctx = jnp.einsum('bjqhd,bkhd->bhjqk', q_row, k_ctx).astype(jnp.float32) * NA_SCALE
        p = softmax_f32(jnp.concatenate([s_ctx, s_lat], axis=-1)).astype(v.dtype)
        return (jnp.einsum('bhjqk,bkhe->bjqhe', p[..., :L], v_ctx)
                + jnp.einsum('bhjqk,bjkhe->bjqhe', p[..., L:], vb))

    out = lax.map(row_fn, (jnp.arange(rows), qg))
    return jnp.moveaxis(out, 0, 1).reshape(B, S, H, d)


def window_attend(q, k, v, k_ctx, v_ctx, sink):
    B, S, H, d = q.shape
    kvh = k.shape[2]
    g = H // kvh
    L = k_ctx.shape[1]
    nb = S // WINDOW
    span = 3 * WINDOW
    pad = ((0, 0), (WINDOW, WINDOW), (0, 0), (0, 0))
    kp, vp = jnp.pad(k, pad), jnp.pad(v, pad)
    qb = jnp.moveaxis(q.reshape(B, nb, WINDOW, kvh, g, d), 1, 0)
    sink_col = jnp.broadcast_to(sink.astype(jnp.float32).reshape(1, kvh, g, 1, 1), (B, kvh, g, WINDOW, 1))

    def blk_fn(args):
        n, q_blk = args
        kb = lax.dynamic_slice_in_dim(kp, n * WINDOW, span, axis=1)
        vb = lax.dynamic_slice_in_dim(vp, n * WINDOW, span, axis=1)
        qpos = n * WINDOW + jnp.arange(WINDOW)
        kpos = n * WINDOW - WINDOW + jnp.arange(span)
        ok = ((jnp.abs(qpos[:, None] - kpos[None, :]) <= WINDOW)
              & (kpos >= 0)[None, :] & (kpos < S)[None, :])
        s_lat = jnp.einsum('bqgrd,bkgd->bgrqk', q_blk, kb).astype(jnp.float32) * GQA_SCALE
        s_lat = jnp.where(ok, s_lat, NEG_INF)
        s_ctx = jnp.einsum('bqgrd,bkgd->bgrqk', q_blk, k_ctx).astype(jnp.float32) * GQA_SCALE
        p = softmax_f32(jnp.concatenate([sink_col, s_ctx, s_lat], axis=-1)).astype(v.dtype)
        o = (jnp.einsum('bgrqk,bkge->bqgre', p[..., 1:1 + L], v_ctx)
             + jnp.einsum('bgrqk,bkge->bqgre', p[..., 1 + L:], vb))
        return o.reshape(B, WINDOW, H, d)

    out = lax.map(blk_fn, (jnp.arange(nb), qb))
    return jnp.moveaxis(out, 0, 1).reshape(B, S, H, d)


def sink_attend(q, k, v, sink):
    B, L, H, d = q.shape
    kvh = k.shape[2]
    g = H // kvh
    qg = q.reshape(B, L, kvh, g, d)
    s = jnp.einsum('bqgrd,bkgd->bgrqk', qg, k).astype(jnp.float32) * GQA_SCALE
    sink_col = jnp.broadcast_to(sink.astype(jnp.float32).reshape(1, kvh, g, 1, 1), (B, kvh, g, L, 1))
    p = softmax_f32(jnp.concatenate([sink_col, s], axis=-1)).astype(v.dtype)
    return jnp.einsum('bgrqk,bkge->bqgre', p[..., 1:], v).reshape(B, L, H, d)


def split_cols(p):
    offs = np.cumsum(IN_SIZES)[:-1].tolist()
    return jnp.split(p, offs, axis=-1)


def mixer_inputs(p, rope, mla_q_norm, mla_w_uq, mla_kv_norm, mla_w_ukv):
    cq, ckv, kr, dq, dk, dv, nq, nk, nv, gq, gk, gv = split_cols(p)
    lead = p.shape[:-1]

    def rot(x, i):
        return x if rope is None else rope_2d(x, rope[i])

    qa = (rmsnorm(cq, mla_q_norm) @ mla_w_uq).reshape(lead + (MLA_HEADS, MLA_NOPE + MLA_ROPE))
    qa = jnp.concatenate([qa[..., :MLA_NOPE], rot(qa[..., MLA_NOPE:], 0)], axis=-1)
    kva = (rmsnorm(ckv, mla_kv_norm) @ mla_w_ukv).reshape(lead + (MLA_HEADS, MLA_NOPE + MLA_V))
    k_rope = rot(kr[..., None, :], 0)
    ka = jnp.concatenate([kva[..., :MLA_NOPE],
                          jnp.broadcast_to(k_rope, lead + (MLA_HEADS, MLA_ROPE))], axis=-1)
    va = kva[..., MLA_NOPE:]
    qb = rot(dq.reshape(lead + (DIFF_HEADS, 2, DIFF_HEAD)), 0)
    kb = rot(dk.reshape(lead + (DIFF_HEADS, 2, DIFF_HEAD)), 0)
    vb = dv.reshape(lead + (DIFF_HEADS, DIFF_V))
    qc = nq.reshape(lead + (NA_HEADS, NA_HEAD))
    kc = nk.reshape(lead + (NA_HEADS, NA_HEAD))
    vc = nv.reshape(lead + (NA_HEADS, NA_HEAD))
    qd = rot(gq.reshape(lead + (GQA_HEADS, GQA_HEAD)), 1)
    kd = rot(gk.reshape(lead + (GQA_KV_HEADS, GQA_HEAD)), 1)
    vd = gv.reshape(lead + (GQA_KV_HEADS, GQA_HEAD))
    return (qa, ka, va, qb, kb, vb, qc, kc, vc, qd, kd, vd)


def gated_merge(u, ys, w_branch, w_gate, b_gate, w_out):
    terms = []
    for i, y in enumerate(ys):
        y = y.reshape(y.shape[:2] + (BRANCH_W,))
        terms.append(jax.nn.sigmoid(u @ w_gate[i] + b_gate[i]) * (y @ w_branch[i]))
    merged = terms[0]
    for t in terms[1:]:
        merged = merged + t
    return merged @ w_out


def token_mix(u_lat, u_ctx, rope, lam_init, w_in, mla_q_norm, mla_w_uq, mla_kv_norm, mla_w_ukv,
              diff_lam, diff_subln, na_rpb, gqa_sink, w_branch, w_gate, b_gate, w_out, ctx_out):
    qa, ka, va, qb, kb, vb, qc, kc, vc, qd, kd, vd = mixer_inputs(
        u_lat @ w_in, rope, mla_q_norm, mla_w_uq, mla_kv_norm, mla_w_ukv)
    qa_c, ka_c, va_c, qb_c, kb_c, vb_c, qc_c, kc_c, vc_c, qd_c, kd_c, vd_c = mixer_inputs(
        u_ctx @ w_in, None, mla_q_norm, mla_w_uq, mla_kv_norm, mla_w_ukv)
    dl = diff_lam.astype(jnp.float32)
    lam = jnp.exp(jnp.sum(dl[0] * dl[1])) - jnp.exp(jnp.sum(dl[2] * dl[3])) + lam_init

    ka_all = jnp.concatenate([ka_c, ka], axis=1)
    va_all = jnp.concatenate([va_c, va], axis=1)
    ya = sweep_query_blocks(lambda qblk: attend(qblk, ka_all, va_all, MLA_SCALE), qa)
    kb_all = jnp.concatenate([kb_c, kb], axis=1)
    vb_all = jnp.concatenate([vb_c, vb], axis=1)
    yb = sweep_query_blocks(lambda qblk: diff_attend(qblk, kb_all, vb_all, lam, DIFF_SCALE), qb)
    yb = rmsnorm(yb, diff_subln, SUBLN_EPS) * (1 - lam_init)
    yc = neighbourhood_attend(qc, kc, vc, kc_c, vc_c, na_rpb)
    yd = window_attend(qd, kd, vd, kd_c, vd_c, gqa_sink)
    y_lat = gated_merge(u_lat, (ya, yb, yc, yd), w_branch, w_gate, b_gate, w_out)
    if not ctx_out:
        return y_lat, None
    ya_c = attend(qa_c, ka_c, va_c, MLA_SCALE)
    yb_c = rmsnorm(diff_attend(qb_c, kb_c, vb_c, lam, DIFF_SCALE), diff_subln, SUBLN_EPS) * (1 - lam_init)
    yc_c = attend(qc_c, kc_c, vc_c, NA_SCALE)
    yd_c = sink_attend(qd_c, kd_c, vd_c, gqa_sink)
    y_ctx = gated_merge(u_ctx, (ya_c, yb_c, yc_c, yd_c), w_branch, w_gate, b_gate, w_out)
    return y_lat, y_ctx


def setup_inputs(seed: int = 0) -> dict:
    key = jax.random.key(seed)
    ks = jax.random.split(key, 32)
    f32 = jnp.float32

    def nrm(k, shape, scale):
        return jax.random.normal(k, shape, f32) * scale

    def gain(k, shape):
        return 1.0 + nrm(k, shape, 0.05)

    D = D_MODEL
    return {
        "x": nrm(ks[0], (BATCH, SEQ, D), 1.0),
        "c": nrm(ks[1], (BATCH, D), 1.0),
        "ctx": nrm(ks[2], (BATCH, CTX_LEN, D), 1.0),
        "c_ctx": nrm(ks[3], (D,), 1.0),
        "w_ada": nrm(ks[4], (DEPTH, D, N_MOD * D), 0.5 * D ** -0.5),
        "b_ada": nrm(ks[5], (DEPTH, N_MOD * D), 0.01),
        "norm_ffn1": gain(ks[6], (DEPTH, D)),
        "ffn1_w_gu": nrm(ks[7], (DEPTH, D, 2 * D_FF), D ** -0.5),
        "ffn1_w_down": nrm(ks[8], (DEPTH, D_FF, D), D_FF ** -0.5),
        "norm_mix": gain(ks[9], (DEPTH, D)),
        "w_in": nrm(ks[10], (DEPTH, D, IN_COLS), D ** -0.5),
        "mla_q_norm": gain(ks[11], (DEPTH, MLA_Q_RANK)),
        "mla_w_uq": nrm(ks[12], (DEPTH, MLA_Q_RANK, MLA_HEADS * (MLA_NOPE + MLA_ROPE)), MLA_Q_RANK ** -0.5),
        "mla_kv_norm": gain(ks[13], (DEPTH, MLA_KV_RANK)),
        "mla_w_ukv": nrm(ks[14], (DEPTH, MLA_KV_RANK, MLA_HEADS * (MLA_NOPE + MLA_V)), MLA_KV_RANK ** -0.5),
        "diff_lam": nrm(ks[15], (DEPTH, 4, DIFF_HEAD), 0.1),
        "diff_subln": gain(ks[16], (DEPTH, DIFF_V)),
        "na_rpb": nrm(ks[17], (DEPTH, NA_HEADS, 2 * NA_ROWS - 1, 2 * NA_COLS - 1), 0.1),
        "gqa_sink": nrm(ks[18], (DEPTH, GQA_HEADS), 0.5),
        "w_branch": nrm(ks[19], (DEPTH, N_BRANCH, BRANCH_W, D), BRANCH_W ** -0.5),
        "w_gate": nrm(ks[20], (DEPTH, N_BRANCH, D, D), D ** -0.5),
        "b_gate": nrm(ks[21], (DEPTH, N_BRANCH, D), 0.01),
        "w_out": nrm(ks[22], (DEPTH, D, D), D ** -0.5),
        "norm_ffn2": gain(ks[23], (DEPTH, D)),
        "ffn2_w_gu": nrm(ks[24], (DEPTH, D, 2 * D_FF), D ** -0.5),
        "ffn2_w_down": nrm(ks[25], (DEPTH, D_FF, D), D_FF ** -0.5),
        "final_norm": gain(ks[26], (D,)),
    }


def reference(x, c, ctx, c_ctx, w_ada, b_ada, norm_ffn1, ffn1_w_gu, ffn1_w_down, norm_mix, w_in,
              mla_q_norm, mla_w_uq, mla_kv_norm, mla_w_ukv, diff_lam, diff_subln, na_rpb, gqa_sink,
              w_branch, w_gate, b_gate, w_out, norm_ffn2, ffn2_w_gu, ffn2_w_down, final_norm):
    B, S, D = x.shape
    rope = (axial_angles(S, MLA_ROPE), axial_angles(S, GQA_HEAD))
    s_c = jax.nn.silu(c)
    s_cc = jax.nn.silu(c_ctx)
    h, hc = x, ctx
    for l in range(DEPTH):
        last = l == DEPTH - 1
        lam_init = 0.8 - 0.6 * math.exp(-0.3 * l)
        mod = (s_c @ w_ada[l] + b_ada[l]).reshape(B, N_MOD, 1, D)
        mod_c = (s_cc @ w_ada[l] + b_ada[l]).reshape(N_MOD, D)
        h = h + 0.5 * mod[:, 2] * swiglu(modulate(h, norm_ffn1[l], mod[:, 0], mod[:, 1]),
                                         ffn1_w_gu[l], ffn1_w_down[l])
        hc = hc + 0.5 * mod_c[2] * swiglu(modulate(hc, norm_ffn1[l], mod_c[0], mod_c[1]),
                                          ffn1_w_gu[l], ffn1_w_down[l])
        u = modulate(h, norm_mix[l], mod[:, 3], mod[:, 4])
        uc = modulate(hc, norm_mix[l], mod_c[3], mod_c[4])
        y, yc = token_mix(u, uc, rope, lam_init, w_in[l], mla_q_norm[l], mla_w_uq[l], mla_kv_norm[l],
                          mla_w_ukv[l], diff_lam[l], diff_subln[l], na_rpb[l], gqa_sink[l],
                          w_branch[l], w_gate[l], b_gate[l], w_out[l], not last)
        h = h + mod[:, 5] * y
        h = h + 0.5 * mod[:, 8] * swiglu(modulate(h, norm_ffn2[l], mod[:, 6], mod[:, 7]),
                                         ffn2_w_gu[l], ffn2_w_down[l])
        if not last:
            hc = hc + mod_c[5] * yc
            hc = hc + 0.5 * mod_c[8] * swiglu(modulate(hc, norm_ffn2[l], mod_c[6], mod_c[7]),
                                              ffn2_w_gu[l], ffn2_w_down[l])
    return rmsnorm(h, final_norm)
```

```python
import math
import os
from contextlib import ExitStack

import numpy as np
import concourse.bass as bass
import concourse.mybir as mybir
from concourse.bass_utils import run_bass_kernel_spmd

F32 = mybir.dt.float32
BF16 = mybir.dt.bfloat16
AF = mybir.ActivationFunctionType
ALU = mybir.AluOpType

D = 1024
S = 8192
L = 256
NT = S + L
DEPTH = 2
DFF = 2816
NJ = DFF // 128
GRID_W = 64
EPS = 1e-6
SUBLN_EPS = 1e-5
MLA_SCALE = 96 ** -0.5
DIFF_SCALE = 32 ** -0.5
NA_SCALE = 64 ** -0.5
GQA_SCALE = 64 ** -0.5
NEG = -30000.0
NDS = 24
DBG_STEP = int(os.environ.get('DBG_STEP', '99'))
DBG_TILES = int(os.environ.get('DBG_TILES', '99'))
NWIN = 2464 + 928 + 640

TILES = [(0, L)] + [(L + 512 * i, 512) for i in range(S // 512)]


class Tk:
    __slots__ = ("w", "r", "t")

    def __init__(self, t=None):
        self.w = {}
        self.r = {}
        self.t = t


class Eng:
    def __init__(self, name, eng, sem, self_sync=True):
        self.name = name
        self.eng = eng
        self.sem = sem
        self.cnt = 0
        self.seen = {}
        self.self_sync = self_sync
        self.dsems = None


class KB:
    def __init__(self, nc, es):
        self.nc = nc
        self.es = es

        def mk(name, eng, self_sync=True):
            return Eng(name, eng, es.enter_context(nc.semaphore("s_" + name)), self_sync)

        self.pe = mk("pe", nc.tensor, False)
        self.act = mk("act", nc.scalar)
        self.dve = mk("dve", nc.vector)
        self.pool = mk("pool", nc.gpsimd)
        self.sp = mk("sp", nc.sync)
        self.engs = [self.pe, self.act, self.dve, self.pool, self.sp]
        self.queues = [self.sp, self.pool]
        for q in self.queues:
            q.dsems = [es.enter_context(nc.semaphore(f"d_{q.name}_{i}")) for i in range(NDS)]
            q.dvals = [0] * NDS
            q.dnext = 0
        self.pfull = es.enter_context(nc.psum_tensor("psfull", [128, 4096], F32))
        self.psum = [Tk(self.pfull[:, 512 * i:512 * (i + 1)]) for i in range(8)]

    def _deps(self, E, reads, writes):
        deps = {}

        def add(d):
            for key, (val, semh) in d.items():
                if key == E.name and not E.self_sync:
                    continue
                if deps.get(key, (0, None))[0] < val:
                    deps[key] = (val, semh)

        for t in reads:
            add(t.w)
        for t in writes:
            add(t.w)
            add(t.r)
        for key, (val, semh) in deps.items():
            if E.seen.get(key, 0) < val:
                E.eng.wait_ge(semh, val)
                E.seen[key] = val

    def op(self, E, fn, reads=(), writes=(), nosync=()):
        self._deps(E, reads, writes)
        ins = fn()
        E.cnt += 1
        ins.then_inc(E.sem, 1)
        tok = (E.cnt, E.sem)
        for t in reads:
            t.r[E.name] = tok
        for t in writes:
            t.w[E.name] = tok
        for t in nosync:
            t.w[E.name] = tok

    def dma(self, Q, out, in_, reads=(), writes=(), nosync=()):
        self._deps(Q, reads, writes)
        j = Q.dnext % NDS
        Q.dnext += 1
        key = ("d", Q.name, j)
        if Q.seen.get(key, 0) < Q.dvals[j]:
            Q.eng.wait_ge(Q.dsems[j], Q.dvals[j])
            Q.seen[key] = Q.dvals[j]
        Q.dvals[j] += 16
        Q.eng.dma_start(out=out, in_=in_).then_inc(Q.dsems[j], 16)
        tok = (Q.dvals[j], Q.dsems[j])
        for t in reads:
            t.r[key] = tok
        for t in writes:
            t.w[key] = tok
        for t in nosync:
            t.w[key] = tok

    def barrier(self):
        for E in self.engs:
            for F in self.engs:
                if F is E and not E.self_sync:
                    continue
                if E.seen.get(F.name, 0) < F.cnt:
                    E.eng.wait_ge(F.sem, F.cnt)
                    E.seen[F.name] = F.cnt
            for Q in self.queues:
                for j in range(NDS):
                    key = ("d", Q.name, j)
                    if E.seen.get(key, 0) < Q.dvals[j]:
                        E.eng.wait_ge(Q.dsems[j], Q.dvals[j])
                        E.seen[key] = Q.dvals[j]

    def sb(self, es, name, shape, dt):
        self.uid = getattr(self, "uid", 0) + 1
        return Tk(es.enter_context(self.nc.sbuf_tensor(f"{name}_u{self.uid}", shape, dt)))

    def mm(self, out_tk, out_ap, lhsT, rhs, start, stop, reads=(), **kw):
        nc = self.nc
        self.op(self.pe, lambda: nc.tensor.matmul(out_ap, lhsT, rhs, start=start, stop=stop, **kw),
                reads=reads, writes=[out_tk])

    def load_cast(self, stg, dst_tk, dst_ap, src_ap, idx):
        nc = self.nc
        s = stg[idx % len(stg)]
        shp = list(src_ap.shape)
        n = int(np.prod(shp[1:]))
        if len(shp) == 3:
            sview = s.t[:, 0:n].rearrange("p (a b) -> p a b", b=shp[2])
        else:
            sview = s.t[:, 0:n]
        self.dma(self.sp, sview, src_ap, writes=[s])
        E = self.pool if idx % 2 == 0 else self.dve
        if E is self.pool:
            self.op(E, lambda: nc.gpsimd.tensor_copy(out=dst_ap, in_=sview), reads=[s], nosync=[dst_tk])
        else:
            self.op(E, lambda: nc.vector.tensor_copy(out=dst_ap, in_=sview), reads=[s], nosync=[dst_tk])


class Prog:
    def __init__(self, debug=(), stop_after=None):
        self.debug = set(debug)
        self.stop_after = stop_after
        self.nc = bass.Bass("TRN2", target_bir_lowering=False)
        self.inputs = {}

    def din(self, name, shape, dt=F32):
        a = self.nc.dram_tensor(name, list(shape), dt, kind="ExternalInput").ap()
        self.inputs[name] = a
        return a

    def dscr(self, name, shape, dt):
        kind = "ExternalOutput" if name in self.debug else "Internal"
        return self.nc.dram_tensor(name, list(shape), dt, kind=kind).ap()

    def build(self):
        nc = self.nc
        self.hT = self.din("hT", [D, NT])
        self.c_in = self.din("c_in", [128, 8, 2])
        self.w_ada = self.din("w_ada", [DEPTH, D, 9 * D])
        self.b_ada = self.din("b_ada", [DEPTH, 128, 72])
        self.norms = self.din("norms", [DEPTH, 128, 3, 8])
        self.fnorm = self.din("fnorm", [128, 8])
        self.w_gu = [self.din("ffn1_w_gu", [DEPTH, D, 2 * DFF]), self.din("ffn2_w_gu", [DEPTH, D, 2 * DFF])]
        self.w_dn = [self.din("ffn1_w_down", [DEPTH, DFF, D]), self.din("ffn2_w_down", [DEPTH, DFF, D])]
        self.out = self.nc.dram_tensor("outT", [D, S], F32, kind="ExternalOutput").ap()
        self.h = self.dscr("h_scr", [D, NT], F32)
        self.h_tk = [Tk() for _ in TILES]
        self.w_in = self.din("w_in_ext", [DEPTH, D, NWIN])
        self.w_uq = self.din("w_uq_ext", [DEPTH, 256, 768])
        self.w_ukv = self.din("w_ukv_ext", [DEPTH, 128, 512])
        self.qkn = self.din("qkn", [DEPTH, 128, 3])
        self.rope = self.din("rope_tab", [4, 128, NT])
        self.lam_in = self.din("diff_lam", [DEPTH, 1, 128])
        self.subln = self.din("diff_subln", [DEPTH, 64, 1])
        self.sink_in = self.din("gqa_sink", [DEPTH, 1, 4])
        self.na_bias = self.din("na_bias", [DEPTH, 3, 4, 128, 8, 512])
        self.win_mask = self.din("win_mask", [128, 6, 512])
        self.w_gate = self.din("w_gate", [DEPTH, 4, D, D])
        self.w_branch = self.din("w_branch", [DEPTH, 4, 256, D])
        self.w_out = self.din("w_out", [DEPTH, D, D])
        self.b_gate = self.din("b_gate", [DEPTH, 128, 4, 8])
        self.U = self.dscr("U_scr", [D, NT], BF16)
        self.QA = self.dscr("QA", [4, 96, NT], BF16)
        self.KA = self.dscr("KA", [4, 96, NT], BF16)
        self.VA = self.dscr("VA", [NT, 260], BF16)
        self.QB = self.dscr("QB", [2, 128, NT], BF16)
        self.KB_ = self.dscr("KB", [2, 128, NT], BF16)
        self.VB = self.dscr("VB", [NT, 260], BF16)
        self.QC = self.dscr("QC", [2, 128, NT], BF16)
        self.KC = self.dscr("KC", [2, 128, NT], BF16)
        self.VC = self.dscr("VC", [NT, 260], BF16)
        self.QD = self.dscr("QD", [2, 128, NT], BF16)
        self.KD = self.dscr("KD", [2, 128, NT], BF16)
        self.VD = self.dscr("VD", [NT, 130], BF16)
        self.Y = self.dscr("Y_scr", [4, 256, NT], BF16)

        with ExitStack() as es:
            K = self.K = KB(nc, es)
            self.ones_bf = K.sb(es, "ones_bf", [128, 128], BF16)
            self.ones_f = K.sb(es, "ones_f", [128, 128], F32)
            self.gs = K.sb(es, "gs", [128, DEPTH, 2, 3, 8], F32)
            self.sh = K.sb(es, "sh", [128, DEPTH, 2, 3, 8], F32)
            self.gt = K.sb(es, "gt", [128, DEPTH, 2, 3, 8], F32)
            self.eps_t = K.sb(es, "eps_t", [128, 4], F32)
            K.op(K.dve, lambda: nc.vector.memset(self.ones_bf.t[:], 1.0), writes=[self.ones_bf])
            K.op(K.dve, lambda: nc.vector.memset(self.ones_f.t[:], 0.0), writes=[self.ones_f])
            K.op(K.dve, lambda: nc.vector.memset(self.ones_f.t[64:65, :], 1.0), writes=[self.ones_f])
            K.op(K.dve, lambda: nc.vector.memset(self.eps_t.t[:, 0:1], EPS), writes=[self.eps_t])
            K.op(K.dve, lambda: nc.vector.memset(self.eps_t.t[:, 1:2], SUBLN_EPS), writes=[self.eps_t])
            for l_ in range(DEPTH):
                li_ = 1.0 - (0.8 - 0.6 * math.exp(-0.3 * l_))
                K.op(K.dve, lambda l_=l_, li_=li_: nc.vector.memset(self.eps_t.t[:, 2 + l_:3 + l_], SUBLN_EPS / (li_ * li_)),
                     writes=[self.eps_t])
            self.phase_mod()
            K.barrier()
            if self.stop_after == "mod":
                return self.finish()
            first = True
            for l in range(DEPTH):
                last = l == DEPTH - 1
                self.phase_ffn(l, 0, src_is_input=first, skip_ctx=False)
                first = False
                K.barrier()
                if self.stop_after == f"ffn1_{l}":
                    return self.finish()
                self.phase_proj(l)
                K.barrier()
                if self.stop_after == f"proj_{l}":
                    return self.finish()
                for mx in "ABCD":
                    self.phase_attn(l, mx, with_ctx=not last)
                    K.barrier()
                    if self.stop_after == f"attn{mx}_{l}":
                        return self.finish()
                self.phase_merge(l, with_ctx=not last)
                K.barrier()
                if self.stop_after == f"merge_{l}":
                    return self.finish()
                self.phase_ffn(l, 1, src_is_input=False, skip_ctx=last)
                K.barrier()
            self.phase_final()
            K.barrier()
        return self.finish()

    def finish(self):
        self.K.barrier()
        return self.nc

    def phase_mod(self):
        nc, K = self.nc, self.K
        with ExitStack() as es:
            s_in = K.sb(es, "s_in", [128, 8, 2], F32)
            wst = [K.sb(es, f"wada{i}", [128, 8, 1024], F32) for i in range(2)]
            bsb = K.sb(es, "bsb", [128, DEPTH, 72], F32)
            nsb = K.sb(es, "nsb", [128, DEPTH, 3, 8], F32)
            modsb = K.sb(es, "modsb", [128, DEPTH, 2, 72], F32)
            K.dma(K.sp, s_in.t[:], self.c_in[:, :, :], writes=[s_in])
            K.dma(K.sp, bsb.t[:], self.b_ada.rearrange("l p j -> p l j"), writes=[bsb])
            K.dma(K.sp, nsb.t[:], self.norms.rearrange("l p s k -> p l s k"), writes=[nsb])
            K.op(K.act, lambda: nc.scalar.activation(out=s_in.t[:], in_=s_in.t[:], func=AF.Silu), reads=[s_in], writes=[s_in])
            ps = K.psum[0]
            it = 0
            for l in range(DEPTH):
                for g in range(9):
                    w = wst[it % 2]
                    it += 1
                    src = self.w_ada[l, :, g * 1024:(g + 1) * 1024].rearrange("(k p) n -> p k n", p=128)
                    for half in range(2):
                        K.dma(K.sp, w.t[:, half * 4:(half + 1) * 4, :], src[:, half * 4:(half + 1) * 4, :],
                              writes=[w] if half == 0 else [], nosync=[w] if half == 1 else [])
                    for n in range(8):
                        col = (l * 72 + g * 8 + n) * 2
                        for k in range(8):
                            K.mm(ps, ps.t[:, col:col + 2], w.t[:, k, n * 128:(n + 1) * 128], s_in.t[:, k, :],
                                 start=(k == 0), stop=(k == 7), reads=[w, s_in])
            psv = ps.t[:, 0:DEPTH * 72 * 2].rearrange("p (l j v) -> p l j v", l=DEPTH, v=2)
            for l in range(DEPTH):
                for v in range(2):
                    K.op(K.dve, lambda l=l, v=v: nc.vector.tensor_tensor(out=modsb.t[:, l, v, :], in0=psv[:, l, :, v],
                                                                        in1=bsb.t[:, l, :], op=ALU.add),
                         reads=[ps, bsb], writes=[modsb])
            for l in range(DEPTH):
                for v in range(2):
                    for s in range(3):
                        o = 3 * s * 8
                        K.op(K.dve, lambda l=l, v=v, s=s, o=o: nc.vector.scalar_tensor_tensor(
                            out=self.gs.t[:, l, v, s, :], in0=modsb.t[:, l, v, o + 8:o + 16], scalar=1.0,
                            in1=nsb.t[:, l, s, :], op0=ALU.add, op1=ALU.mult),
                            reads=[modsb, nsb], writes=[self.gs])
                        K.op(K.dve, lambda l=l, v=v, s=s, o=o: nc.vector.tensor_copy(
                            out=self.sh.t[:, l, v, s, :], in_=modsb.t[:, l, v, o:o + 8]),
                            reads=[modsb], writes=[self.sh])
                        K.op(K.dve, lambda l=l, v=v, s=s, o=o: nc.vector.tensor_scalar(
                            out=self.gt.t[:, l, v, s, :], in0=modsb.t[:, l, v, o + 16:o + 24],
                            scalar1=(1.0 if s == 1 else 0.5), scalar2=None, op0=ALU.mult),
                            reads=[modsb], writes=[self.gt])
            if "mod_dbg" in self.debug:
                dbg = self.nc.dram_tensor("mod_dbg", [128, DEPTH * 2 * 72], F32, kind="ExternalOutput").ap()
                K.dma(K.sp, dbg[:, :], modsb.t[:].rearrange("p l v j -> p (l v j)"), reads=[modsb])

    def norm_mod(self, es_bufs, xin, n, gs_ap, sh_ap, u, nk=8):
        nc, K = self.nc, self.K
        sq, xn, rt, rstd, ps_ss = es_bufs
        for k in range(nk):
            s = sq[k % 2]
            K.op(K.act, lambda k=k, s=s: nc.scalar.activation(out=s.t[:, :n], in_=xin.t[:, k, :n], func=AF.Square),
                 reads=[xin], writes=[s])
            K.mm(ps_ss, ps_ss.t[:, :n], self.ones_bf.t[:, :], s.t[:, :n], start=(k == 0), stop=(k == nk - 1),
                 reads=[s, self.ones_bf])
        K.op(K.act, lambda: nc.scalar.activation(out=rt.t[:, :n], in_=ps_ss.t[:, :n], func=AF.Sqrt,
                                                 scale=1.0 / (128 * nk), bias=self.eps_t.t[:, 0:1]),
             reads=[ps_ss, self.eps_t], writes=[rt])
        K.op(K.dve, lambda: nc.vector.reciprocal(out=rstd.t[:, :n], in_=rt.t[:, :n]), reads=[rt], writes=[rstd])
        for k in range(nk):
            x2 = xn[k % 2]
            K.op(K.dve, lambda k=k, x2=x2: nc.vector.tensor_tensor(out=x2.t[:, :n], in0=xin.t[:, k, :n], in1=rstd.t[:, :n],
                                                                    op=ALU.mult),
                 reads=[xin, rstd], writes=[x2])
            if sh_ap is not None:
                K.op(K.act, lambda k=k, x2=x2: nc.scalar.activation(out=u.t[:, k, :n], in_=x2.t[:, :n], func=AF.Identity,
                                                                     scale=gs_ap(k), bias=sh_ap(k)),
                     reads=[x2, self.gs, self.sh], nosync=[u] if k else [], writes=[] if k else [u])
            else:
                K.op(K.act, lambda k=k, x2=x2: nc.scalar.activation(out=u.t[:, k, :n], in_=x2.t[:, :n], func=AF.Identity,
                                                                     scale=gs_ap(k)),
                     reads=[x2], nosync=[u] if k else [], writes=[] if k else [u])

    def norm_bufs(self, es, pfx, ps_idx):
        K = self.K
        sq = [K.sb(es, f"{pfx}sq{i}", [128, 512], BF16) for i in range(2)]
        xn = [K.sb(es, f"{pfx}xn{i}", [128, 512], F32) for i in range(2)]
        rt = K.sb(es, f"{pfx}rt", [128, 512], F32)
        rstd = K.sb(es, f"{pfx}rstd", [128, 512], F32)
        return (sq, xn, rt, rstd, K.psum[ps_idx])

    def phase_ffn(self, l, which, src_is_input, skip_ctx):
        nc, K = self.nc, self.K
        s_idx = 0 if which == 0 else 2
        with ExitStack() as es:
            wgu = K.sb(es, "wgu", [128, 8, 2 * DFF], BF16)
            wdn = K.sb(es, "wdn", [128, NJ, D], BF16)
            xin = K.sb(es, "xin", [128, 8, 512], F32)
            stg = [Tk(xin.t[:, 4 * i:4 * i + 4, :].rearrange("p a b -> p (a b)")) for i in range(2)]
            u = K.sb(es, "u", [128, 8, 512], BF16)
            hmid = K.sb(es, "hmid", [128, NJ, 512], BF16)
            sg = [K.sb(es, f"sg{i}", [128, 512], F32) for i in range(2)]
            xr = [K.sb(es, f"xr{i}", [128, 512], F32) for i in range(3)]
            nb = self.norm_bufs(es, "f", 0)
            idx = 0
            src_gu = self.w_gu[which][l].rearrange("(k p) n -> p k n", p=128)
            for k in range(8):
                for c0 in range(0, 2 * DFF, 2048):
                    c1 = min(c0 + 2048, 2 * DFF)
                    K.load_cast(stg, wgu, wgu.t[:, k, c0:c1], src_gu[:, k, c0:c1], idx)
                    idx += 1
            src_dn = self.w_dn[which][l].rearrange("(j p) n -> p j n", p=128)
            for j in range(0, NJ, 2):
                K.load_cast(stg, wdn, wdn.t[:, j:j + 2, :], src_dn[:, j:j + 2, :], idx)
                idx += 1
            hsrc = self.hT if src_is_input else self.h
            K.barrier()
            tis = [ti for ti in range(len(TILES)) if not (ti == 0 and skip_ctx)]

            def load_x(ti):
                t0, n = TILES[ti]
                K.dma(K.sp, xin.t[:, :, :n], hsrc[:, t0:t0 + n].rearrange("(k p) n -> p k n", p=128), writes=[xin])

            def norm(ti):
                t0, n = TILES[ti]
                v = 1 if ti == 0 else 0
                self.norm_mod(nb, xin, n,
                              lambda k: self.gs.t[:, l, v, s_idx, k:k + 1],
                              lambda k: self.sh.t[:, l, v, s_idx, k:k + 1], u)

            load_x(tis[0])
            norm(tis[0])
            xrc = 0
            for ii, ti in enumerate(tis):
                t0, n = TILES[ti]
                v = 1 if ti == 0 else 0
                nxt = tis[ii + 1] if ii + 1 < len(tis) else None
                if nxt is not None:
                    load_x(nxt)
                for j in range(NJ):
                    pg = K.psum[1 + (j % 2)]
                    pu = K.psum[3 + (j % 2)]
                    for k in range(8):
                        K.mm(pg, pg.t[:, :n], wgu.t[:, k, j * 128:(j + 1) * 128], u.t[:, k, :n],
                             start=(k == 0), stop=(k == 7), reads=[wgu, u])
                    for k in range(8):
                        K.mm(pu, pu.t[:, :n], wgu.t[:, k, DFF + j * 128:DFF + (j + 1) * 128], u.t[:, k, :n],
                             start=(k == 0), stop=(k == 7), reads=[wgu, u])
                    s = sg[j % 2]
                    K.op(K.act, lambda s=s, pg=pg: nc.scalar.activation(out=s.t[:, :n], in_=pg.t[:, :n], func=AF.Silu),
                         reads=[pg], writes=[s])
                    K.op(K.dve, lambda s=s, pu=pu, j=j: nc.vector.tensor_tensor(out=hmid.t[:, j, :n], in0=s.t[:, :n],
                                                                               in1=pu.t[:, :n], op=ALU.mult),
                         reads=[s, pu], writes=[hmid] if j == 0 else [], nosync=[hmid] if j else [])
                if nxt is not None:
                    norm(nxt)
                for c in range(8):
                    po = K.psum[5 + (c % 2)]
                    x_ = xr[xrc % 3]
                    xrc += 1
                    hrow = hsrc[c * 128:(c + 1) * 128, t0:t0 + n]
                    K.dma(K.sp, x_.t[:, :n], hrow, writes=[x_])
                    for j in range(NJ):
                        K.mm(po, po.t[:, :n], wdn.t[:, j, c * 128:(c + 1) * 128], hmid.t[:, j, :n],
                             start=(j == 0), stop=(j == NJ - 1), reads=[wdn, hmid])
                    K.op(K.dve, lambda c=c, po=po, x_=x_: nc.vector.scalar_tensor_tensor(
                        out=x_.t[:, :n], in0=po.t[:, :n], scalar=self.gt.t[:, l, v, s_idx, c:c + 1],
                        in1=x_.t[:, :n], op0=ALU.mult, op1=ALU.add),
                        reads=[po, self.gt, x_], writes=[x_])
                    K.dma(K.sp, self.h[c * 128:(c + 1) * 128, t0:t0 + n], x_.t[:, :n], reads=[x_])

    def phase_proj(self, l):
        nc, K = self.nc, self.K
        with ExitStack() as es:
            win = K.sb(es, "win", [128, 8, NWIN], BF16)
            wuq = K.sb(es, "wuq", [128, 2, 768], BF16)
            wukv = K.sb(es, "wukv", [128, 512], BF16)
            qkn = K.sb(es, "qkn", [128, 3], F32)
            xinb = [K.sb(es, f"xin{i}", [128, 8, 512], F32) for i in range(2)]
            stg = [Tk(xinb[0].t[:, 4 * i:4 * i + 4, :].rearrange("p a b -> p (a b)")) for i in range(2)]
            ubuf = [K.sb(es, f"u{i}", [128, 8, 512], BF16) for i in range(2)]
            rtab = [K.sb(es, f"rtab{i}", [128, 4, 512], F32) for i in range(2)]
            cq_sb = K.sb(es, "cq_sb", [128, 2, 512], F32)
            ckv_sb = K.sb(es, "ckv_sb", [128, 1, 512], F32)
            cqn = K.sb(es, "cqn", [128, 2, 512], BF16)
            ckvn = K.sb(es, "ckvn", [128, 1, 512], BF16)
            stages = [K.sb(es, f"pst{i}", [128, 512], BF16) for i in range(6)]
            vst = [K.sb(es, f"vst{i}", [128, 260], BF16) for i in range(4)]
            ra = [K.sb(es, f"ra{i}", [128, 512], F32) for i in range(2)]
            rb = [K.sb(es, f"rb{i}", [128, 512], F32) for i in range(2)]
            nb = self.norm_bufs(es, "p", 0)
            for v_ in vst:
                K.op(K.dve, lambda v_=v_: nc.vector.memset(v_.t[:], 1.0), writes=[v_])
            K.dma(K.sp, qkn.t[:], self.qkn[l], writes=[qkn])
            idx = 0
            src = self.w_in[l].rearrange("(k p) n -> p k n", p=128)
            for k in range(8):
                for c0 in range(0, NWIN, 2048):
                    c1 = min(c0 + 2048, NWIN)
                    K.load_cast(stg, win, win.t[:, k, c0:c1], src[:, k, c0:c1], idx)
                    idx += 1
            K.load_cast(stg, wuq, wuq.t[:, :, :], self.w_uq[l].rearrange("(k p) n -> p k n", p=128), idx)
            idx += 1
            K.load_cast(stg, wukv, wukv.t[:, :], self.w_ukv[l], idx)
            idx += 1
            K.barrier()
            st = {"bank": 0, "stage": 0, "vst": 0, "ev": 0, "rr": 0}

            def bank():
                b = K.psum[1 + st["bank"] % 7]
                st["bank"] += 1
                return b

            def stage():
                t = stages[st["stage"] % len(stages)]
                st["stage"] += 1
                return t

            def vstage():
                t = vst[st["vst"] % len(vst)]
                st["vst"] += 1
                return t

            def evac(dst_tk, dst_ap, ps_tk, src_ap, nosync=False, eng=None):
                st["ev"] += 1
                w = [] if nosync else [dst_tk]
                ns = [dst_tk] if nosync else []
                use_act = (st["ev"] % 2 == 0) if eng is None else (eng == "act")
                if use_act:
                    K.op(K.act, lambda: nc.scalar.activation(out=dst_ap, in_=src_ap, func=AF.Copy), reads=[ps_tk], writes=w, nosync=ns)
                else:
                    K.op(K.dve, lambda: nc.vector.tensor_copy(out=dst_ap, in_=src_ap), reads=[ps_tk], writes=w, nosync=ns)

            def p_load(ti):
                t0, n = TILES[ti]
                K.dma(K.sp, xinb[ti % 2].t[:, :, :n], self.h[:, t0:t0 + n].rearrange("(k p) n -> p k n", p=128),
                      writes=[xinb[ti % 2]])
                K.dma(K.sp, rtab[ti % 2].t[:, :, :n], self.rope[:, :, t0:t0 + n].rearrange("a p n -> p a n"),
                      writes=[rtab[ti % 2]])

            def p_norm(ti):
                t0, n = TILES[ti]
                v = 1 if ti == 0 else 0
                self.norm_mod(nb, xinb[ti % 2], n, lambda k: self.gs.t[:, l, v, 1, k:k + 1],
                              lambda k: self.sh.t[:, l, v, 1, k:k + 1], ubuf[ti % 2])
                K.dma(K.sp, self.U[:, t0:t0 + n].rearrange("(k p) n -> p k n", p=128), ubuf[ti % 2].t[:, :, :n],
                      reads=[ubuf[ti % 2]])

            NTI = min(len(TILES), DBG_TILES)
            p_load(0)
            if NTI > 1:
                p_load(1)
            p_norm(0)
            for ti, (t0, n) in enumerate(TILES):
                if ti >= NTI:
                    break
                u = ubuf[ti % 2]
                rt_ = rtab[ti % 2]

                def proj(col0, M):
                    b = bank()
                    for k in range(8):
                        K.mm(b, b.t[0:M, :n], win.t[:, k, col0:col0 + M], u.t[:, k, :n], start=(k == 0), stop=(k == 7),
                             reads=[win, u])
                    return b

                def rope_out(bn, bs, p0, p1, tab, dst_tk, dst_ap_fn):
                    i = st["rr"] % 2
                    st["rr"] += 1
                    a_, b_ = ra[i], rb[i]
                    K.op(K.dve, lambda: nc.vector.tensor_tensor(out=a_.t[p0:p1, :n], in0=bn.t[p0:p1, :n],
                                                                in1=rt_.t[p0:p1, tab, :n], op=ALU.mult),
                         reads=[bn, rt_], writes=[a_])
                    K.op(K.dve, lambda: nc.vector.tensor_tensor(out=b_.t[p0:p1, :n], in0=bs.t[p0:p1, :n],
                                                                in1=rt_.t[p0:p1, tab + 1, :n], op=ALU.mult),
                         reads=[bs, rt_], writes=[b_])
                    K.op(K.pool, lambda: nc.gpsimd.tensor_tensor(out=dst_ap_fn(p0, p1), in0=a_.t[p0:p1, :n],
                                                                 in1=b_.t[p0:p1, :n], op=ALU.add),
                         reads=[a_, b_], writes=[dst_tk])

                if DBG_STEP < 2:
                    continue
                for k in range(2):
                    b = proj(k * 128, 128)
                    evac(cq_sb, cq_sb.t[:, k, :n], b, b.t[:, :n], nosync=(k == 1))
                self.norm_mod(nb, cq_sb, n, lambda k: qkn.t[:, k:k + 1], None, cqn, nk=2)
                b = proj(256, 128)
                evac(ckv_sb, ckv_sb.t[:, 0, :n], b, b.t[:, :n])
                self.norm_mod(nb, ckv_sb, n, lambda k: qkn.t[:, 2:3], None, ckvn, nk=1)
                if ti + 1 < NTI:
                    p_norm(ti + 1)
                if DBG_STEP < 3:
                    continue
                bn, bs = proj(384, 128), proj(2464, 128)
                sg_ = stage()
                rope_out(bn, bs, 0, 32, 0, sg_, lambda p0, p1: sg_.t[p0:p1, :n])
                for h in range(4):
                    K.dma(K.sp, self.KA[h, 64:96, t0:t0 + n], sg_.t[0:32, :n], reads=[sg_])
                if DBG_STEP < 4:
                    continue
                for (c_n, c_s, tab, dst) in [
                    (416, 2496, 0, self.QB[0]), (544, 2624, 0, self.QB[1]),
                    (672, 2752, 0, self.KB_[0]), (800, 2880, 0, self.KB_[1]),
                    (1952, 3008, 2, self.QD[0]), (2080, 3136, 2, self.QD[1]),
                ]:
                    bn, bs = proj(c_n, 128), proj(c_s, 128)
                    sg_ = stage()
                    rope_out(bn, bs, 0, 128, tab, sg_, lambda p0, p1, sg_=sg_: sg_.t[p0:p1, :n])
                    K.dma(K.sp, dst[:, t0:t0 + n], sg_.t[:, :n], reads=[sg_])
                if DBG_STEP < 5:
                    continue
                bn, bs = proj(2208, 128), proj(3264, 128)
                sg_ = stage()
                rope_out(bn, bs, 0, 128, 2, sg_, lambda p0, p1, sg_=sg_: sg_.t[p0:p1, :n])
                for g in range(2):
                    for dup in range(2):
                        K.dma(K.sp, self.KD[g, 64 * dup:64 * dup + 64, t0:t0 + n], sg_.t[64 * g:64 * g + 64, :n], reads=[sg_])
                if DBG_STEP < 6:
                    continue
                for (c_n, dst) in [(1184, self.QC[0]), (1312, self.QC[1]), (1440, self.KC[0]), (1568, self.KC[1])]:
                    b = proj(c_n, 128)
                    sg_ = stage()
                    evac(sg_, sg_.t[:, :n], b, b.t[:, :n])
                    K.dma(K.sp, dst[:, t0:t0 + n], sg_.t[:, :n], reads=[sg_])
                if DBG_STEP < 7:
                    continue
                for h in range(4):
                    b1, b2 = bank(), bank()
                    for k in range(2):
                        K.mm(b1, b1.t[0:96, :n], wuq.t[:, k, h * 192:h * 192 + 96], cqn.t[:, k, :n], start=(k == 0), stop=(k == 1),
                             reads=[wuq, cqn])
                    for k in range(2):
                        K.mm(b2, b2.t[0:96, :n], wuq.t[:, k, h * 192 + 96:h * 192 + 192], cqn.t[:, k, :n], start=(k == 0),
                             stop=(k == 1), reads=[wuq, cqn])
                    sg_ = stage()
                    evac(sg_, sg_.t[0:64, :n], b1, b1.t[0:64, :n], eng="dve")
                    rope_out(b1, b2, 64, 96, 0, sg_, lambda p0, p1, sg_=sg_: sg_.t[p0:p1, :n])
                    K.dma(K.sp, self.QA[h, :, t0:t0 + n], sg_.t[0:96, :n], reads=[sg_])
                if DBG_STEP < 8:
                    continue
                for i in range(2):
                    b = bank()
                    K.mm(b, b.t[:, :n], wukv.t[:, i * 128:(i + 1) * 128], ckvn.t[:, 0, :n], start=True, stop=True,
                         reads=[wukv, ckvn])
                    sg_ = stage()
                    evac(sg_, sg_.t[:, :n], b, b.t[:, :n])
                    for hh in range(2):
                        K.dma(K.sp, self.KA[2 * i + hh, 0:64, t0:t0 + n], sg_.t[64 * hh:64 * hh + 64, :n], reads=[sg_])
                if DBG_STEP < 9:
                    continue
                for sblk in range(n // 128):
                    tok = slice(sblk * 128, (sblk + 1) * 128)
                    r0 = t0 + sblk * 128
                    b = bank()
                    K.mm(b, b.t[:, 0:256], ckvn.t[:, 0, tok], wukv.t[:, 256:512], start=True, stop=True, reads=[wukv, ckvn])
                    vs = vstage()
                    if DBG_STEP >= 10:
                        evac(vs, vs.t[:, :].rearrange("p (h e) -> p h e", e=65)[:, :, 0:64],
                             b, b.t[:, 0:256].rearrange("p (h e) -> p h e", e=64))
                    if DBG_STEP >= 11:
                        K.dma(K.sp, self.VA[r0:r0 + 128, :], vs.t[:, :], reads=[vs])
                    if DBG_STEP < 12:
                        continue
                    bA, bB = bank(), bank()
                    for k in range(8):
                        K.mm(bA, bA.t[:, 0:512], u.t[:, k, tok], win.t[:, k, 3392:3904], start=(k == 0), stop=(k == 7),
                             reads=[win, u])
                    for k in range(8):
                        K.mm(bB, bB.t[:, 0:128], u.t[:, k, tok], win.t[:, k, 3904:4032], start=(k == 0), stop=(k == 7),
                             reads=[win, u])
                    if sblk == 0 and ti + 2 < NTI:
                        p_load(ti + 2)
                    for (bb, c0, nh, dst) in [(bA, 0, 4, self.VB), (bA, 256, 4, self.VC), (bB, 0, 2, self.VD)]:
                        vs = vstage()
                        if DBG_STEP < 13:
                            continue
                        evac(vs, vs.t[:, 0:nh * 65].rearrange("p (h e) -> p h e", e=65)[:, :, 0:64],
                             bb, bb.t[:, c0:c0 + nh * 64].rearrange("p (h e) -> p h e", e=64),
                             eng=("dve" if bb is bA else "act"))
                        if DBG_STEP < 14:
                            continue
                        K.dma(K.sp, dst[r0:r0 + 128, :], vs.t[:, 0:nh * 65], reads=[vs])


    def phase_attn(self, l, mx, with_ctx):
        nc, K = self.nc, self.K
        mi = "ABCD".index(mx)
        lam_init = 0.8 - 0.6 * math.exp(-0.3 * l)
        with ExitStack() as es:
            nvh = 2 if mx == "D" else 4
            vt = K.sb(es, "vt", [128, 66, nvh * 65], BF16)
            Vsrc = {"A": self.VA, "B": self.VB, "C": self.VC, "D": self.VD}[mx]
            vsrc = Vsrc.rearrange("(c p) f -> p c f", p=128)
            for c0 in range(0, 66, 11):
                K.dma(K.sp, vt.t[:, c0:c0 + 11, :], vsrc[:, c0:c0 + 11, :], nosync=[vt])
            if mx == "A":
                kt = [K.sb(es, f"kt{h}", [96, NT], BF16) for h in range(4)]
                for h in range(4):
                    K.dma(K.sp, kt[h].t[:, :], self.KA[h], writes=[kt[h]])
                qsb = [K.sb(es, f"q{i}", [96, 4, 512], BF16) for i in range(2)]
                Qsrc = self.QA
            else:
                Ksrc = {"B": self.KB_, "C": self.KC, "D": self.KD}[mx]
                kt = [K.sb(es, f"kt{h}", [128, NT], BF16) for h in range(2)]
                for h in range(2):
                    K.dma(K.sp, kt[h].t[:, :], Ksrc[h], writes=[kt[h]])
                qsb = [K.sb(es, f"q{i}", [128, 2, 512], BF16) for i in range(2)]
                Qsrc = {"B": self.QB, "C": self.QC, "D": self.QD}[mx]
            ptiles = [K.sb(es, f"pt{i}", [128, 1024], BF16) for i in range(3)]
            rd = [K.sb(es, f"rd{i}", [128, 512], F32) for i in range(2)]
            rbs = [K.sb(es, f"rbs{i}", [128, 512], F32) for i in range(2)]
            yst = [K.sb(es, f"yst{i}", [128, 512], BF16) for i in range(3)]
            for r_ in rd:
                K.op(K.dve, lambda r_=r_: nc.vector.memset(r_.t[:], 0.0), writes=[r_])
            if mx != "A":
                nrng = 4 if mx == "B" else 2
                qm = [[K.sb(es, f"qm{a}_{b_}", [128, 512], BF16) for b_ in range(2)] for a in range(nrng)]
                for a in range(nrng):
                    for b_ in range(2):
                        K.op(K.pool, lambda a=a, b_=b_: nc.gpsimd.memset(qm[a][b_].t[:], 0.0), writes=[qm[a][b_]])
                qmc = [0] * nrng

                def masked_q(q, c2, p0, W, n):
                    a = p0 // W
                    t = qm[a][qmc[a] % 2]
                    qmc[a] += 1
                    K.op(K.pool, lambda: nc.gpsimd.tensor_copy(out=t.t[p0:p0 + W, :n], in_=q.t[p0:p0 + W, c2, :n]),
                         reads=[q], writes=[t])
                    return t
            sgroups = [Tk(K.pfull[:, 1024 * g:1024 * (g + 1)]) for g in range(2)]
            ybanks = K.psum[4:6]
            bbanks = K.psum[6:8]
            cnt = {"y": 0, "b": 0, "yst": 0, "tmp": 0, "s": 0, "p": 0}
            if mx in "CD":
                tmpf = [K.sb(es, f"tmpf{i}", [128, 1024], F32) for i in range(2)]
            if mx == "C":
                biasb = [K.sb(es, f"biasb{i}", [128, 8, 512], F32) for i in range(2)]
            if mx == "D":
                maskt = K.sb(es, "maskt", [128, 6, 512], F32)
                K.dma(K.sp, maskt.t[:], self.win_mask[:, :, :], writes=[maskt])
                esk = K.sb(es, "esk", [128, 4], F32)
                K.dma(K.sp, esk.t[64:65, :], self.sink_in[l], writes=[esk])
                K.op(K.act, lambda: nc.scalar.activation(out=esk.t[64:65, :], in_=esk.t[64:65, :], func=AF.Exp),
                     reads=[esk], writes=[esk])
            if mx == "B":
                lamt = K.sb(es, "lamt", [128, 128], F32)
                lprod = K.sb(es, "lprod", [128, 64], F32)
                lsum = K.sb(es, "lsum", [128, 2], F32)
                nlam = K.sb(es, "nlam", [128, 1], F32)
                subg = K.sb(es, "subg", [64, 1], F32)
                y1n = K.sb(es, "y1n", [64, 512], F32)
                t2 = K.sb(es, "t2", [64, 512], F32)
                yd = K.sb(es, "yd", [64, 512], F32)
                sq = K.sb(es, "bsq", [128, 512], BF16)
                K.op(K.dve, lambda: nc.vector.memset(sq.t[:], 0.0), writes=[sq])
                brt = K.sb(es, "brt", [64, 512], F32)
                brs = K.sb(es, "brs", [64, 512], F32)
                K.dma(K.sp, lamt.t[64:65, :], self.lam_in[l], writes=[lamt])
                K.dma(K.sp, subg.t[:, :], self.subln[l], writes=[subg])
                lv = lamt.t[64:65, :].rearrange("p (a b c) -> p a b c", a=2, b=2)
                K.op(K.dve, lambda: nc.vector.tensor_tensor(out=lprod.t[64:65, :].rearrange("p (a c) -> p a c", a=2),
                                                            in0=lv[:, :, 0, :], in1=lv[:, :, 1, :], op=ALU.mult),
                     reads=[lamt], writes=[lprod])
                K.op(K.dve, lambda: nc.vector.reduce_sum(out=lsum.t[64:65, :], in_=lprod.t[64:65, :].rearrange("p (a c) -> p a c", a=2),
                                                         axis=mybir.AxisListType.X), reads=[lprod], writes=[lsum])
                K.op(K.act, lambda: nc.scalar.activation(out=lsum.t[64:65, :], in_=lsum.t[64:65, :], func=AF.Exp),
                     reads=[lsum], writes=[lsum])
                K.op(K.dve, lambda: nc.vector.tensor_tensor(out=nlam.t[64:65, :], in0=lsum.t[64:65, 1:2], in1=lsum.t[64:65, 0:1],
                                                            op=ALU.subtract), reads=[lsum], writes=[nlam])
                K.op(K.dve, lambda: nc.vector.tensor_scalar(out=nlam.t[64:65, :], in0=nlam.t[64:65, :], scalar1=-lam_init,
                                                            scalar2=None, op0=ALU.add), reads=[nlam], writes=[nlam])
                if "lam_dbg" in self.debug:
                    dbg = self.nc.dram_tensor("lam_dbg", [1, 1], F32, kind="ExternalOutput").ap()
                    K.dma(K.sp, dbg[:, :], nlam.t[64:65, :], reads=[nlam])

            def gview(t, G, n):
                return t.t[:, 0:G * 512].rearrange("p (g x) -> p g x", x=512)[:, :, :n]

            def core(n, q_ap, q_tk, kT_fn, k_tk, v_fn, groups, scale, ybank, M, bias_fn=None, after=None):
                ng = len(groups)
                total = sum(len(g) for g in groups)
                hist = []
                done = 0
                for i in range(ng + 1):
                    if i < ng:
                        gl = groups[i]
                        G = len(gl)
                        sg = sgroups[cnt["s"] % 2]
                        cnt["s"] += 1
                        for a, c in enumerate(gl):
                            K.mm(sg, sg.t[:, a * 512:a * 512 + n], kT_fn(c), q_ap, True, True, reads=[q_tk, k_tk])
                        p = ptiles[cnt["p"] % 3]
                        cnt["p"] += 1
                        bias = bias_fn(i) if bias_fn is not None else None
                        if bias is None:
                            K.op(K.act, lambda sg=sg, p=p, G=G: nc.scalar.activation(out=gview(p, G, n), in_=gview(sg, G, n),
                                                                                     func=AF.Exp, scale=scale),
                                 reads=[sg], writes=[p])
                        else:
                            btk, bap = bias
                            tf = tmpf[cnt["tmp"] % 2]
                            cnt["tmp"] += 1
                            K.op(K.dve, lambda sg=sg, tf=tf, bap=bap, G=G: nc.vector.scalar_tensor_tensor(
                                out=gview(tf, G, n), in0=gview(sg, G, n), scalar=scale, in1=bap, op0=ALU.mult, op1=ALU.add),
                                reads=[sg, btk], writes=[tf])
                            K.op(K.act, lambda tf=tf, p=p, G=G: nc.scalar.activation(out=gview(p, G, n), in_=gview(tf, G, n),
                                                                                     func=AF.Exp),
                                 reads=[tf], writes=[p])
                        hist.append((p, gl))
                    if i >= 1:
                        p, gl = hist[i - 1]
                        for a, c in enumerate(gl):
                            K.mm(ybank, ybank.t[0:M, :n], v_fn(c), p.t[:, a * 512:a * 512 + n], start=(done == 0),
                                 stop=(done == total - 1), reads=[vt, p])
                            done += 1
                    if i == min(6, ng) and after is not None:
                        after()
                        after = None
                if after is not None:
                    after()

            def finalize(n, ybank, out_tk, out_ap, den_add=None, mul=None):
                r_ = rd[cnt["b"] % 2]
                rb_ = rbs[cnt["b"] % 2]
                bb = bbanks[cnt["b"] % 2]
                cnt["b"] += 1
                if den_add is not None:
                    K.op(K.dve, lambda: nc.vector.tensor_scalar(out=r_.t[64:65, :n], in0=ybank.t[64:65, :n], scalar1=den_add,
                                                                scalar2=None, op0=ALU.add), reads=[ybank], writes=[r_])
                    K.op(K.dve, lambda: nc.vector.reciprocal(out=r_.t[64:65, :n], in_=r_.t[64:65, :n]), reads=[r_], writes=[r_])
                else:
                    K.op(K.dve, lambda: nc.vector.reciprocal(out=r_.t[64:65, :n], in_=ybank.t[64:65, :n]), reads=[ybank], writes=[r_])
                if mul is not None:
                    K.op(K.dve, lambda: nc.vector.tensor_scalar(out=r_.t[64:65, :n], in0=r_.t[64:65, :n], scalar1=mul,
                                                                scalar2=None, op0=ALU.mult), reads=[r_], writes=[r_])
                K.mm(bb, bb.t[:, :n], self.ones_f.t[:, :], r_.t[:, :n], True, True, reads=[r_, self.ones_f])
                K.op(K.act, lambda: nc.scalar.activation(out=rb_.t[0:64, :n], in_=bb.t[0:64, :n], func=AF.Copy),
                     reads=[bb], writes=[rb_])
                K.op(K.dve, lambda: nc.vector.tensor_tensor(out=out_ap, in0=ybank.t[0:64, :n], in1=rb_.t[0:64, :n], op=ALU.mult),
                     reads=[ybank, rb_], writes=[out_tk])

            def next_y():
                b = ybanks[cnt["y"] % 2]
                cnt["y"] += 1
                return b

            def next_yst():
                t = yst[cnt["yst"] % 3]
                cnt["yst"] += 1
                return t

            def pairs(lst):
                return [lst[i:i + 2] for i in range(0, len(lst), 2)]

            def store_y(h, t0, n, ys):
                K.dma(K.sp, self.Y[mi, h * 64:(h + 1) * 64, t0:t0 + n], ys.t[0:64, :n], reads=[ys])

            tis = [ti for ti in range(len(TILES)) if not (ti == 0 and not with_ctx)]
            if mx == "C":
                order = [(ti, h) for h in range(4) for ti in tis]
            else:
                order = [(ti, h) for ti in tis for h in range(4)]
            units = []
            for (ti, h) in order:
                if mx == "B":
                    units.append({"ti": ti, "h": h, "m": 0})
                    units.append({"ti": ti, "h": h, "m": 1})
                else:
                    units.append({"ti": ti, "h": h, "m": 0})
            pending = [None]
            last_q = [None, None]
            qctr = [0]
            last_bias = [None, None]
            bctr = [0]
            allch = list(range(66))

            def prep(U_):
                ti, h, m_ = U_["ti"], U_["h"], U_["m"]
                t0, n = TILES[ti]
                if last_q[0] != ti:
                    q = qsb[qctr[0] % 2]
                    qctr[0] += 1
                    if mx == "A":
                        K.dma(K.sp, q.t[:, :, :n], Qsrc[:, :, t0:t0 + n].rearrange("h p t -> p h t"), writes=[q])
                    else:
                        K.dma(K.sp, q.t[:, :, :n], Qsrc[:, :, t0:t0 + n].rearrange("c p t -> p c t"), writes=[q])
                    last_q[0], last_q[1] = ti, q
                q = last_q[1]
                U_["q"] = q
                if mx == "B":
                    v_ = 2 * h + m_
                    c2, pm = v_ // 4, v_ % 4
                    U_["c2"] = c2
                    U_["qt"] = masked_q(q, c2, 32 * pm, 32, n)
                elif mx in "CD":
                    c2, p0 = h // 2, 64 * (h % 2)
                    U_["c2"] = c2
                    U_["qt"] = masked_q(q, c2, p0, 64, n)
                if mx == "C" and ti != 0:
                    i_lat = ti - 1
                    typ = 0 if i_lat == 0 else (2 if i_lat == 15 else 1)
                    if last_bias[0] != (typ, h):
                        bt = biasb[bctr[0] % 2]
                        bctr[0] += 1
                        K.dma(K.sp, bt.t[:, :, :], self.na_bias[l, typ, h], writes=[bt])
                        last_bias[0], last_bias[1] = (typ, h), bt
                    U_["bt"] = last_bias[1]

            def run(U_):
                ti, h, m_ = U_["ti"], U_["h"], U_["m"]
                t0, n = TILES[ti]
                i_lat = ti - 1
                q = U_["q"]
                if mx == "A":
                    groups = pairs([0, 1] if ti == 0 else allch)
                    yb = next_y()
                    core(n, q.t[0:96, h, :n], q, lambda c, h=h: kt[h].t[0:96, c * 128:(c + 1) * 128], kt[h],
                         lambda c, h=h: vt.t[:, c, h * 65:h * 65 + 65], groups, MLA_SCALE, yb, 65, after=pending[0])

                    def fin(n=n, yb=yb, h=h, t0=t0):
                        ys = next_yst()
                        finalize(n, yb, ys, ys.t[0:64, :n])
                        store_y(h, t0, n, ys)
                    pending[0] = fin
                elif mx == "B":
                    groups = pairs([0, 1] if ti == 0 else allch)
                    c2 = U_["c2"]
                    yb = next_y()
                    qt_ = U_["qt"]
                    core(n, qt_.t[:, :n], qt_,
                         lambda c, c2=c2: kt[c2].t[:, c * 128:(c + 1) * 128], kt[c2],
                         lambda c, h=h: vt.t[:, c, h * 65:h * 65 + 65], groups, DIFF_SCALE, yb, 65, after=pending[0])
                    if m_ == 0:
                        def fin(n=n, yb=yb):
                            finalize(n, yb, y1n, y1n.t[0:64, :n])
                    else:
                        def fin(n=n, yb=yb, h=h, t0=t0):
                            finalize(n, yb, t2, t2.t[0:64, :n], mul=nlam.t[64:65, 0:1])
                            K.op(K.pool, lambda: nc.gpsimd.tensor_tensor(out=yd.t[:, :n], in0=y1n.t[:, :n], in1=t2.t[:, :n],
                                                                         op=ALU.add), reads=[y1n, t2], writes=[yd])
                            K.op(K.act, lambda: nc.scalar.activation(out=sq.t[0:64, :n], in_=yd.t[:, :n], func=AF.Square),
                                 reads=[yd], writes=[sq])
                            bb = bbanks[cnt["b"] % 2]
                            cnt["b"] += 1
                            K.mm(bb, bb.t[:, :n], self.ones_bf.t[:, :], sq.t[:, :n], True, True, reads=[sq, self.ones_bf])
                            li = 1.0 - lam_init
                            K.op(K.act, lambda bb=bb: nc.scalar.activation(out=brt.t[:, :n], in_=bb.t[0:64, :n], func=AF.Sqrt,
                                                                          scale=1.0 / (64 * li * li),
                                                                          bias=self.eps_t.t[0:64, 2 + l:3 + l]),
                                 reads=[bb, self.eps_t], writes=[brt])
                            K.op(K.dve, lambda: nc.vector.reciprocal(out=brs.t[:, :n], in_=brt.t[:, :n]), reads=[brt], writes=[brs])
                            ys = next_yst()
                            K.op(K.dve, lambda ys=ys: nc.vector.scalar_tensor_tensor(out=ys.t[0:64, :n], in0=yd.t[:, :n],
                                                                                    scalar=subg.t[:, 0:1], in1=brs.t[:, :n],
                                                                                    op0=ALU.mult, op1=ALU.mult),
                                 reads=[yd, brs, subg], writes=[ys])
                            store_y(h, t0, n, ys)
                    pending[0] = fin
                elif mx == "C":
                    c2 = U_["c2"]
                    bias_fn = None
                    if ti == 0:
                        groups = pairs([0, 1])
                    else:
                        fc = min(max(4 * i_lat - 2, 0), 56)
                        groups = pairs([0, 1] + [2 + fc + kc for kc in range(8)])
                        bt = U_["bt"]
                        bias_fn = (lambda i, bt=bt, n=n: None if i < 1 else (bt, bt.t[:, 2 * (i - 1):2 * i, :n]))
                    yb = next_y()
                    qt_ = U_["qt"]
                    core(n, qt_.t[:, :n], qt_,
                         lambda c, c2=c2: kt[c2].t[:, c * 128:(c + 1) * 128], kt[c2],
                         lambda c, h=h: vt.t[:, c, h * 65:h * 65 + 65], groups, NA_SCALE, yb, 65, bias_fn=bias_fn,
                         after=pending[0])

                    def fin(n=n, yb=yb, h=h, t0=t0):
                        ys = next_yst()
                        finalize(n, yb, ys, ys.t[0:64, :n])
                        store_y(h, t0, n, ys)
                    pending[0] = fin
                else:
                    g = h // 2
                    bias_fn = None
                    if ti == 0:
                        groups = pairs([0, 1])
                    else:
                        rel = [kc for kc in range(6) if 0 <= 4 * i_lat - 1 + kc < 64]
                        groups = [[0, 1]] + pairs([2 + 4 * i_lat - 1 + kc for kc in rel])
                        relp = pairs(rel)
                        bias_fn = (lambda i, relp=relp, n=n: None if i < 1 else
                                   (maskt, maskt.t[:, relp[i - 1][0]:relp[i - 1][-1] + 1, :n]))
                    yb = next_y()
                    qt_ = U_["qt"]
                    core(n, qt_.t[:, :n], qt_,
                         lambda c, g=g: kt[g].t[:, c * 128:(c + 1) * 128], kt[g],
                         lambda c, g=g: vt.t[:, c, g * 65:g * 65 + 65], groups, GQA_SCALE, yb, 65, bias_fn=bias_fn,
                         after=pending[0])

                    def fin(n=n, yb=yb, h=h, t0=t0):
                        ys = next_yst()
                        finalize(n, yb, ys, ys.t[0:64, :n], den_add=esk.t[64:65, h:h + 1])
                        store_y(h, t0, n, ys)
                    pending[0] = fin

            prep(units[0])
            for ui, U_ in enumerate(units):
                if ui + 1 < len(units):
                    prep(units[ui + 1])
                run(U_)
            if pending[0] is not None:
                pending[0]()

    def phase_merge(self, l, with_ctx):
        nc, K = self.nc, self.K
        with ExitStack() as es:
            wg = K.sb(es, "wg", [128, 4, 8, D], BF16)
            wb = K.sb(es, "wb", [128, 4, 2, D], BF16)
            wo = K.sb(es, "wo", [128, 8, D], BF16)
            bg = K.sb(es, "bg", [128, 4, 8], F32)
            stg = [K.sb(es, f"mstg{i}", [128, 2048], F32) for i in range(2)]
            ub = [K.sb(es, f"u{i}", [128, 8, 512], BF16) for i in range(2)]
            ytb = [K.sb(es, f"yt{i}", [128, 4, 2, 512], BF16) for i in range(2)]
            mer = K.sb(es, "mer", [128, 8, 512], BF16)
            sgt = [K.sb(es, f"sgt{i}", [128, 512], F32) for i in range(2)]
            tmp = [K.sb(es, f"mtmp{i}", [128, 512], F32) for i in range(2)]
            macc = [K.sb(es, f"macc{i}", [128, 512], F32) for i in range(2)]
            xr = [K.sb(es, f"xr{i}", [128, 512], F32) for i in range(3)]
            K.dma(K.sp, bg.t[:], self.b_gate[l], writes=[bg])
            idx = 0
            for i in range(4):
                src = self.w_gate[l, i].rearrange("(k p) n -> p k n", p=128)
                for k in range(0, 8, 2):
                    K.load_cast(stg, wg, wg.t[:, i, k:k + 2, :], src[:, k:k + 2, :], idx)
                    idx += 1
                srcb = self.w_branch[l, i].rearrange("(k p) n -> p k n", p=128)
                K.load_cast(stg, wb, wb.t[:, i, :, :], srcb, idx)
                idx += 1
            srco = self.w_out[l].rearrange("(k p) n -> p k n", p=128)
            for k in range(0, 8, 2):
                K.load_cast(stg, wo, wo.t[:, k:k + 2, :], srco[:, k:k + 2, :], idx)
                idx += 1
            K.barrier()
            tis = [ti for ti in range(len(TILES)) if not (ti == 0 and not with_ctx)]

            def load_in(ii):
                t0, n = TILES[tis[ii]]
                u, yt = ub[ii % 2], ytb[ii % 2]
                K.dma(K.sp, u.t[:, :, :n], self.U[:, t0:t0 + n].rearrange("(k p) n -> p k n", p=128), writes=[u])
                for i in range(4):
                    K.dma(K.sp, yt.t[:, i, :, :n], self.Y[i, :, t0:t0 + n].rearrange("(k p) n -> p k n", p=128),
                          writes=[yt] if i == 0 else [], nosync=[yt] if i else [])

            load_in(0)
            ctr = 0
            xrc = 0
            for ii, ti in enumerate(tis):
                t0, n = TILES[ti]
                v = 1 if ti == 0 else 0
                u, yt = ub[ii % 2], ytb[ii % 2]
                if ii + 1 < len(tis):
                    load_in(ii + 1)
                for c in range(8):
                    cs = slice(c * 128, (c + 1) * 128)
                    ma = macc[c % 2]
                    for i in range(4):
                        pg = K.psum[(ctr % 2)]
                        pb = K.psum[2 + (ctr % 2)]
                        ctr += 1
                        for k in range(8):
                            K.mm(pg, pg.t[:, :n], wg.t[:, i, k, cs], u.t[:, k, :n], start=(k == 0), stop=(k == 7), reads=[wg, u])
                        for k in range(2):
                            K.mm(pb, pb.t[:, :n], wb.t[:, i, k, cs], yt.t[:, i, k, :n], start=(k == 0), stop=(k == 1), reads=[wb, yt])
                        sg_ = sgt[ctr % 2]
                        K.op(K.act, lambda pg=pg, sg_=sg_, i=i, c=c: nc.scalar.activation(
                            out=sg_.t[:, :n], in_=pg.t[:, :n], func=AF.Sigmoid, bias=bg.t[:, i, c:c + 1]),
                            reads=[pg, bg], writes=[sg_])
                        if i == 0:
                            K.op(K.dve, lambda pb=pb, sg_=sg_, ma=ma: nc.vector.tensor_tensor(
                                out=ma.t[:, :n], in0=sg_.t[:, :n], in1=pb.t[:, :n], op=ALU.mult),
                                reads=[sg_, pb], writes=[ma])
                        else:
                            tp = tmp[ctr % 2]
                            K.op(K.dve, lambda pb=pb, sg_=sg_, tp=tp: nc.vector.tensor_tensor(
                                out=tp.t[:, :n], in0=sg_.t[:, :n], in1=pb.t[:, :n], op=ALU.mult),
                                reads=[sg_, pb], writes=[tp])
                            if i < 3:
                                K.op(K.pool, lambda tp=tp, ma=ma: nc.gpsimd.tensor_tensor(
                                    out=ma.t[:, :n], in0=ma.t[:, :n], in1=tp.t[:, :n], op=ALU.add),
                                    reads=[tp, ma], writes=[ma])
                            else:
                                K.op(K.pool, lambda tp=tp, ma=ma, c=c: nc.gpsimd.tensor_tensor(
                                    out=mer.t[:, c, :n], in0=ma.t[:, :n], in1=tp.t[:, :n], op=ALU.add),
                                    reads=[tp, ma], writes=[mer] if c == 0 else [], nosync=[mer] if c else [])
                for c2 in range(8):
                    po = K.psum[4 + (c2 % 2)]
                    x_ = xr[xrc % 3]
                    xrc += 1
                    K.dma(K.sp, x_.t[:, :n], self.h[c2 * 128:(c2 + 1) * 128, t0:t0 + n], writes=[x_])
                    for c in range(8):
                        K.mm(po, po.t[:, :n], wo.t[:, c, c2 * 128:(c2 + 1) * 128], mer.t[:, c, :n], start=(c == 0), stop=(c == 7),
                             reads=[wo, mer])
                    K.op(K.dve, lambda c2=c2, po=po, x_=x_: nc.vector.scalar_tensor_tensor(
                        out=x_.t[:, :n], in0=po.t[:, :n], scalar=self.gt.t[:, l, v, 1, c2:c2 + 1],
                        in1=x_.t[:, :n], op0=ALU.mult, op1=ALU.add),
                        reads=[po, self.gt, x_], writes=[x_])
                    K.dma(K.sp, self.h[c2 * 128:(c2 + 1) * 128, t0:t0 + n], x_.t[:, :n], reads=[x_])

    def phase_final(self):
        nc, K = self.nc, self.K
        with ExitStack() as es:
            xin = [K.sb(es, f"fx{i}", [128, 8, 512], F32) for i in range(2)]
            fn = K.sb(es, "fn", [128, 8], F32)
            nb = self.norm_bufs(es, "n", 0)
            sq, xn, rt, rstd, ps_ss = nb
            K.dma(K.sp, fn.t[:], self.fnorm[:, :], writes=[fn])
            for ti, (t0, n) in enumerate(TILES):
                if ti == 0:
                    continue
                x = xin[ti % 2]
                K.dma(K.sp, x.t[:, :, :n], self.h[:, t0:t0 + n].rearrange("(k p) n -> p k n", p=128),
                      reads=[self.h_tk[ti]], writes=[x])
                for k in range(8):
                    s = sq[k % 2]
                    K.op(K.act, lambda k=k, s=s: nc.scalar.activation(out=s.t[:, :n], in_=x.t[:, k, :n], func=AF.Square),
                         reads=[x], writes=[s])
                    K.mm(ps_ss, ps_ss.t[:, :n], self.ones_bf.t[:, :], s.t[:, :n], start=(k == 0), stop=(k == 7),
                         reads=[s, self.ones_bf])
                K.op(K.act, lambda: nc.scalar.activation(out=rt.t[:, :n], in_=ps_ss.t[:, :n], func=AF.Sqrt,
                                                         scale=1.0 / D, bias=self.eps_t.t[:, 0:1]),
                     reads=[ps_ss, self.eps_t], writes=[rt])
                K.op(K.dve, lambda: nc.vector.reciprocal(out=rstd.t[:, :n], in_=rt.t[:, :n]), reads=[rt], writes=[rstd])
                for k in range(8):
                    K.op(K.dve, lambda k=k: nc.vector.scalar_tensor_tensor(
                        out=x.t[:, k, :n], in0=x.t[:, k, :n], scalar=fn.t[:, k:k + 1], in1=rstd.t[:, :n],
                        op0=ALU.mult, op1=ALU.mult), reads=[x, rstd, fn], writes=[x])
                K.dma(K.sp, self.out[:, t0 - L:t0 - L + n].rearrange("(k p) n -> p k n", p=128), x.t[:, :, :n],
                      reads=[x])


def _rope_tables():
    f = np.float32
    t = np.arange(S)
    rows = (t // GRID_W).astype(f)
    cols = (t % GRID_W).astype(f)

    def tabs(dim):
        half = dim // 2
        freqs = np.power(f(10000.0), -np.arange(0, half, 2, dtype=f) / f(half)).astype(f)
        ar = (rows[:, None] * freqs).astype(f)
        ac = (cols[:, None] * freqs).astype(f)
        q = dim // 4
        cos = np.concatenate([np.cos(ar), np.cos(ar), np.cos(ac), np.cos(ac)], axis=1)
        sin = np.concatenate([-np.sin(ar), np.sin(ar), -np.sin(ac), np.sin(ac)], axis=1)
        cos = np.concatenate([np.ones((L, dim), f), cos.astype(f)], axis=0)
        sin = np.concatenate([np.zeros((L, dim), f), sin.astype(f)], axis=0)
        rep = 128 // dim
        return np.tile(cos.T, (rep, 1)), np.tile(sin.T, (rep, 1))

    c32, s32 = tabs(32)
    c64, s64 = tabs(64)
    return np.ascontiguousarray(np.stack([c32, s32, c64, s64], axis=0).astype(f))


def _swap_idx(dim, nvec):
    q = dim // 4
    base = np.arange(dim)
    partner = np.where((base % (2 * q)) < q, base + q, base - q)
    return np.concatenate([v * dim + partner for v in range(nvec)])


def _na_bias(rpb):
    f = np.float32
    out = np.full((3, 4, 128, 8, 512), NEG, f)
    qr = np.arange(8)[:, None].repeat(64, 1).reshape(-1)
    qc = np.tile(np.arange(64), 8)
    p = np.arange(128)
    for typ, (r0, row0) in enumerate([(0, 0), (8, 4), (120, 112)]):
        r = r0 + qr
        rs = np.clip(r - 4, 0, 120)
        cs = np.clip(qc - 8, 0, 48)
        for kc in range(8):
            kr = row0 + 2 * kc + p // 64
            kcol = p % 64
            ok = ((kr[:, None] >= rs[None, :]) & (kr[:, None] < rs[None, :] + 8)
                  & (kcol[:, None] >= cs[None, :]) & (kcol[:, None] < cs[None, :] + 16))
            dr = np.clip(kr[:, None] - r[None, :] + 7, 0, 14)
            dc = np.clip(kcol[:, None] - qc[None, :] + 15, 0, 30)
            for h in range(4):
                vals = rpb[h][dr, dc]
                out[typ, h, :, kc, :] = np.where(ok, vals, f(NEG))
    return out


def _win_mask():
    f = np.float32
    p = np.arange(128)[:, None]
    q = np.arange(512)[None, :]
    m = np.zeros((128, 6, 512), f)
    for kc in range(6):
        j = 128 * kc - 128 + p
        m[:, kc, :] = np.where(np.abs(q - j) <= 128, f(0.0), f(NEG))
    return m


def prep_shared(inp):
    f = np.float32
    m = {}
    m["w_ada"] = inp["w_ada"]
    m["b_ada"] = np.ascontiguousarray(inp["b_ada"].reshape(DEPTH, 72, 128).transpose(0, 2, 1))
    nr = np.stack([inp["norm_ffn1"], inp["norm_mix"], inp["norm_ffn2"]], axis=1)
    m["norms"] = np.ascontiguousarray(nr.reshape(DEPTH, 3, 8, 128).transpose(0, 3, 1, 2))
    m["fnorm"] = np.ascontiguousarray(inp["final_norm"].reshape(8, 128).T)
    for k in ("ffn1_w_gu", "ffn2_w_gu", "ffn1_w_down", "ffn2_w_down", "w_gate", "w_branch", "w_out"):
        m[k] = inp[k]
    w = inp["w_in"]
    sw32 = lambda c0, nv: w[:, :, c0 + _swap_idx(32, nv)]
    sw64 = lambda c0, nv: w[:, :, c0 + _swap_idx(64, nv)]
    m["w_in_ext"] = np.ascontiguousarray(np.concatenate(
        [w, sw32(384, 1), sw32(416, 8), sw32(672, 8), sw64(1952, 4), sw64(2208, 2),
         w[:, :, 928:1184], w[:, :, 1696:1952], w[:, :, 2336:2464]], axis=2))
    wq = inp["mla_w_uq"].reshape(DEPTH, 256, 4, 96)
    wq_sw = np.concatenate([wq[..., :64], wq[..., 64 + _swap_idx(32, 1)]], axis=-1)
    m["w_uq_ext"] = np.ascontiguousarray(np.concatenate([wq, wq_sw], axis=-1).reshape(DEPTH, 256, 768))
    wkv = inp["mla_w_ukv"].reshape(DEPTH, 128, 4, 128)
    m["w_ukv_ext"] = np.ascontiguousarray(np.concatenate(
        [wkv[..., :64].reshape(DEPTH, 128, 256), wkv[..., 64:].reshape(DEPTH, 128, 256)], axis=-1))
    qn = inp["mla_q_norm"].reshape(DEPTH, 2, 128).transpose(0, 2, 1)
    m["qkn"] = np.ascontiguousarray(np.concatenate([qn, inp["mla_kv_norm"][:, :, None]], axis=-1))
    m["rope_tab"] = _rope_tables()
    m["diff_lam"] = np.ascontiguousarray(inp["diff_lam"].reshape(DEPTH, 1, 128))
    m["diff_subln"] = np.ascontiguousarray(inp["diff_subln"].reshape(DEPTH, 64, 1))
    m["gqa_sink"] = np.ascontiguousarray(inp["gqa_sink"].reshape(DEPTH, 1, 4))
    m["na_bias"] = np.stack([_na_bias(inp["na_rpb"][l]) for l in range(DEPTH)], axis=0)
    m["win_mask"] = _win_mask()
    m["b_gate"] = np.ascontiguousarray(inp["b_gate"].reshape(DEPTH, 4, 8, 128).transpose(0, 3, 1, 2))
    return {k: np.ascontiguousarray(v.astype(f)) for k, v in m.items()}


def prep_core(inp, b):
    f = np.float32
    m = {}
    m["hT"] = np.ascontiguousarray(np.concatenate([inp["ctx"][b], inp["x"][b]], axis=0).T.astype(f))
    cc = np.stack([inp["c"][b], inp["c_ctx"]], axis=-1)
    m["c_in"] = np.ascontiguousarray(cc.reshape(8, 128, 2).transpose(1, 0, 2).astype(f))
    return m


def prep_inputs(inp, b):
    m = prep_shared(inp)
    m.update(prep_core(inp, b))
    return m


_CACHE = {}


def kernel(**inputs):
    inp = {k: np.asarray(v) for k, v in inputs.items()}
    if "nc" not in _CACHE:
        p = Prog()
        _CACHE["nc"] = p.build()
        _CACHE["names"] = list(p.inputs.keys())
    nc = _CACHE["nc"]
    in_maps = []
    shared = prep_shared(inp)
    for b in range(8):
        m = dict(shared)
        m.update(prep_core(inp, b))
        in_maps.append({k: m[k] for k in _CACHE["names"]})
    res = run_bass_kernel_spmd(nc, in_maps, core_ids=list(range(8)))
    out = np.stack([np.ascontiguousarray(res.results[b]["outT"].T) for b in range(8)], axis=0)
    return out.astype(np.float32)
```

```python
import math
import os
from contextlib import ExitStack

import numpy as np
import concourse.bass as bass
import concourse.mybir as mybir
from concourse.bass_utils import run_bass_kernel_spmd

F32 = mybir.dt.float32
BF16 = mybir.dt.bfloat16
AF = mybir.ActivationFunctionType
ALU = mybir.AluOpType

D = 1024
S = 8192
L = 256
NT = S + L
DEPTH = 2
DFF = 2816
NJ = DFF // 128
GRID_W = 64
EPS = 1e-6
SUBLN_EPS = 1e-5
MLA_SCALE = 96 ** -0.5
DIFF_SCALE = 32 ** -0.5
NA_SCALE = 64 ** -0.5
GQA_SCALE = 64 ** -0.5
NEG = -30000.0
NDS = 24
DBG_STEP = int(os.environ.get('DBG_STEP', '99'))
DBG_TILES = int(os.environ.get('DBG_TILES', '99'))
NWIN = 2464 + 928 + 640

TILES = [(0, L)] + [(L + 512 * i, 512) for i in range(S // 512)]


class Tk:
    __slots__ = ("w", "r", "t")

    def __init__(self, t=None):
        self.w = {}
        self.r = {}
        self.t = t


class Eng:
    def __init__(self, name, eng, sem, self_sync=True):
        self.name = name
        self.eng = eng
        self.sem = sem
        self.cnt = 0
        self.seen = {}
        self.self_sync = self_sync
        self.dsems = None


class KB:
    def __init__(self, nc, es):
        self.nc = nc
        self.es = es

        def mk(name, eng, self_sync=True):
            return Eng(name, eng, es.enter_context(nc.semaphore("s_" + name)), self_sync)

        self.pe = mk("pe", nc.tensor, False)
        self.act = mk("act", nc.scalar)
        self.dve = mk("dve", nc.vector)
        self.pool = mk("pool", nc.gpsimd)
        self.sp = mk("sp", nc.sync)
        self.engs = [self.pe, self.act, self.dve, self.pool, self.sp]
        self.queues = [self.sp, self.pool]
        for q in self.queues:
            q.dsems = [es.enter_context(nc.semaphore(f"d_{q.name}_{i}")) for i in range(NDS)]
            q.dvals = [0] * NDS
            q.dnext = 0
        self.pfull = es.enter_context(nc.psum_tensor("psfull", [128, 4096], F32))
        self.psum = [Tk(self.pfull[:, 512 * i:512 * (i + 1)]) for i in range(8)]

    def _deps(self, E, reads, writes):
        deps = {}

        def add(d):
            for key, (val, semh) in d.items():
                if key == E.name and not E.self_sync:
                    continue
                if deps.get(key, (0, None))[0] < val:
                    deps[key] = (val, semh)

        for t in reads:
            add(t.w)
        for t in writes:
            add(t.w)
            add(t.r)
        for key, (val, semh) in deps.items():
            if E.seen.get(key, 0) < val:
                E.eng.wait_ge(semh, val)
                E.seen[key] = val

    def op(self, E, fn, reads=(), writes=(), nosync=()):
        self._deps(E, reads, writes)
        ins = fn()
        E.cnt += 1
        ins.then_inc(E.sem, 1)
        tok = (E.cnt, E.sem)
        for t in reads:
            t.r[E.name] = tok
        for t in writes:
            t.w[E.name] = tok
        for t in nosync:
            t.w[E.name] = tok

    def dma(self, Q, out, in_, reads=(), writes=(), nosync=()):
        self._deps(Q, reads, writes)
        j = Q.dnext % NDS
        Q.dnext += 1
        key = ("d", Q.name, j)
        if Q.seen.get(key, 0) < Q.dvals[j]:
            Q.eng.wait_ge(Q.dsems[j], Q.dvals[j])
            Q.seen[key] = Q.dvals[j]
        Q.dvals[j] += 16
        Q.eng.dma_start(out=out, in_=in_).then_inc(Q.dsems[j], 16)
        tok = (Q.dvals[j], Q.dsems[j])
        for t in reads:
            t.r[key] = tok
        for t in writes:
            t.w[key] = tok
        for t in nosync:
            t.w[key] = tok

    def barrier(self):
        for E in self.engs:
            for F in self.engs:
                if F is E and not E.self_sync:
                    continue
                if E.seen.get(F.name, 0) < F.cnt:
                    E.eng.wait_ge(F.sem, F.cnt)
                    E.seen[F.name] = F.cnt
            for Q in self.queues:
                for j in range(NDS):
                    key = ("d", Q.name, j)
                    if E.seen.get(key, 0) < Q.dvals[j]:
                        E.eng.wait_ge(Q.dsems[j], Q.dvals[j])
                        E.seen[key] = Q.dvals[j]

    def sb(self, es, name, shape, dt):
        self.uid = getattr(self, "uid", 0) + 1
        return Tk(es.enter_context(self.nc.sbuf_tensor(f"{name}_u{self.uid}", shape, dt)))

    def mm(self, out_tk, out_ap, lhsT, rhs, start, stop, reads=(), **kw):
        nc = self.nc
        self.op(self.pe, lambda: nc.tensor.matmul(out_ap, lhsT, rhs, start=start, stop=stop, **kw),
                reads=reads, writes=[out_tk])

    def load_cast(self, stg, dst_tk, dst_ap, src_ap, idx):
        nc = self.nc
        s = stg[idx % len(stg)]
        shp = list(src_ap.shape)
        n = int(np.prod(shp[1:]))
        if len(shp) == 3:
            sview = s.t[:, 0:n].rearrange("p (a b) -> p a b", b=shp[2])
        else:
            sview = s.t[:, 0:n]
        self.dma(self.sp, sview, src_ap, writes=[s])
        E = self.pool if idx % 2 == 0 else self.dve
        if E is self.pool:
            self.op(E, lambda: nc.gpsimd.tensor_copy(out=dst_ap, in_=sview), reads=[s], nosync=[dst_tk])
        else:
            self.op(E, lambda: nc.vector.tensor_copy(out=dst_ap, in_=sview), reads=[s], nosync=[dst_tk])


class Prog:
    def __init__(self, debug=(), stop_after=None):
        self.debug = set(debug)
        self.stop_after = stop_after
        self.nc = bass.Bass("TRN2", target_bir_lowering=False)
        self.inputs = {}

    def din(self, name, shape, dt=F32):
        a = self.nc.dram_tensor(name, list(shape), dt, kind="ExternalInput").ap()
        self.inputs[name] = a
        return a

    def dscr(self, name, shape, dt):
        kind = "ExternalOutput" if name in self.debug else "Internal"
        return self.nc.dram_tensor(name, list(shape), dt, kind=kind).ap()

    def build(self):
        nc = self.nc
        self.hT = self.din("hT", [D, NT])
        self.c_in = self.din("c_in", [128, 8, 2])
        self.w_ada = self.din("w_ada", [DEPTH, D, 9 * D])
        self.b_ada = self.din("b_ada", [DEPTH, 128, 72])
        self.norms = self.din("norms", [DEPTH, 128, 3, 8])
        self.fnorm = self.din("fnorm", [128, 8])
        self.w_gu = [self.din("ffn1_w_gu", [DEPTH, D, 2 * DFF]), self.din("ffn2_w_gu", [DEPTH, D, 2 * DFF])]
        self.w_dn = [self.din("ffn1_w_down", [DEPTH, DFF, D]), self.din("ffn2_w_down", [DEPTH, DFF, D])]
        self.out = self.nc.dram_tensor("outT", [D, S], F32, kind="ExternalOutput").ap()
        self.h = self.dscr("h_scr", [D, NT], F32)
        self.h_tk = [Tk() for _ in TILES]
        self.w_in = self.din("w_in_ext", [DEPTH, D, NWIN])
        self.w_uq = self.din("w_uq_ext", [DEPTH, 256, 768])
        self.w_ukv = self.din("w_ukv_ext", [DEPTH, 128, 512])
        self.qkn = self.din("qkn", [DEPTH, 128, 3])
        self.rope = self.din("rope_tab", [4, 128, NT])
        self.lam_in = self.din("diff_lam", [DEPTH, 1, 128])
        self.subln = self.din("diff_subln", [DEPTH, 64, 1])
        self.sink_in = self.din("gqa_sink", [DEPTH, 1, 4])
        self.na_bias = self.din("na_bias", [DEPTH, 3, 4, 128, 8, 512])
        self.win_mask = self.din("win_mask", [128, 6, 512])
        self.ident_in = self.din("ident", [128, 128])
        self.w_gate = self.din("w_gate", [DEPTH, 4, D, D])
        self.w_branch = self.din("w_branch", [DEPTH, 4, 256, D])
        self.w_out = self.din("w_out", [DEPTH, D, D])
        self.b_gate = self.din("b_gate", [DEPTH, 128, 4, 8])
        self.U = self.dscr("U_scr", [D, NT], BF16)
        self.QA = self.dscr("QA", [4, 96, NT], BF16)
        self.KA = self.dscr("KA", [4, 96, NT], BF16)
        self.VA = self.dscr("VA", [NT, 260], BF16)
        self.QB = self.dscr("QB", [2, 128, NT], BF16)
        self.KB_ = self.dscr("KB", [2, 128, NT], BF16)
        self.VB = self.dscr("VB", [NT, 260], BF16)
        self.QC = self.dscr("QC", [2, 128, NT], BF16)
        self.KC = self.dscr("KC", [2, 128, NT], BF16)
        self.VC = self.dscr("VC", [NT, 260], BF16)
        self.QD = self.dscr("QD", [2, 128, NT], BF16)
        self.KD = self.dscr("KD", [2, 128, NT], BF16)
        self.VD = self.dscr("VD", [NT, 130], BF16)
        self.Y = self.dscr("Y_scr", [4, 256, NT], BF16)

        with ExitStack() as es:
            K = self.K = KB(nc, es)
            self.ones_bf = K.sb(es, "ones_bf", [128, 128], BF16)
            self.ones_f = K.sb(es, "ones_f", [128, 128], F32)
            self.gs = K.sb(es, "gs", [128, DEPTH, 2, 3, 8], F32)
            self.sh = K.sb(es, "sh", [128, DEPTH, 2, 3, 8], F32)
            self.gt = K.sb(es, "gt", [128, DEPTH, 2, 3, 8], F32)
            self.eps_t = K.sb(es, "eps_t", [128, 4], F32)
            self.ident = K.sb(es, "ident_bf", [128, 128], BF16)
            K.dma(K.sp, self.ones_f.t[:, :], self.ident_in[:, :], writes=[self.ones_f])
            K.op(K.dve, lambda: nc.vector.tensor_copy(out=self.ident.t[:], in_=self.ones_f.t[:]), reads=[self.ones_f],
                 writes=[self.ident])
            K.op(K.dve, lambda: nc.vector.memset(self.ones_bf.t[:], 1.0), writes=[self.ones_bf])
            K.op(K.dve, lambda: nc.vector.memset(self.ones_f.t[:], 0.0), writes=[self.ones_f])
            K.op(K.dve, lambda: nc.vector.memset(self.ones_f.t[64:65, :], 1.0), writes=[self.ones_f])
            K.op(K.dve, lambda: nc.vector.memset(self.eps_t.t[:, 0:1], EPS), writes=[self.eps_t])
            K.op(K.dve, lambda: nc.vector.memset(self.eps_t.t[:, 1:2], SUBLN_EPS), writes=[self.eps_t])
            for l_ in range(DEPTH):
                li_ = 1.0 - (0.8 - 0.6 * math.exp(-0.3 * l_))
                K.op(K.dve, lambda l_=l_, li_=li_: nc.vector.memset(self.eps_t.t[:, 2 + l_:3 + l_], SUBLN_EPS / (li_ * li_)),
                     writes=[self.eps_t])
            self.phase_mod()
            K.barrier()
            if self.stop_after == "mod":
                return self.finish()
            first = True
            for l in range(DEPTH):
                last = l == DEPTH - 1
                self.phase_ffn(l, 0, src_is_input=first, skip_ctx=False)
                first = False
                K.barrier()
                if self.stop_after == f"ffn1_{l}":
                    return self.finish()
                self.phase_proj(l)
                K.barrier()
                if self.stop_after == f"proj_{l}":
                    return self.finish()
                for mx in "ABCD":
                    self.phase_attn(l, mx, with_ctx=not last)
                    K.barrier()
                    if self.stop_after == f"attn{mx}_{l}":
                        return self.finish()
                self.phase_merge(l, with_ctx=not last)
                K.barrier()
                if self.stop_after == f"merge_{l}":
                    return self.finish()
                self.phase_ffn(l, 1, src_is_input=False, skip_ctx=last)
                K.barrier()
            self.phase_final()
            K.barrier()
        return self.finish()

    def finish(self):
        self.K.barrier()
        return self.nc

    def phase_mod(self):
        nc, K = self.nc, self.K
        with ExitStack() as es:
            s_in = K.sb(es, "s_in", [128, 8, 2], F32)
            wst = [K.sb(es, f"wada{i}", [128, 8, 1024], F32) for i in range(2)]
            bsb = K.sb(es, "bsb", [128, DEPTH, 72], F32)
            nsb = K.sb(es, "nsb", [128, DEPTH, 3, 8], F32)
            modsb = K.sb(es, "modsb", [128, DEPTH, 2, 72], F32)
            K.dma(K.sp, s_in.t[:], self.c_in[:, :, :], writes=[s_in])
            K.dma(K.sp, bsb.t[:], self.b_ada.rearrange("l p j -> p l j"), writes=[bsb])
            K.dma(K.sp, nsb.t[:], self.norms.rearrange("l p s k -> p l s k"), writes=[nsb])
            K.op(K.act, lambda: nc.scalar.activation(out=s_in.t[:], in_=s_in.t[:], func=AF.Silu), reads=[s_in], writes=[s_in])
            ps = K.psum[0]
            it = 0
            for l in range(DEPTH):
                for g in range(9):
                    w = wst[it % 2]
                    it += 1
                    src = self.w_ada[l, :, g * 1024:(g + 1) * 1024].rearrange("(k p) n -> p k n", p=128)
                    for half in range(2):
                        K.dma(K.sp, w.t[:, half * 4:(half + 1) * 4, :], src[:, half * 4:(half + 1) * 4, :],
                              writes=[w] if half == 0 else [], nosync=[w] if half == 1 else [])
                    for n in range(8):
                        col = (l * 72 + g * 8 + n) * 2
                        for k in range(8):
                            K.mm(ps, ps.t[:, col:col + 2], w.t[:, k, n * 128:(n + 1) * 128], s_in.t[:, k, :],
                                 start=(k == 0), stop=(k == 7), reads=[w, s_in])
            psv = ps.t[:, 0:DEPTH * 72 * 2].rearrange("p (l j v) -> p l j v", l=DEPTH, v=2)
            for l in range(DEPTH):
                for v in range(2):
                    K.op(K.dve, lambda l=l, v=v: nc.vector.tensor_tensor(out=modsb.t[:, l, v, :], in0=psv[:, l, :, v],
                                                                        in1=bsb.t[:, l, :], op=ALU.add),
                         reads=[ps, bsb], writes=[modsb])
            for l in range(DEPTH):
                for v in range(2):
                    for s in range(3):
                        o = 3 * s * 8
                        K.op(K.dve, lambda l=l, v=v, s=s, o=o: nc.vector.scalar_tensor_tensor(
                            out=self.gs.t[:, l, v, s, :], in0=modsb.t[:, l, v, o + 8:o + 16], scalar=1.0,
                            in1=nsb.t[:, l, s, :], op0=ALU.add, op1=ALU.mult),
                            reads=[modsb, nsb], writes=[self.gs])
                        K.op(K.dve, lambda l=l, v=v, s=s, o=o: nc.vector.tensor_copy(
                            out=self.sh.t[:, l, v, s, :], in_=modsb.t[:, l, v, o:o + 8]),
                            reads=[modsb], writes=[self.sh])
                        K.op(K.dve, lambda l=l, v=v, s=s, o=o: nc.vector.tensor_scalar(
                            out=self.gt.t[:, l, v, s, :], in0=modsb.t[:, l, v, o + 16:o + 24],
                            scalar1=(1.0 if s == 1 else 0.5), scalar2=None, op0=ALU.mult),
                            reads=[modsb], writes=[self.gt])
            if "mod_dbg" in self.debug:
                dbg = self.nc.dram_tensor("mod_dbg", [128, DEPTH * 2 * 72], F32, kind="ExternalOutput").ap()
                K.dma(K.sp, dbg[:, :], modsb.t[:].rearrange("p l v j -> p (l v j)"), reads=[modsb])

    def norm_mod(self, es_bufs, xin, n, gs_ap, sh_ap, u, nk=8):
        nc, K = self.nc, self.K
        sq, xn, rt, rstd, ps_ss = es_bufs
        for k in range(nk):
            s = sq[k % 2]
            K.op(K.act, lambda k=k, s=s: nc.scalar.activation(out=s.t[:, :n], in_=xin.t[:, k, :n], func=AF.Square),
                 reads=[xin], writes=[s])
            K.mm(ps_ss, ps_ss.t[:, :n], self.ones_bf.t[:, :], s.t[:, :n], start=(k == 0), stop=(k == nk - 1),
                 reads=[s, self.ones_bf])
        K.op(K.act, lambda: nc.scalar.activation(out=rt.t[:, :n], in_=ps_ss.t[:, :n], func=AF.Sqrt,
                                                 scale=1.0 / (128 * nk), bias=self.eps_t.t[:, 0:1]),
             reads=[ps_ss, self.eps_t], writes=[rt])
        K.op(K.dve, lambda: nc.vector.reciprocal(out=rstd.t[:, :n], in_=rt.t[:, :n]), reads=[rt], writes=[rstd])
        for k in range(nk):
            x2 = xn[k % 2]
            K.op(K.dve, lambda k=k, x2=x2: nc.vector.tensor_tensor(out=x2.t[:, :n], in0=xin.t[:, k, :n], in1=rstd.t[:, :n],
                                                                    op=ALU.mult),
                 reads=[xin, rstd], writes=[x2])
            if sh_ap is not None:
                K.op(K.act, lambda k=k, x2=x2: nc.scalar.activation(out=u.t[:, k, :n], in_=x2.t[:, :n], func=AF.Identity,
                                                                     scale=gs_ap(k), bias=sh_ap(k)),
                     reads=[x2, self.gs, self.sh], nosync=[u] if k else [], writes=[] if k else [u])
            else:
                K.op(K.act, lambda k=k, x2=x2: nc.scalar.activation(out=u.t[:, k, :n], in_=x2.t[:, :n], func=AF.Identity,
                                                                     scale=gs_ap(k)),
                     reads=[x2], nosync=[u] if k else [], writes=[] if k else [u])

    def norm_bufs(self, es, pfx, ps_idx):
        K = self.K
        sq = [K.sb(es, f"{pfx}sq{i}", [128, 512], BF16) for i in range(2)]
        xn = [K.sb(es, f"{pfx}xn{i}", [128, 512], F32) for i in range(2)]
        rt = K.sb(es, f"{pfx}rt", [128, 512], F32)
        rstd = K.sb(es, f"{pfx}rstd", [128, 512], F32)
        return (sq, xn, rt, rstd, K.psum[ps_idx])

    def phase_ffn(self, l, which, src_is_input, skip_ctx):
        nc, K = self.nc, self.K
        s_idx = 0 if which == 0 else 2
        with ExitStack() as es:
            wgu = K.sb(es, "wgu", [128, 8, 2 * DFF], BF16)
            wdn = K.sb(es, "wdn", [128, NJ, D], BF16)
            xin = K.sb(es, "xin", [128, 8, 512], F32)
            stg = [Tk(xin.t[:, 4 * i:4 * i + 4, :].rearrange("p a b -> p (a b)")) for i in range(2)]
            u = K.sb(es, "u", [128, 8, 512], BF16)
            hmid = K.sb(es, "hmid", [128, NJ, 512], BF16)
            sg = [K.sb(es, f"sg{i}", [128, 512], F32) for i in range(2)]
            xr = [K.sb(es, f"xr{i}", [128, 512], F32) for i in range(3)]
            nb = self.norm_bufs(es, "f", 0)
            idx = 0
            src_gu = self.w_gu[which][l].rearrange("(k p) n -> p k n", p=128)
            for k in range(8):
                for c0 in range(0, 2 * DFF, 2048):
                    c1 = min(c0 + 2048, 2 * DFF)
                    K.load_cast(stg, wgu, wgu.t[:, k, c0:c1], src_gu[:, k, c0:c1], idx)
                    idx += 1
            src_dn = self.w_dn[which][l].rearrange("(j p) n -> p j n", p=128)
            for j in range(0, NJ, 2):
                K.load_cast(stg, wdn, wdn.t[:, j:j + 2, :], src_dn[:, j:j + 2, :], idx)
                idx += 1
            hsrc = self.hT if src_is_input else self.h
            K.barrier()
            tis = [ti for ti in range(len(TILES)) if not (ti == 0 and skip_ctx)]

            def load_x(ti):
                t0, n = TILES[ti]
                K.dma(K.sp, xin.t[:, :, :n], hsrc[:, t0:t0 + n].rearrange("(k p) n -> p k n", p=128), writes=[xin])

            def norm(ti):
                t0, n = TILES[ti]
                v = 1 if ti == 0 else 0
                self.norm_mod(nb, xin, n,
                              lambda k: self.gs.t[:, l, v, s_idx, k:k + 1],
                              lambda k: self.sh.t[:, l, v, s_idx, k:k + 1], u)

            load_x(tis[0])
            norm(tis[0])
            xrc = 0
            for ii, ti in enumerate(tis):
                t0, n = TILES[ti]
                v = 1 if ti == 0 else 0
                nxt = tis[ii + 1] if ii + 1 < len(tis) else None
                if nxt is not None:
                    load_x(nxt)
                for j in range(NJ):
                    pg = K.psum[1 + (j % 2)]
                    pu = K.psum[3 + (j % 2)]
                    for k in range(8):
                        K.mm(pg, pg.t[:, :n], wgu.t[:, k, j * 128:(j + 1) * 128], u.t[:, k, :n],
                             start=(k == 0), stop=(k == 7), reads=[wgu, u])
                    for k in range(8):
                        K.mm(pu, pu.t[:, :n], wgu.t[:, k, DFF + j * 128:DFF + (j + 1) * 128], u.t[:, k, :n],
                             start=(k == 0), stop=(k == 7), reads=[wgu, u])
                    s = sg[j % 2]
                    K.op(K.act, lambda s=s, pg=pg: nc.scalar.activation(out=s.t[:, :n], in_=pg.t[:, :n], func=AF.Silu),
                         reads=[pg], writes=[s])
                    K.op(K.dve, lambda s=s, pu=pu, j=j: nc.vector.tensor_tensor(out=hmid.t[:, j, :n], in0=s.t[:, :n],
                                                                               in1=pu.t[:, :n], op=ALU.mult),
                         reads=[s, pu], writes=[hmid] if j == 0 else [], nosync=[hmid] if j else [])
                if nxt is not None:
                    norm(nxt)
                for c in range(8):
                    po = K.psum[5 + (c % 2)]
                    x_ = xr[xrc % 3]
                    xrc += 1
                    hrow = hsrc[c * 128:(c + 1) * 128, t0:t0 + n]
                    K.dma(K.sp, x_.t[:, :n], hrow, writes=[x_])
                    for j in range(NJ):
                        K.mm(po, po.t[:, :n], wdn.t[:, j, c * 128:(c + 1) * 128], hmid.t[:, j, :n],
                             start=(j == 0), stop=(j == NJ - 1), reads=[wdn, hmid])
                    K.op(K.dve, lambda c=c, po=po, x_=x_: nc.vector.scalar_tensor_tensor(
                        out=x_.t[:, :n], in0=po.t[:, :n], scalar=self.gt.t[:, l, v, s_idx, c:c + 1],
                        in1=x_.t[:, :n], op0=ALU.mult, op1=ALU.add),
                        reads=[po, self.gt, x_], writes=[x_])
                    K.dma(K.sp, self.h[c * 128:(c + 1) * 128, t0:t0 + n], x_.t[:, :n], reads=[x_])

    def phase_proj(self, l):
        nc, K = self.nc, self.K
        with ExitStack() as es:
            win = K.sb(es, "win", [128, 8, NWIN], BF16)
            wuq = K.sb(es, "wuq", [128, 2, 768], BF16)
            wukv = K.sb(es, "wukv", [128, 512], BF16)
            qkn = K.sb(es, "qkn", [128, 3], F32)
            xinb = [K.sb(es, f"xin{i}", [128, 8, 512], F32) for i in range(2)]
            stg = [Tk(xinb[0].t[:, 4 * i:4 * i + 4, :].rearrange("p a b -> p (a b)")) for i in range(2)]
            ubuf = [K.sb(es, f"u{i}", [128, 8, 512], BF16) for i in range(2)]
            rtab = [K.sb(es, f"rtab{i}", [128, 4, 512], F32) for i in range(2)]
            cq_sb = K.sb(es, "cq_sb", [128, 2, 512], F32)
            ckv_sb = K.sb(es, "ckv_sb", [128, 1, 512], F32)
            cqn = K.sb(es, "cqn", [128, 2, 512], BF16)
            ckvn = K.sb(es, "ckvn", [128, 1, 512], BF16)
            stages = [K.sb(es, f"pst{i}", [128, 512], BF16) for i in range(6)]
            vst = [K.sb(es, f"vst{i}", [128, 260], BF16) for i in range(4)]
            ra = [K.sb(es, f"ra{i}", [128, 512], F32) for i in range(2)]
            rb = [K.sb(es, f"rb{i}", [128, 512], F32) for i in range(2)]
            nb = self.norm_bufs(es, "p", 0)
            for v_ in vst:
                K.op(K.dve, lambda v_=v_: nc.vector.memset(v_.t[:], 1.0), writes=[v_])
            K.dma(K.sp, qkn.t[:], self.qkn[l], writes=[qkn])
            idx = 0
            src = self.w_in[l].rearrange("(k p) n -> p k n", p=128)
            for k in range(8):
                for c0 in range(0, NWIN, 2048):
                    c1 = min(c0 + 2048, NWIN)
                    K.load_cast(stg, win, win.t[:, k, c0:c1], src[:, k, c0:c1], idx)
                    idx += 1
            K.load_cast(stg, wuq, wuq.t[:, :, :], self.w_uq[l].rearrange("(k p) n -> p k n", p=128), idx)
            idx += 1
            K.load_cast(stg, wukv, wukv.t[:, :], self.w_ukv[l], idx)
            idx += 1
            K.barrier()
            st = {"bank": 0, "stage": 0, "vst": 0, "ev": 0, "rr": 0}

            def bank():
                b = K.psum[1 + st["bank"] % 7]
                st["bank"] += 1
                return b

            def stage():
                t = stages[st["stage"] % len(stages)]
                st["stage"] += 1
                return t

            def vstage():
                t = vst[st["vst"] % len(vst)]
                st["vst"] += 1
                return t

            def evac(dst_tk, dst_ap, ps_tk, src_ap, nosync=False, eng=None):
                st["ev"] += 1
                w = [] if nosync else [dst_tk]
                ns = [dst_tk] if nosync else []
                use_act = (st["ev"] % 2 == 0) if eng is None else (eng == "act")
                if use_act:
                    K.op(K.act, lambda: nc.scalar.activation(out=dst_ap, in_=src_ap, func=AF.Copy), reads=[ps_tk], writes=w, nosync=ns)
                else:
                    K.op(K.dve, lambda: nc.vector.tensor_copy(out=dst_ap, in_=src_ap), reads=[ps_tk], writes=w, nosync=ns)

            def p_load(ti):
                t0, n = TILES[ti]
                K.dma(K.sp, xinb[ti % 2].t[:, :, :n], self.h[:, t0:t0 + n].rearrange("(k p) n -> p k n", p=128),
                      writes=[xinb[ti % 2]])
                K.dma(K.sp, rtab[ti % 2].t[:, :, :n], self.rope[:, :, t0:t0 + n].rearrange("a p n -> p a n"),
                      writes=[rtab[ti % 2]])

            def p_norm(ti):
                t0, n = TILES[ti]
                v = 1 if ti == 0 else 0
                self.norm_mod(nb, xinb[ti % 2], n, lambda k: self.gs.t[:, l, v, 1, k:k + 1],
                              lambda k: self.sh.t[:, l, v, 1, k:k + 1], ubuf[ti % 2])
                K.dma(K.sp, self.U[:, t0:t0 + n].rearrange("(k p) n -> p k n", p=128), ubuf[ti % 2].t[:, :, :n],
                      reads=[ubuf[ti % 2]])

            NTI = min(len(TILES), DBG_TILES)
            p_load(0)
            if NTI > 1:
                p_load(1)
            p_norm(0)
            for ti, (t0, n) in enumerate(TILES):
                if ti >= NTI:
                    break
                u = ubuf[ti % 2]
                rt_ = rtab[ti % 2]

                def proj(col0, M):
                    b = bank()
                    for k in range(8):
                        K.mm(b, b.t[0:M, :n], win.t[:, k, col0:col0 + M], u.t[:, k, :n], start=(k == 0), stop=(k == 7),
                             reads=[win, u])
                    return b

                def rope_out(bn, bs, p0, p1, tab, dst_tk, dst_ap_fn):
                    i = st["rr"] % 2
                    st["rr"] += 1
                    a_, b_ = ra[i], rb[i]
                    K.op(K.dve, lambda: nc.vector.tensor_tensor(out=a_.t[p0:p1, :n], in0=bn.t[p0:p1, :n],
                                                                in1=rt_.t[p0:p1, tab, :n], op=ALU.mult),
                         reads=[bn, rt_], writes=[a_])
                    K.op(K.dve, lambda: nc.vector.tensor_tensor(out=b_.t[p0:p1, :n], in0=bs.t[p0:p1, :n],
                                                                in1=rt_.t[p0:p1, tab + 1, :n], op=ALU.mult),
                         reads=[bs, rt_], writes=[b_])
                    K.op(K.pool, lambda: nc.gpsimd.tensor_tensor(out=dst_ap_fn(p0, p1), in0=a_.t[p0:p1, :n],
                                                                 in1=b_.t[p0:p1, :n], op=ALU.add),
                         reads=[a_, b_], writes=[dst_tk])

                if DBG_STEP < 2:
                    continue
                for k in range(2):
                    b = proj(k * 128, 128)
                    evac(cq_sb, cq_sb.t[:, k, :n], b, b.t[:, :n], nosync=(k == 1))
                self.norm_mod(nb, cq_sb, n, lambda k: qkn.t[:, k:k + 1], None, cqn, nk=2)
                b = proj(256, 128)
                evac(ckv_sb, ckv_sb.t[:, 0, :n], b, b.t[:, :n])
                self.norm_mod(nb, ckv_sb, n, lambda k: qkn.t[:, 2:3], None, ckvn, nk=1)
                if ti + 1 < NTI:
                    p_norm(ti + 1)
                if DBG_STEP < 3:
                    continue
                bn, bs = proj(384, 128), proj(2464, 128)
                sg_ = stage()
                rope_out(bn, bs, 0, 32, 0, sg_, lambda p0, p1: sg_.t[p0:p1, :n])
                for h in range(4):
                    K.dma(K.sp, self.KA[h, 64:96, t0:t0 + n], sg_.t[0:32, :n], reads=[sg_])
                if DBG_STEP < 4:
                    continue
                for (c_n, c_s, tab, dst) in [
                    (416, 2496, 0, self.QB[0]), (544, 2624, 0, self.QB[1]),
                    (672, 2752, 0, self.KB_[0]), (800, 2880, 0, self.KB_[1]),
                    (1952, 3008, 2, self.QD[0]), (2080, 3136, 2, self.QD[1]),
                ]:
                    bn, bs = proj(c_n, 128), proj(c_s, 128)
                    sg_ = stage()
                    rope_out(bn, bs, 0, 128, tab, sg_, lambda p0, p1, sg_=sg_: sg_.t[p0:p1, :n])
                    K.dma(K.sp, dst[:, t0:t0 + n], sg_.t[:, :n], reads=[sg_])
                if DBG_STEP < 5:
                    continue
                bn, bs = proj(2208, 128), proj(3264, 128)
                sg_ = stage()
                rope_out(bn, bs, 0, 128, 2, sg_, lambda p0, p1, sg_=sg_: sg_.t[p0:p1, :n])
                for g in range(2):
                    for dup in range(2):
                        K.dma(K.sp, self.KD[g, 64 * dup:64 * dup + 64, t0:t0 + n], sg_.t[64 * g:64 * g + 64, :n], reads=[sg_])
                if DBG_STEP < 6:
                    continue
                for (c_n, dst) in [(1184, self.QC[0]), (1312, self.QC[1]), (1440, self.KC[0]), (1568, self.KC[1])]:
                    b = proj(c_n, 128)
                    sg_ = stage()
                    evac(sg_, sg_.t[:, :n], b, b.t[:, :n])
                    K.dma(K.sp, dst[:, t0:t0 + n], sg_.t[:, :n], reads=[sg_])
                if DBG_STEP < 7:
                    continue
                for h in range(4):
                    b1, b2 = bank(), bank()
                    for k in range(2):
                        K.mm(b1, b1.t[0:96, :n], wuq.t[:, k, h * 192:h * 192 + 96], cqn.t[:, k, :n], start=(k == 0), stop=(k == 1),
                             reads=[wuq, cqn])
                    for k in range(2):
                        K.mm(b2, b2.t[0:96, :n], wuq.t[:, k, h * 192 + 96:h * 192 + 192], cqn.t[:, k, :n], start=(k == 0),
                             stop=(k == 1), reads=[wuq, cqn])
                    sg_ = stage()
                    evac(sg_, sg_.t[0:64, :n], b1, b1.t[0:64, :n], eng="dve")
                    rope_out(b1, b2, 64, 96, 0, sg_, lambda p0, p1, sg_=sg_: sg_.t[p0:p1, :n])
                    K.dma(K.sp, self.QA[h, :, t0:t0 + n], sg_.t[0:96, :n], reads=[sg_])
                if DBG_STEP < 8:
                    continue
                for i in range(2):
                    b = bank()
                    K.mm(b, b.t[:, :n], wukv.t[:, i * 128:(i + 1) * 128], ckvn.t[:, 0, :n], start=True, stop=True,
                         reads=[wukv, ckvn])
                    sg_ = stage()
                    evac(sg_, sg_.t[:, :n], b, b.t[:, :n])
                    for hh in range(2):
                        K.dma(K.sp, self.KA[2 * i + hh, 0:64, t0:t0 + n], sg_.t[64 * hh:64 * hh + 64, :n], reads=[sg_])
                if DBG_STEP < 9:
                    continue
                for sblk in range(n // 128):
                    tok = slice(sblk * 128, (sblk + 1) * 128)
                    r0 = t0 + sblk * 128
                    b = bank()
                    K.mm(b, b.t[:, 0:256], ckvn.t[:, 0, tok], wukv.t[:, 256:512], start=True, stop=True, reads=[wukv, ckvn])
                    vs = vstage()
                    if DBG_STEP >= 10:
                        evac(vs, vs.t[:, :].rearrange("p (h e) -> p h e", e=65)[:, :, 0:64],
                             b, b.t[:, 0:256].rearrange("p (h e) -> p h e", e=64))
                    if DBG_STEP >= 11:
                        K.dma(K.sp, self.VA[r0:r0 + 128, :], vs.t[:, :], reads=[vs])
                    if DBG_STEP < 12:
                        continue
                    bA, bB = bank(), bank()
                    for k in range(8):
                        K.mm(bA, bA.t[:, 0:512], u.t[:, k, tok], win.t[:, k, 3392:3904], start=(k == 0), stop=(k == 7),
                             reads=[win, u])
                    for k in range(8):
                        K.mm(bB, bB.t[:, 0:128], u.t[:, k, tok], win.t[:, k, 3904:4032], start=(k == 0), stop=(k == 7),
                             reads=[win, u])
                    if sblk == 0 and ti + 2 < NTI:
                        p_load(ti + 2)
                    for (bb, c0, nh, dst) in [(bA, 0, 4, self.VB), (bA, 256, 4, self.VC), (bB, 0, 2, self.VD)]:
                        vs = vstage()
                        if DBG_STEP < 13:
                            continue
                        evac(vs, vs.t[:, 0:nh * 65].rearrange("p (h e) -> p h e", e=65)[:, :, 0:64],
                             bb, bb.t[:, c0:c0 + nh * 64].rearrange("p (h e) -> p h e", e=64),
                             eng=("dve" if bb is bA else "act"))
                        if DBG_STEP < 14:
                            continue
                        K.dma(K.sp, dst[r0:r0 + 128, :], vs.t[:, 0:nh * 65], reads=[vs])


    def phase_attn(self, l, mx, with_ctx):
        nc, K = self.nc, self.K
        mi = "ABCD".index(mx)
        lam_init = 0.8 - 0.6 * math.exp(-0.3 * l)
        with ExitStack() as es:
            nvh = 2 if mx == "D" else 4
            vt = K.sb(es, "vt", [128, 66, nvh * 65], BF16)
            Vsrc = {"A": self.VA, "B": self.VB, "C": self.VC, "D": self.VD}[mx]
            vsrc = Vsrc.rearrange("(c p) f -> p c f", p=128)
            for c0 in range(0, 66, 11):
                K.dma(K.sp, vt.t[:, c0:c0 + 11, :], vsrc[:, c0:c0 + 11, :], nosync=[vt])
            if mx == "A":
                kt = [K.sb(es, f"kt{h}", [96, NT], BF16) for h in range(4)]
                for h in range(4):
                    K.dma(K.sp, kt[h].t[:, :], self.KA[h], writes=[kt[h]])
                qsb = [K.sb(es, f"q{i}", [96, 4, 512], BF16) for i in range(2)]
                Qsrc = self.QA
            else:
                Ksrc = {"B": self.KB_, "C": self.KC, "D": self.KD}[mx]
                kt = [K.sb(es, f"kt{h}", [128, NT], BF16) for h in range(2)]
                for h in range(2):
                    K.dma(K.sp, kt[h].t[:, :], Ksrc[h], writes=[kt[h]])
                qsb = [K.sb(es, f"q{i}", [128, 2, 512], BF16) for i in range(2)]
                Qsrc = {"B": self.QB, "C": self.QC, "D": self.QD}[mx]
            ptiles = [K.sb(es, f"pt{i}", [128, 1024], BF16) for i in range(4)]
            rd = [K.sb(es, f"rd{i}", [128, 512], F32) for i in range(2)]
            rbs = [K.sb(es, f"rbs{i}", [128, 512], F32) for i in range(2)]
            yst = [K.sb(es, f"yst{i}", [128, 512], BF16) for i in range(3)]
            for r_ in rd:
                K.op(K.dve, lambda r_=r_: nc.vector.memset(r_.t[:], 0.0), writes=[r_])
            if mx != "A":
                nrng = 4 if mx == "B" else 2
                qm = [[K.sb(es, f"qm{a}_{b_}", [128, 512], BF16) for b_ in range(2)] for a in range(nrng)]
                for a in range(nrng):
                    for b_ in range(2):
                        K.op(K.pool, lambda a=a, b_=b_: nc.gpsimd.memset(qm[a][b_].t[:], 0.0), writes=[qm[a][b_]])
                qmc = [0] * nrng

                def masked_q(q, c2, p0, W, n):
                    a = p0 // W
                    t = qm[a][qmc[a] % 2]
                    qmc[a] += 1
                    K.op(K.pool, lambda: nc.gpsimd.tensor_copy(out=t.t[p0:p0 + W, :n], in_=q.t[p0:p0 + W, c2, :n]),
                         reads=[q], writes=[t])
                    return t
            sgroups = [Tk(K.pfull[:, 1024 * g:1024 * (g + 1)]) for g in range(3)]
            ybanks = K.psum[6:8]

            def next_bb():
                g_ = sgroups[cnt["s"] % 3]
                cnt["s"] += 1
                return g_
            cnt = {"y": 0, "b": 0, "yst": 0, "tmp": 0, "s": 0, "p": 0}
            if mx == "C":
                biasf = K.sb(es, "biasf", [128, 8, 512], F32)
                biasb = [K.sb(es, f"biasb{i}", [128, 8, 512], BF16) for i in range(2)]
            if mx == "D":
                maskf = K.sb(es, "maskf", [128, 6, 512], F32)
                maskt = K.sb(es, "maskt", [128, 6, 512], BF16)
                K.dma(K.sp, maskf.t[:], self.win_mask[:, :, :], writes=[maskf])
                K.op(K.pool, lambda: nc.gpsimd.tensor_copy(out=maskt.t[:], in_=maskf.t[:]), reads=[maskf], writes=[maskt])
                esk = K.sb(es, "esk", [128, 4], F32)
                K.dma(K.sp, esk.t[64:65, :], self.sink_in[l], writes=[esk])
                K.op(K.act, lambda: nc.scalar.activation(out=esk.t[64:65, :], in_=esk.t[64:65, :], func=AF.Exp),
                     reads=[esk], writes=[esk])
            if mx == "B":
                lamt = K.sb(es, "lamt", [128, 128], F32)
                lprod = K.sb(es, "lprod", [128, 64], F32)
                lsum = K.sb(es, "lsum", [128, 2], F32)
                nlam = K.sb(es, "nlam", [128, 1], F32)
                subg = K.sb(es, "subg", [64, 1], F32)
                y1n = K.sb(es, "y1n", [64, 512], F32)
                t2 = K.sb(es, "t2", [64, 512], F32)
                yd = K.sb(es, "yd", [64, 512], F32)
                sq = K.sb(es, "bsq", [128, 512], BF16)
                K.op(K.dve, lambda: nc.vector.memset(sq.t[:], 0.0), writes=[sq])
                brt = K.sb(es, "brt", [64, 512], F32)
                brs = K.sb(es, "brs", [64, 512], F32)
                K.dma(K.sp, lamt.t[64:65, :], self.lam_in[l], writes=[lamt])
                K.dma(K.sp, subg.t[:, :], self.subln[l], writes=[subg])
                lv = lamt.t[64:65, :].rearrange("p (a b c) -> p a b c", a=2, b=2)
                K.op(K.dve, lambda: nc.vector.tensor_tensor(out=lprod.t[64:65, :].rearrange("p (a c) -> p a c", a=2),
                                                            in0=lv[:, :, 0, :], in1=lv[:, :, 1, :], op=ALU.mult),
                     reads=[lamt], writes=[lprod])
                K.op(K.dve, lambda: nc.vector.reduce_sum(out=lsum.t[64:65, :], in_=lprod.t[64:65, :].rearrange("p (a c) -> p a c", a=2),
                                                         axis=mybir.AxisListType.X), reads=[lprod], writes=[lsum])
                K.op(K.act, lambda: nc.scalar.activation(out=lsum.t[64:65, :], in_=lsum.t[64:65, :], func=AF.Exp),
                     reads=[lsum], writes=[lsum])
                K.op(K.dve, lambda: nc.vector.tensor_tensor(out=nlam.t[64:65, :], in0=lsum.t[64:65, 1:2], in1=lsum.t[64:65, 0:1],
                                                            op=ALU.subtract), reads=[lsum], writes=[nlam])
                K.op(K.dve, lambda: nc.vector.tensor_scalar(out=nlam.t[64:65, :], in0=nlam.t[64:65, :], scalar1=-lam_init,
                                                            scalar2=None, op0=ALU.add), reads=[nlam], writes=[nlam])
                if "lam_dbg" in self.debug:
                    dbg = self.nc.dram_tensor("lam_dbg", [1, 1], F32, kind="ExternalOutput").ap()
                    K.dma(K.sp, dbg[:, :], nlam.t[64:65, :], reads=[nlam])

            def gview(t, G, n):
                return t.t[:, 0:G * 512].rearrange("p (g x) -> p g x", x=512)[:, :, :n]

            def core(n, q_ap, q_tk, kT_fn, k_tk, v_fn, groups, scale, ybank, M, bias_fn=None, after=None):
                ng = len(groups)
                total = sum(len(g) for g in groups)
                hist = []
                done = 0
                LA = 2
                for i in range(ng + LA):
                    if i < ng:
                        gl = groups[i]
                        G = len(gl)
                        sg = sgroups[cnt["s"] % 3]
                        cnt["s"] += 1
                        bias = bias_fn(i) if bias_fn is not None else None
                        for a, c in enumerate(gl):
                            K.mm(sg, sg.t[:, a * 512:a * 512 + n], kT_fn(c), q_ap, True, bias is None, reads=[q_tk, k_tk])
                            if bias is not None:
                                K.mm(sg, sg.t[:, a * 512:a * 512 + n], self.ident.t[:, :], bias[1](a), False, True,
                                     reads=[bias[0], self.ident])
                        p = ptiles[cnt["p"] % 4]
                        cnt["p"] += 1
                        if True:
                            K.op(K.act, lambda sg=sg, p=p, G=G: nc.scalar.activation(out=gview(p, G, n), in_=gview(sg, G, n),
                                                                                     func=AF.Exp, scale=scale),
                                 reads=[sg], writes=[p])
                        hist.append((p, gl))
                    if i >= LA:
                        p, gl = hist[i - LA]
                        for a, c in enumerate(gl):
                            K.mm(ybank, ybank.t[0:M, :n], v_fn(c), p.t[:, a * 512:a * 512 + n], start=(done == 0),
                                 stop=(done == total - 1), reads=[vt, p])
                            done += 1
                    if i == min(6, ng) and after is not None:
                        after()
                        after = None
                if after is not None:
                    after()

            def finalize(n, ybank, out_tk, out_ap, den_add=None, mul=None):
                r_ = rd[cnt["b"] % 2]
                rb_ = rbs[cnt["b"] % 2]
                bb = next_bb()
                cnt["b"] += 1
                if den_add is not None:
                    K.op(K.dve, lambda: nc.vector.tensor_scalar(out=r_.t[64:65, :n], in0=ybank.t[64:65, :n], scalar1=den_add,
                                                                scalar2=None, op0=ALU.add), reads=[ybank], writes=[r_])
                    K.op(K.dve, lambda: nc.vector.reciprocal(out=r_.t[64:65, :n], in_=r_.t[64:65, :n]), reads=[r_], writes=[r_])
                else:
                    K.op(K.dve, lambda: nc.vector.reciprocal(out=r_.t[64:65, :n], in_=ybank.t[64:65, :n]), reads=[ybank], writes=[r_])
                if mul is not None:
                    K.op(K.dve, lambda: nc.vector.tensor_scalar(out=r_.t[64:65, :n], in0=r_.t[64:65, :n], scalar1=mul,
                                                                scalar2=None, op0=ALU.mult), reads=[r_], writes=[r_])
                K.mm(bb, bb.t[:, :n], self.ones_f.t[:, :], r_.t[:, :n], True, True, reads=[r_, self.ones_f])
                K.op(K.act, lambda: nc.scalar.activation(out=rb_.t[0:64, :n], in_=bb.t[0:64, :n], func=AF.Copy),
                     reads=[bb], writes=[rb_])
                K.op(K.dve, lambda: nc.vector.tensor_tensor(out=out_ap, in0=ybank.t[0:64, :n], in1=rb_.t[0:64, :n], op=ALU.mult),
                     reads=[ybank, rb_], writes=[out_tk])

            def next_y():
                b = ybanks[cnt["y"] % 2]
                cnt["y"] += 1
                return b

            def next_yst():
                t = yst[cnt["yst"] % 3]
                cnt["yst"] += 1
                return t

            def pairs(lst):
                return [lst[i:i + 2] for i in range(0, len(lst), 2)]

            def store_y(h, t0, n, ys):
                K.dma(K.sp, self.Y[mi, h * 64:(h + 1) * 64, t0:t0 + n], ys.t[0:64, :n], reads=[ys])

            tis = [ti for ti in range(len(TILES)) if not (ti == 0 and not with_ctx)]
            if mx == "C":
                order = [(ti, h) for h in range(4) for ti in tis]
            else:
                order = [(ti, h) for ti in tis for h in range(4)]
            units = []
            for (ti, h) in order:
                if mx == "B":
                    units.append({"ti": ti, "h": h, "m": 0})
                    units.append({"ti": ti, "h": h, "m": 1})
                else:
                    units.append({"ti": ti, "h": h, "m": 0})
            pending = [None]
            last_q = [None, None]
            qctr = [0]
            last_bias = [None, None]
            bctr = [0]
            allch = list(range(66))

            def prep(U_):
                ti, h, m_ = U_["ti"], U_["h"], U_["m"]
                t0, n = TILES[ti]
                if last_q[0] != ti:
                    q = qsb[qctr[0] % 2]
                    qctr[0] += 1
                    if mx == "A":
                        K.dma(K.sp, q.t[:, :, :n], Qsrc[:, :, t0:t0 + n].rearrange("h p t -> p h t"), writes=[q])
                    else:
                        K.dma(K.sp, q.t[:, :, :n], Qsrc[:, :, t0:t0 + n].rearrange("c p t -> p c t"), writes=[q])
                    last_q[0], last_q[1] = ti, q
                q = last_q[1]
                U_["q"] = q
                if mx == "B":
                    v_ = 2 * h + m_
                    c2, pm = v_ // 4, v_ % 4
                    U_["c2"] = c2
                    U_["qt"] = masked_q(q, c2, 32 * pm, 32, n)
                elif mx in "CD":
                    c2, p0 = h // 2, 64 * (h % 2)
                    U_["c2"] = c2
                    U_["qt"] = masked_q(q, c2, p0, 64, n)
                if mx == "C" and ti != 0:
                    i_lat = ti - 1
                    typ = 0 if i_lat == 0 else (2 if i_lat == 15 else 1)
                    if last_bias[0] != (typ, h):
                        bt = biasb[bctr[0] % 2]
                        bctr[0] += 1
                        K.dma(K.sp, biasf.t[:, :, :], self.na_bias[l, typ, h], writes=[biasf])
                        K.op(K.pool, lambda bt=bt: nc.gpsimd.tensor_scalar(out=bt.t[:], in0=biasf.t[:], scalar1=1.0 / NA_SCALE,
                                                                          scalar2=None, op0=ALU.mult),
                             reads=[biasf], writes=[bt])
                        last_bias[0], last_bias[1] = (typ, h), bt
                    U_["bt"] = last_bias[1]

            def run(U_):
                ti, h, m_ = U_["ti"], U_["h"], U_["m"]
                t0, n = TILES[ti]
                i_lat = ti - 1
                q = U_["q"]
                if mx == "A":
                    groups = pairs([0, 1] if ti == 0 else allch)
                    yb = next_y()
                    core(n, q.t[0:96, h, :n], q, lambda c, h=h: kt[h].t[0:96, c * 128:(c + 1) * 128], kt[h],
                         lambda c, h=h: vt.t[:, c, h * 65:h * 65 + 65], groups, MLA_SCALE, yb, 65, after=pending[0])

                    def fin(n=n, yb=yb, h=h, t0=t0):
                        ys = next_yst()
                        finalize(n, yb, ys, ys.t[0:64, :n])
                        store_y(h, t0, n, ys)
                    pending[0] = fin
                elif mx == "B":
                    groups = pairs([0, 1] if ti == 0 else allch)
                    c2 = U_["c2"]
                    yb = next_y()
                    qt_ = U_["qt"]
                    core(n, qt_.t[:, :n], qt_,
                         lambda c, c2=c2: kt[c2].t[:, c * 128:(c + 1) * 128], kt[c2],
                         lambda c, h=h: vt.t[:, c, h * 65:h * 65 + 65], groups, DIFF_SCALE, yb, 65, after=pending[0])
                    if m_ == 0:
                        def fin(n=n, yb=yb):
                            finalize(n, yb, y1n, y1n.t[0:64, :n])
                    else:
                        def fin(n=n, yb=yb, h=h, t0=t0):
                            finalize(n, yb, t2, t2.t[0:64, :n], mul=nlam.t[64:65, 0:1])
                            K.op(K.pool, lambda: nc.gpsimd.tensor_tensor(out=yd.t[:, :n], in0=y1n.t[:, :n], in1=t2.t[:, :n],
                                                                         op=ALU.add), reads=[y1n, t2], writes=[yd])
                            K.op(K.act, lambda: nc.scalar.activation(out=sq.t[0:64, :n], in_=yd.t[:, :n], func=AF.Square),
                                 reads=[yd], writes=[sq])
                            bb = next_bb()
                            K.mm(bb, bb.t[:, :n], self.ones_bf.t[:, :], sq.t[:, :n], True, True, reads=[sq, self.ones_bf])
                            li = 1.0 - lam_init
                            K.op(K.act, lambda bb=bb: nc.scalar.activation(out=brt.t[:, :n], in_=bb.t[0:64, :n], func=AF.Sqrt,
                                                                          scale=1.0 / (64 * li * li),
                                                                          bias=self.eps_t.t[0:64, 2 + l:3 + l]),
                                 reads=[bb, self.eps_t], writes=[brt])
                            K.op(K.dve, lambda: nc.vector.reciprocal(out=brs.t[:, :n], in_=brt.t[:, :n]), reads=[brt], writes=[brs])
                            ys = next_yst()
                            K.op(K.dve, lambda ys=ys: nc.vector.scalar_tensor_tensor(out=ys.t[0:64, :n], in0=yd.t[:, :n],
                                                                                    scalar=subg.t[:, 0:1], in1=brs.t[:, :n],
                                                                                    op0=ALU.mult, op1=ALU.mult),
                                 reads=[yd, brs, subg], writes=[ys])
                            store_y(h, t0, n, ys)
                    pending[0] = fin
                elif mx == "C":
                    c2 = U_["c2"]
                    bias_fn = None
                    if ti == 0:
                        groups = pairs([0, 1])
                    else:
                        fc = min(max(4 * i_lat - 2, 0), 56)
                        groups = pairs([0, 1] + [2 + fc + kc for kc in range(8)])
                        bt = U_["bt"]
                        bias_fn = (lambda i, bt=bt, n=n: None if i < 1 else
                                   (bt, lambda a, i=i: bt.t[:, 2 * (i - 1) + a, :n]))
                    yb = next_y()
                    qt_ = U_["qt"]
                    core(n, qt_.t[:, :n], qt_,
                         lambda c, c2=c2: kt[c2].t[:, c * 128:(c + 1) * 128], kt[c2],
                         lambda c, h=h: vt.t[:, c, h * 65:h * 65 + 65], groups, NA_SCALE, yb, 65, bias_fn=bias_fn,
                         after=pending[0])

                    def fin(n=n, yb=yb, h=h, t0=t0):
                        ys = next_yst()
                        finalize(n, yb, ys, ys.t[0:64, :n])
                        store_y(h, t0, n, ys)
                    pending[0] = fin
                else:
                    g = h // 2
                    bias_fn = None
                    if ti == 0:
                        groups = pairs([0, 1])
                    else:
                        rel = [kc for kc in range(6) if 0 <= 4 * i_lat - 1 + kc < 64]
                        groups = [[0, 1]] + pairs([2 + 4 * i_lat - 1 + kc for kc in rel])
                        relp = pairs(rel)
                        bias_fn = (lambda i, relp=relp, n=n: None if i < 1 else
                                   (maskt, lambda a, i=i: maskt.t[:, relp[i - 1][a], :n]))
                    yb = next_y()
                    qt_ = U_["qt"]
                    core(n, qt_.t[:, :n], qt_,
                         lambda c, g=g: kt[g].t[:, c * 128:(c + 1) * 128], kt[g],
                         lambda c, g=g: vt.t[:, c, g * 65:g * 65 + 65], groups, GQA_SCALE, yb, 65, bias_fn=bias_fn,
                         after=pending[0])

                    def fin(n=n, yb=yb, h=h, t0=t0):
                        ys = next_yst()
                        finalize(n, yb, ys, ys.t[0:64, :n], den_add=esk.t[64:65, h:h + 1])
                        store_y(h, t0, n, ys)
                    pending[0] = fin

            prep(units[0])
            for ui, U_ in enumerate(units):
                if ui + 1 < len(units):
                    prep(units[ui + 1])
                run(U_)
            if pending[0] is not None:
                pending[0]()

    def phase_merge(self, l, with_ctx):
        nc, K = self.nc, self.K
        with ExitStack() as es:
            wg = K.sb(es, "wg", [128, 4, 8, D], BF16)
            wb = K.sb(es, "wb", [128, 4, 2, D], BF16)
            wo = K.sb(es, "wo", [128, 8, D], BF16)
            bg = K.sb(es, "bg", [128, 4, 8], F32)
            stg = [K.sb(es, f"mstg{i}", [128, 2048], F32) for i in range(2)]
            ub = [K.sb(es, f"u{i}", [128, 8, 512], BF16) for i in range(2)]
            ytb = [K.sb(es, f"yt{i}", [128, 4, 2, 512], BF16) for i in range(2)]
            mer = K.sb(es, "mer", [128, 8, 512], BF16)
            sgt = [K.sb(es, f"sgt{i}", [128, 512], F32) for i in range(2)]
            tmp = [K.sb(es, f"mtmp{i}", [128, 512], F32) for i in range(2)]
            macc = [K.sb(es, f"macc{i}", [128, 512], F32) for i in range(2)]
            xr = [K.sb(es, f"xr{i}", [128, 512], F32) for i in range(3)]
            K.dma(K.sp, bg.t[:], self.b_gate[l], writes=[bg])
            idx = 0
            for i in range(4):
                src = self.w_gate[l, i].rearrange("(k p) n -> p k n", p=128)
                for k in range(0, 8, 2):
                    K.load_cast(stg, wg, wg.t[:, i, k:k + 2, :], src[:, k:k + 2, :], idx)
                    idx += 1
                srcb = self.w_branch[l, i].rearrange("(k p) n -> p k n", p=128)
                K.load_cast(stg, wb, wb.t[:, i, :, :], srcb, idx)
                idx += 1
            srco = self.w_out[l].rearrange("(k p) n -> p k n", p=128)
            for k in range(0, 8, 2):
                K.load_cast(stg, wo, wo.t[:, k:k + 2, :], srco[:, k:k + 2, :], idx)
                idx += 1
            K.barrier()
            tis = [ti for ti in range(len(TILES)) if not (ti == 0 and not with_ctx)]

            def load_in(ii):
                t0, n = TILES[tis[ii]]
                u, yt = ub[ii % 2], ytb[ii % 2]
                K.dma(K.sp, u.t[:, :, :n], self.U[:, t0:t0 + n].rearrange("(k p) n -> p k n", p=128), writes=[u])
                for i in range(4):
                    K.dma(K.sp, yt.t[:, i, :, :n], self.Y[i, :, t0:t0 + n].rearrange("(k p) n -> p k n", p=128),
                          writes=[yt] if i == 0 else [], nosync=[yt] if i else [])

            load_in(0)
            ctr = 0
            xrc = 0
            for ii, ti in enumerate(tis):
                t0, n = TILES[ti]
                v = 1 if ti == 0 else 0
                u, yt = ub[ii % 2], ytb[ii % 2]
                if ii + 1 < len(tis):
                    load_in(ii + 1)
                for c in range(8):
                    cs = slice(c * 128, (c + 1) * 128)
                    ma = macc[c % 2]
                    for i in range(4):
                        pg = K.psum[(ctr % 2)]
                        pb = K.psum[2 + (ctr % 2)]
                        ctr += 1
                        for k in range(8):
                            K.mm(pg, pg.t[:, :n], wg.t[:, i, k, cs], u.t[:, k, :n], start=(k == 0), stop=(k == 7), reads=[wg, u])
                        for k in range(2):
                            K.mm(pb, pb.t[:, :n], wb.t[:, i, k, cs], yt.t[:, i, k, :n], start=(k == 0), stop=(k == 1), reads=[wb, yt])
                        sg_ = sgt[ctr % 2]
                        K.op(K.act, lambda pg=pg, sg_=sg_, i=i, c=c: nc.scalar.activation(
                            out=sg_.t[:, :n], in_=pg.t[:, :n], func=AF.Sigmoid, bias=bg.t[:, i, c:c + 1]),
                            reads=[pg, bg], writes=[sg_])
                        if i == 0:
                            K.op(K.dve, lambda pb=pb, sg_=sg_, ma=ma: nc.vector.tensor_tensor(
                                out=ma.t[:, :n], in0=sg_.t[:, :n], in1=pb.t[:, :n], op=ALU.mult),
                                reads=[sg_, pb], writes=[ma])
                        else:
                            tp = tmp[ctr % 2]
                            K.op(K.dve, lambda pb=pb, sg_=sg_, tp=tp: nc.vector.tensor_tensor(
                                out=tp.t[:, :n], in0=sg_.t[:, :n], in1=pb.t[:, :n], op=ALU.mult),
                                reads=[sg_, pb], writes=[tp])
                            if i < 3:
                                K.op(K.pool, lambda tp=tp, ma=ma: nc.gpsimd.tensor_tensor(
                                    out=ma.t[:, :n], in0=ma.t[:, :n], in1=tp.t[:, :n], op=ALU.add),
                                    reads=[tp, ma], writes=[ma])
                            else:
                                K.op(K.pool, lambda tp=tp, ma=ma, c=c: nc.gpsimd.tensor_tensor(
                                    out=mer.t[:, c, :n], in0=ma.t[:, :n], in1=tp.t[:, :n], op=ALU.add),
                                    reads=[tp, ma], writes=[mer] if c == 0 else [], nosync=[mer] if c else [])
                for c2 in range(8):
                    po = K.psum[4 + (c2 % 2)]
                    x_ = xr[xrc % 3]
                    xrc += 1
                    K.dma(K.sp, x_.t[:, :n], self.h[c2 * 128:(c2 + 1) * 128, t0:t0 + n], writes=[x_])
                    for c in range(8):
                        K.mm(po, po.t[:, :n], wo.t[:, c, c2 * 128:(c2 + 1) * 128], mer.t[:, c, :n], start=(c == 0), stop=(c == 7),
                             reads=[wo, mer])
                    K.op(K.dve, lambda c2=c2, po=po, x_=x_: nc.vector.scalar_tensor_tensor(
                        out=x_.t[:, :n], in0=po.t[:, :n], scalar=self.gt.t[:, l, v, 1, c2:c2 + 1],
                        in1=x_.t[:, :n], op0=ALU.mult, op1=ALU.add),
                        reads=[po, self.gt, x_], writes=[x_])
                    K.dma(K.sp, self.h[c2 * 128:(c2 + 1) * 128, t0:t0 + n], x_.t[:, :n], reads=[x_])

    def phase_final(self):
        nc, K = self.nc, self.K
        with ExitStack() as es:
            xin = [K.sb(es, f"fx{i}", [128, 8, 512], F32) for i in range(2)]
            fn = K.sb(es, "fn", [128, 8], F32)
            nb = self.norm_bufs(es, "n", 0)
            sq, xn, rt, rstd, ps_ss = nb
            K.dma(K.sp, fn.t[:], self.fnorm[:, :], writes=[fn])
            for ti, (t0, n) in enumerate(TILES):
                if ti == 0:
                    continue
                x = xin[ti % 2]
                K.dma(K.sp, x.t[:, :, :n], self.h[:, t0:t0 + n].rearrange("(k p) n -> p k n", p=128),
                      reads=[self.h_tk[ti]], writes=[x])
                for k in range(8):
                    s = sq[k % 2]
                    K.op(K.act, lambda k=k, s=s: nc.scalar.activation(out=s.t[:, :n], in_=x.t[:, k, :n], func=AF.Square),
                         reads=[x], writes=[s])
                    K.mm(ps_ss, ps_ss.t[:, :n], self.ones_bf.t[:, :], s.t[:, :n], start=(k == 0), stop=(k == 7),
                         reads=[s, self.ones_bf])
                K.op(K.act, lambda: nc.scalar.activation(out=rt.t[:, :n], in_=ps_ss.t[:, :n], func=AF.Sqrt,
                                                         scale=1.0 / D, bias=self.eps_t.t[:, 0:1]),
                     reads=[ps_ss, self.eps_t], writes=[rt])
                K.op(K.dve, lambda: nc.vector.reciprocal(out=rstd.t[:, :n], in_=rt.t[:, :n]), reads=[rt], writes=[rstd])
                for k in range(8):
                    K.op(K.dve, lambda k=k: nc.vector.scalar_tensor_tensor(
                        out=x.t[:, k, :n], in0=x.t[:, k, :n], scalar=fn.t[:, k:k + 1], in1=rstd.t[:, :n],
                        op0=ALU.mult, op1=ALU.mult), reads=[x, rstd, fn], writes=[x])
                K.dma(K.sp, self.out[:, t0 - L:t0 - L + n].rearrange("(k p) n -> p k n", p=128), x.t[:, :, :n],
                      reads=[x])


def _rope_tables():
    f = np.float32
    t = np.arange(S)
    rows = (t // GRID_W).astype(f)
    cols = (t % GRID_W).astype(f)

    def tabs(dim):
        half = dim // 2
        freqs = np.power(f(10000.0), -np.arange(0, half, 2, dtype=f) / f(half)).astype(f)
        ar = (rows[:, None] * freqs).astype(f)
        ac = (cols[:, None] * freqs).astype(f)
        q = dim // 4
        cos = np.concatenate([np.cos(ar), np.cos(ar), np.cos(ac), np.cos(ac)], axis=1)
        sin = np.concatenate([-np.sin(ar), np.sin(ar), -np.sin(ac), np.sin(ac)], axis=1)
        cos = np.concatenate([np.ones((L, dim), f), cos.astype(f)], axis=0)
        sin = np.concatenate([np.zeros((L, dim), f), sin.astype(f)], axis=0)
        rep = 128 // dim
        return np.tile(cos.T, (rep, 1)), np.tile(sin.T, (rep, 1))

    c32, s32 = tabs(32)
    c64, s64 = tabs(64)
    return np.ascontiguousarray(np.stack([c32, s32, c64, s64], axis=0).astype(f))


def _swap_idx(dim, nvec):
    q = dim // 4
    base = np.arange(dim)
    partner = np.where((base % (2 * q)) < q, base + q, base - q)
    return np.concatenate([v * dim + partner for v in range(nvec)])


def _na_bias(rpb):
    f = np.float32
    out = np.full((3, 4, 128, 8, 512), NEG, f)
    qr = np.arange(8)[:, None].repeat(64, 1).reshape(-1)
    qc = np.tile(np.arange(64), 8)
    p = np.arange(128)
    for typ, (r0, row0) in enumerate([(0, 0), (8, 4), (120, 112)]):
        r = r0 + qr
        rs = np.clip(r - 4, 0, 120)
        cs = np.clip(qc - 8, 0, 48)
        for kc in range(8):
            kr = row0 + 2 * kc + p // 64
            kcol = p % 64
            ok = ((kr[:, None] >= rs[None, :]) & (kr[:, None] < rs[None, :] + 8)
                  & (kcol[:, None] >= cs[None, :]) & (kcol[:, None] < cs[None, :] + 16))
            dr = np.clip(kr[:, None] - r[None, :] + 7, 0, 14)
            dc = np.clip(kcol[:, None] - qc[None, :] + 15, 0, 30)
            for h in range(4):
                vals = rpb[h][dr, dc]
                out[typ, h, :, kc, :] = np.where(ok, vals, f(NEG))
    return out


def _win_mask():
    f = np.float32
    p = np.arange(128)[:, None]
    q = np.arange(512)[None, :]
    m = np.zeros((128, 6, 512), f)
    for kc in range(6):
        j = 128 * kc - 128 + p
        m[:, kc, :] = np.where(np.abs(q - j) <= 128, f(0.0), f(NEG))
    return m


def prep_shared(inp):
    f = np.float32
    m = {}
    m["w_ada"] = inp["w_ada"]
    m["b_ada"] = np.ascontiguousarray(inp["b_ada"].reshape(DEPTH, 72, 128).transpose(0, 2, 1))
    nr = np.stack([inp["norm_ffn1"], inp["norm_mix"], inp["norm_ffn2"]], axis=1)
    m["norms"] = np.ascontiguousarray(nr.reshape(DEPTH, 3, 8, 128).transpose(0, 3, 1, 2))
    m["fnorm"] = np.ascontiguousarray(inp["final_norm"].reshape(8, 128).T)
    for k in ("ffn1_w_gu", "ffn2_w_gu", "ffn1_w_down", "ffn2_w_down", "w_gate", "w_branch", "w_out"):
        m[k] = inp[k]
    w = inp["w_in"]
    sw32 = lambda c0, nv: w[:, :, c0 + _swap_idx(32, nv)]
    sw64 = lambda c0, nv: w[:, :, c0 + _swap_idx(64, nv)]
    m["w_in_ext"] = np.ascontiguousarray(np.concatenate(
        [w, sw32(384, 1), sw32(416, 8), sw32(672, 8), sw64(1952, 4), sw64(2208, 2),
         w[:, :, 928:1184], w[:, :, 1696:1952], w[:, :, 2336:2464]], axis=2))
    wq = inp["mla_w_uq"].reshape(DEPTH, 256, 4, 96)
    wq_sw = np.concatenate([wq[..., :64], wq[..., 64 + _swap_idx(32, 1)]], axis=-1)
    m["w_uq_ext"] = np.ascontiguousarray(np.concatenate([wq, wq_sw], axis=-1).reshape(DEPTH, 256, 768))
    wkv = inp["mla_w_ukv"].reshape(DEPTH, 128, 4, 128)
    m["w_ukv_ext"] = np.ascontiguousarray(np.concatenate(
        [wkv[..., :64].reshape(DEPTH, 128, 256), wkv[..., 64:].reshape(DEPTH, 128, 256)], axis=-1))
    qn = inp["mla_q_norm"].reshape(DEPTH, 2, 128).transpose(0, 2, 1)
    m["qkn"] = np.ascontiguousarray(np.concatenate([qn, inp["mla_kv_norm"][:, :, None]], axis=-1))
    m["rope_tab"] = _rope_tables()
    m["diff_lam"] = np.ascontiguousarray(inp["diff_lam"].reshape(DEPTH, 1, 128))
    m["diff_subln"] = np.ascontiguousarray(inp["diff_subln"].reshape(DEPTH, 64, 1))
    m["gqa_sink"] = np.ascontiguousarray(inp["gqa_sink"].reshape(DEPTH, 1, 4))
    m["na_bias"] = np.stack([_na_bias(inp["na_rpb"][l]) for l in range(DEPTH)], axis=0)
    m["win_mask"] = _win_mask()
    m["ident"] = np.eye(128, dtype=f)
    m["b_gate"] = np.ascontiguousarray(inp["b_gate"].reshape(DEPTH, 4, 8, 128).transpose(0, 3, 1, 2))
    return {k: np.ascontiguousarray(v.astype(f)) for k, v in m.items()}


def prep_core(inp, b):
    f = np.float32
    m = {}
    m["hT"] = np.ascontiguousarray(np.concatenate([inp["ctx"][b], inp["x"][b]], axis=0).T.astype(f))
    cc = np.stack([inp["c"][b], inp["c_ctx"]], axis=-1)
    m["c_in"] = np.ascontiguousarray(cc.reshape(8, 128, 2).transpose(1, 0, 2).astype(f))
    return m


def prep_inputs(inp, b):
    m = prep_shared(inp)
    m.update(prep_core(inp, b))
    return m


_CACHE = {}


def kernel(**inputs):
    inp = {k: np.asarray(v) for k, v in inputs.items()}
    if "nc" not in _CACHE:
        p = Prog()
        _CACHE["nc"] = p.build()
        _CACHE["names"] = list(p.inputs.keys())
    nc = _CACHE["nc"]
    in_maps = []
    shared = prep_shared(inp)
    for b in range(8):
        m = dict(shared)
        m.update(prep_core(inp, b))
        in_maps.append({k: m[k] for k in _CACHE["names"]})
    res = run_bass_kernel_spmd(nc, in_maps, core_ids=list(range(8)))
    out = np.stack([np.ascontiguousarray(res.results[b]["outT"].T) for b in range(8)], axis=0)
    return out.astype(np.float32)
```

```python
import math
import os
from contextlib import ExitStack

import numpy as np
import concourse.bass as bass
import concourse.mybir as mybir
from concourse.bass_utils import run_bass_kernel_spmd

F32 = mybir.dt.float32
BF16 = mybir.dt.bfloat16
AF = mybir.ActivationFunctionType
ALU = mybir.AluOpType

D = 1024
S = 8192
L = 256
NT = S + L
DEPTH = 2
DFF = 2816
NJ = DFF // 128
GRID_W = 64
EPS = 1e-6
SUBLN_EPS = 1e-5
MLA_SCALE = 96 ** -0.5
DIFF_SCALE = 32 ** -0.5
NA_SCALE = 64 ** -0.5
GQA_SCALE = 64 ** -0.5
NEG = -30000.0
NDS = 24
DBG_STEP = int(os.environ.get('DBG_STEP', '99'))
DBG_TILES = int(os.environ.get('DBG_TILES', '99'))
NWIN = 2464 + 928 + 640

TILES = [(0, L)] + [(L + 512 * i, 512) for i in range(S // 512)]


class Tk:
    __slots__ = ("w", "r", "t")

    def __init__(self, t=None):
        self.w = {}
        self.r = {}
        self.t = t


class Eng:
    def __init__(self, name, eng, sem, self_sync=True):
        self.name = name
        self.eng = eng
        self.sem = sem
        self.cnt = 0
        self.seen = {}
        self.self_sync = self_sync
        self.dsems = None


class KB:
    def __init__(self, nc, es):
        self.nc = nc
        self.es = es

        def mk(name, eng, self_sync=True):
            return Eng(name, eng, es.enter_context(nc.semaphore("s_" + name)), self_sync)

        self.pe = mk("pe", nc.tensor, False)
        self.act = mk("act", nc.scalar)
        self.dve = mk("dve", nc.vector)
        self.pool = mk("pool", nc.gpsimd)
        self.sp = mk("sp", nc.sync)
        self.engs = [self.pe, self.act, self.dve, self.pool, self.sp]
        self.queues = [self.sp, self.pool]
        for q in self.queues:
            q.dsems = [es.enter_context(nc.semaphore(f"d_{q.name}_{i}")) for i in range(NDS)]
            q.dvals = [0] * NDS
            q.dnext = 0
        self.pfull = es.enter_context(nc.psum_tensor("psfull", [128, 4096], F32))
        self.psum = [Tk(self.pfull[:, 512 * i:512 * (i + 1)]) for i in range(8)]

    def _deps(self, E, reads, writes):
        deps = {}

        def add(d):
            for key, (val, semh) in d.items():
                if key == E.name and not E.self_sync:
                    continue
                if deps.get(key, (0, None))[0] < val:
                    deps[key] = (val, semh)

        for t in reads:
            add(t.w)
        for t in writes:
            add(t.w)
            add(t.r)
        for key, (val, semh) in deps.items():
            if E.seen.get(key, 0) < val:
                E.eng.wait_ge(semh, val)
                E.seen[key] = val

    def op(self, E, fn, reads=(), writes=(), nosync=()):
        self._deps(E, reads, writes)
        ins = fn()
        E.cnt += 1
        ins.then_inc(E.sem, 1)
        tok = (E.cnt, E.sem)
        for t in reads:
            t.r[E.name] = tok
        for t in writes:
            t.w[E.name] = tok
        for t in nosync:
            t.w[E.name] = tok

    def dma(self, Q, out, in_, reads=(), writes=(), nosync=()):
        self._deps(Q, reads, writes)
        j = Q.dnext % NDS
        Q.dnext += 1
        key = ("d", Q.name, j)
        if Q.seen.get(key, 0) < Q.dvals[j]:
            Q.eng.wait_ge(Q.dsems[j], Q.dvals[j])
            Q.seen[key] = Q.dvals[j]
        Q.dvals[j] += 16
        Q.eng.dma_start(out=out, in_=in_).then_inc(Q.dsems[j], 16)
        tok = (Q.dvals[j], Q.dsems[j])
        for t in reads:
            t.r[key] = tok
        for t in writes:
            t.w[key] = tok
        for t in nosync:
            t.w[key] = tok

    def barrier(self):
        for E in self.engs:
            for F in self.engs:
                if F is E and not E.self_sync:
                    continue
                if E.seen.get(F.name, 0) < F.cnt:
                    E.eng.wait_ge(F.sem, F.cnt)
                    E.seen[F.name] = F.cnt
            for Q in self.queues:
                for j in range(NDS):
                    key = ("d", Q.name, j)
                    if E.seen.get(key, 0) < Q.dvals[j]:
                        E.eng.wait_ge(Q.dsems[j], Q.dvals[j])
                        E.seen[key] = Q.dvals[j]

    def sb(self, es, name, shape, dt):
        self.uid = getattr(self, "uid", 0) + 1
        return Tk(es.enter_context(self.nc.sbuf_tensor(f"{name}_u{self.uid}", shape, dt)))

    def mm(self, out_tk, out_ap, lhsT, rhs, start, stop, reads=(), **kw):
        nc = self.nc
        self.op(self.pe, lambda: nc.tensor.matmul(out_ap, lhsT, rhs, start=start, stop=stop, **kw),
                reads=reads, writes=[out_tk])

    def load_cast(self, stg, dst_tk, dst_ap, src_ap, idx):
        nc = self.nc
        s = stg[idx % len(stg)]
        shp = list(src_ap.shape)
        n = int(np.prod(shp[1:]))
        if len(shp) == 3:
            sview = s.t[:, 0:n].rearrange("p (a b) -> p a b", b=shp[2])
        else:
            sview = s.t[:, 0:n]
        self.dma(self.sp, sview, src_ap, writes=[s])
        E = self.pool if idx % 2 == 0 else self.dve
        if E is self.pool:
            self.op(E, lambda: nc.gpsimd.tensor_copy(out=dst_ap, in_=sview), reads=[s], nosync=[dst_tk])
        else:
            self.op(E, lambda: nc.vector.tensor_copy(out=dst_ap, in_=sview), reads=[s], nosync=[dst_tk])


class Prog:
    def __init__(self, debug=(), stop_after=None):
        self.debug = set(debug)
        self.stop_after = stop_after
        self.nc = bass.Bass("TRN2", target_bir_lowering=False)
        self.inputs = {}

    def din(self, name, shape, dt=F32):
        a = self.nc.dram_tensor(name, list(shape), dt, kind="ExternalInput").ap()
        self.inputs[name] = a
        return a

    def dscr(self, name, shape, dt):
        kind = "ExternalOutput" if name in self.debug else "Internal"
        return self.nc.dram_tensor(name, list(shape), dt, kind=kind).ap()

    def build(self):
        nc = self.nc
        self.hT = self.din("hT", [D, NT])
        self.c_in = self.din("c_in", [128, 8, 2])
        self.w_ada = self.din("w_ada", [DEPTH, D, 9 * D])
        self.b_ada = self.din("b_ada", [DEPTH, 128, 72])
        self.norms = self.din("norms", [DEPTH, 128, 3, 8])
        self.fnorm = self.din("fnorm", [128, 8])
        self.w_gu = [self.din("ffn1_w_gu", [DEPTH, D, 2 * DFF]), self.din("ffn2_w_gu", [DEPTH, D, 2 * DFF])]
        self.w_dn = [self.din("ffn1_w_down", [DEPTH, DFF, D]), self.din("ffn2_w_down", [DEPTH, DFF, D])]
        self.out = self.nc.dram_tensor("outT", [D, S], F32, kind="ExternalOutput").ap()
        self.h = self.dscr("h_scr", [D, NT], F32)
        self.h_tk = [Tk() for _ in TILES]
        self.w_in = self.din("w_in_ext", [DEPTH, D, NWIN])
        self.w_uq = self.din("w_uq_ext", [DEPTH, 256, 768])
        self.w_ukv = self.din("w_ukv_ext", [DEPTH, 128, 512])
        self.qkn = self.din("qkn", [DEPTH, 128, 3])
        self.rope = self.din("rope_tab", [4, 128, NT])
        self.lam_in = self.din("diff_lam", [DEPTH, 1, 128])
        self.subln = self.din("diff_subln", [DEPTH, 64, 1])
        self.sink_in = self.din("gqa_sink", [DEPTH, 1, 4])
        self.na_bias = self.din("na_bias", [DEPTH, 3, 4, 128, 8, 512])
        self.win_mask = self.din("win_mask", [128, 6, 512])
        self.ident_in = self.din("ident", [128, 128])
        self.w_gate = self.din("w_gate", [DEPTH, 4, D, D])
        self.w_branch = self.din("w_branch", [DEPTH, 4, 256, D])
        self.w_out = self.din("w_out", [DEPTH, D, D])
        self.b_gate = self.din("b_gate", [DEPTH, 128, 4, 8])
        self.U = self.dscr("U_scr", [D, NT], BF16)
        self.QA = self.dscr("QA", [4, 96, NT], BF16)
        self.KA = self.dscr("KA", [4, 96, NT], BF16)
        self.VA = self.dscr("VA", [NT, 260], BF16)
        self.QB = self.dscr("QB", [2, 128, NT], BF16)
        self.KB_ = self.dscr("KB", [2, 128, NT], BF16)
        self.VB = self.dscr("VB", [NT, 260], BF16)
        self.QC = self.dscr("QC", [2, 128, NT], BF16)
        self.KC = self.dscr("KC", [2, 128, NT], BF16)
        self.VC = self.dscr("VC", [NT, 260], BF16)
        self.QD = self.dscr("QD", [2, 128, NT], BF16)
        self.KD = self.dscr("KD", [2, 128, NT], BF16)
        self.VD = self.dscr("VD", [NT, 130], BF16)
        self.Y = self.dscr("Y_scr", [4, 256, NT], BF16)

        with ExitStack() as es:
            K = self.K = KB(nc, es)
            self.ones_bf = K.sb(es, "ones_bf", [128, 128], BF16)
            self.ones_f = K.sb(es, "ones_f", [128, 128], F32)
            self.gs = K.sb(es, "gs", [128, DEPTH, 2, 3, 8], F32)
            self.sh = K.sb(es, "sh", [128, DEPTH, 2, 3, 8], F32)
            self.gt = K.sb(es, "gt", [128, DEPTH, 2, 3, 8], F32)
            self.eps_t = K.sb(es, "eps_t", [128, 4], F32)
            self.ident = K.sb(es, "ident_bf", [128, 128], BF16)
            K.dma(K.sp, self.ones_f.t[:, :], self.ident_in[:, :], writes=[self.ones_f])
            K.op(K.dve, lambda: nc.vector.tensor_copy(out=self.ident.t[:], in_=self.ones_f.t[:]), reads=[self.ones_f],
                 writes=[self.ident])
            K.op(K.dve, lambda: nc.vector.memset(self.ones_bf.t[:], 1.0), writes=[self.ones_bf])
            K.op(K.dve, lambda: nc.vector.memset(self.ones_f.t[:], 0.0), writes=[self.ones_f])
            K.op(K.dve, lambda: nc.vector.memset(self.ones_f.t[64:65, :], 1.0), writes=[self.ones_f])
            K.op(K.dve, lambda: nc.vector.memset(self.eps_t.t[:, 0:1], EPS), writes=[self.eps_t])
            K.op(K.dve, lambda: nc.vector.memset(self.eps_t.t[:, 1:2], SUBLN_EPS), writes=[self.eps_t])
            for l_ in range(DEPTH):
                li_ = 1.0 - (0.8 - 0.6 * math.exp(-0.3 * l_))
                K.op(K.dve, lambda l_=l_, li_=li_: nc.vector.memset(self.eps_t.t[:, 2 + l_:3 + l_], SUBLN_EPS / (li_ * li_)),
                     writes=[self.eps_t])
            self.phase_mod()
            K.barrier()
            if self.stop_after == "mod":
                return self.finish()
            first = True
            for l in range(DEPTH):
                last = l == DEPTH - 1
                self.phase_ffn(l, 0, src_is_input=first, skip_ctx=False)
                first = False
                K.barrier()
                if self.stop_after == f"ffn1_{l}":
                    return self.finish()
                self.phase_proj(l)
                K.barrier()
                if self.stop_after == f"proj_{l}":
                    return self.finish()
                for mx in "ABCD":
                    self.phase_attn(l, mx, with_ctx=not last)
                    K.barrier()
                    if self.stop_after == f"attn{mx}_{l}":
                        return self.finish()
                self.phase_merge(l, with_ctx=not last)
                K.barrier()
                if self.stop_after == f"merge_{l}":
                    return self.finish()
                self.phase_ffn(l, 1, src_is_input=False, skip_ctx=last)
                K.barrier()
            self.phase_final()
            K.barrier()
        return self.finish()

    def finish(self):
        self.K.barrier()
        return self.nc

    def phase_mod(self):
        nc, K = self.nc, self.K
        with ExitStack() as es:
            s_in = K.sb(es, "s_in", [128, 8, 2], F32)
            wst = [K.sb(es, f"wada{i}", [128, 8, 1024], F32) for i in range(2)]
            bsb = K.sb(es, "bsb", [128, DEPTH, 72], F32)
            nsb = K.sb(es, "nsb", [128, DEPTH, 3, 8], F32)
            modsb = K.sb(es, "modsb", [128, DEPTH, 2, 72], F32)
            K.dma(K.sp, s_in.t[:], self.c_in[:, :, :], writes=[s_in])
            K.dma(K.sp, bsb.t[:], self.b_ada.rearrange("l p j -> p l j"), writes=[bsb])
            K.dma(K.sp, nsb.t[:], self.norms.rearrange("l p s k -> p l s k"), writes=[nsb])
            K.op(K.act, lambda: nc.scalar.activation(out=s_in.t[:], in_=s_in.t[:], func=AF.Silu), reads=[s_in], writes=[s_in])
            ps = K.psum[0]
            it = 0
            for l in range(DEPTH):
                for g in range(9):
                    w = wst[it % 2]
                    it += 1
                    src = self.w_ada[l, :, g * 1024:(g + 1) * 1024].rearrange("(k p) n -> p k n", p=128)
                    for half in range(2):
                        K.dma(K.sp, w.t[:, half * 4:(half + 1) * 4, :], src[:, half * 4:(half + 1) * 4, :],
                              writes=[w] if half == 0 else [], nosync=[w] if half == 1 else [])
                    for n in range(8):
                        col = (l * 72 + g * 8 + n) * 2
                        for k in range(8):
                            K.mm(ps, ps.t[:, col:col + 2], w.t[:, k, n * 128:(n + 1) * 128], s_in.t[:, k, :],
                                 start=(k == 0), stop=(k == 7), reads=[w, s_in])
            psv = ps.t[:, 0:DEPTH * 72 * 2].rearrange("p (l j v) -> p l j v", l=DEPTH, v=2)
            for l in range(DEPTH):
                for v in range(2):
                    K.op(K.dve, lambda l=l, v=v: nc.vector.tensor_tensor(out=modsb.t[:, l, v, :], in0=psv[:, l, :, v],
                                                                        in1=bsb.t[:, l, :], op=ALU.add),
                         reads=[ps, bsb], writes=[modsb])
            for l in range(DEPTH):
                for v in range(2):
                    for s in range(3):
                        o = 3 * s * 8
                        K.op(K.dve, lambda l=l, v=v, s=s, o=o: nc.vector.scalar_tensor_tensor(
                            out=self.gs.t[:, l, v, s, :], in0=modsb.t[:, l, v, o + 8:o + 16], scalar=1.0,
                            in1=nsb.t[:, l, s, :], op0=ALU.add, op1=ALU.mult),
                            reads=[modsb, nsb], writes=[self.gs])
                        K.op(K.dve, lambda l=l, v=v, s=s, o=o: nc.vector.tensor_copy(
                            out=self.sh.t[:, l, v, s, :], in_=modsb.t[:, l, v, o:o + 8]),
                            reads=[modsb], writes=[self.sh])
                        K.op(K.dve, lambda l=l, v=v, s=s, o=o: nc.vector.tensor_scalar(
                            out=self.gt.t[:, l, v, s, :], in0=modsb.t[:, l, v, o + 16:o + 24],
                            scalar1=(1.0 if s == 1 else 0.5), scalar2=None, op0=ALU.mult),
                            reads=[modsb], writes=[self.gt])
            if "mod_dbg" in self.debug:
                dbg = self.nc.dram_tensor("mod_dbg", [128, DEPTH * 2 * 72], F32, kind="ExternalOutput").ap()
                K.dma(K.sp, dbg[:, :], modsb.t[:].rearrange("p l v j -> p (l v j)"), reads=[modsb])

    def norm_mod(self, es_bufs, xin, n, gs_ap, sh_ap, u, nk=8):
        nc, K = self.nc, self.K
        sq, xn, rt, rstd, ps_ss = es_bufs
        for k in range(nk):
            s = sq[k % 2]
            K.op(K.act, lambda k=k, s=s: nc.scalar.activation(out=s.t[:, :n], in_=xin.t[:, k, :n], func=AF.Square),
                 reads=[xin], writes=[s])
            K.mm(ps_ss, ps_ss.t[:, :n], self.ones_bf.t[:, :], s.t[:, :n], start=(k == 0), stop=(k == nk - 1),
                 reads=[s, self.ones_bf])
        K.op(K.act, lambda: nc.scalar.activation(out=rt.t[:, :n], in_=ps_ss.t[:, :n], func=AF.Sqrt,
                                                 scale=1.0 / (128 * nk), bias=self.eps_t.t[:, 0:1]),
             reads=[ps_ss, self.eps_t], writes=[rt])
        K.op(K.dve, lambda: nc.vector.reciprocal(out=rstd.t[:, :n], in_=rt.t[:, :n]), reads=[rt], writes=[rstd])
        for k in range(nk):
            x2 = xn[k % 2]
            K.op(K.dve, lambda k=k, x2=x2: nc.vector.tensor_tensor(out=x2.t[:, :n], in0=xin.t[:, k, :n], in1=rstd.t[:, :n],
                                                                    op=ALU.mult),
                 reads=[xin, rstd], writes=[x2])
            if sh_ap is not None:
                K.op(K.act, lambda k=k, x2=x2: nc.scalar.activation(out=u.t[:, k, :n], in_=x2.t[:, :n], func=AF.Identity,
                                                                     scale=gs_ap(k), bias=sh_ap(k)),
                     reads=[x2, self.gs, self.sh], nosync=[u] if k else [], writes=[] if k else [u])
            else:
                K.op(K.act, lambda k=k, x2=x2: nc.scalar.activation(out=u.t[:, k, :n], in_=x2.t[:, :n], func=AF.Identity,
                                                                     scale=gs_ap(k)),
                     reads=[x2], nosync=[u] if k else [], writes=[] if k else [u])

    def norm_bufs(self, es, pfx, ps_idx):
        K = self.K
        sq = [K.sb(es, f"{pfx}sq{i}", [128, 512], BF16) for i in range(2)]
        xn = [K.sb(es, f"{pfx}xn{i}", [128, 512], F32) for i in range(2)]
        rt = K.sb(es, f"{pfx}rt", [128, 512], F32)
        rstd = K.sb(es, f"{pfx}rstd", [128, 512], F32)
        return (sq, xn, rt, rstd, K.psum[ps_idx])

    def phase_ffn(self, l, which, src_is_input, skip_ctx):
        nc, K = self.nc, self.K
        s_idx = 0 if which == 0 else 2
        with ExitStack() as es:
            wgu = K.sb(es, "wgu", [128, 8, 2 * DFF], BF16)
            wdn = K.sb(es, "wdn", [128, NJ, D], BF16)
            xin = K.sb(es, "xin", [128, 8, 512], F32)
            stg = [Tk(xin.t[:, 4 * i:4 * i + 4, :].rearrange("p a b -> p (a b)")) for i in range(2)]
            u = K.sb(es, "u", [128, 8, 512], BF16)
            hmid = K.sb(es, "hmid", [128, NJ, 512], BF16)
            sg = [K.sb(es, f"sg{i}", [128, 512], F32) for i in range(2)]
            xr = [K.sb(es, f"xr{i}", [128, 512], F32) for i in range(3)]
            nb = self.norm_bufs(es, "f", 0)
            idx = 0
            src_gu = self.w_gu[which][l].rearrange("(k p) n -> p k n", p=128)
            for k in range(8):
                for c0 in range(0, 2 * DFF, 2048):
                    c1 = min(c0 + 2048, 2 * DFF)
                    K.load_cast(stg, wgu, wgu.t[:, k, c0:c1], src_gu[:, k, c0:c1], idx)
                    idx += 1
            src_dn = self.w_dn[which][l].rearrange("(j p) n -> p j n", p=128)
            for j in range(0, NJ, 2):
                K.load_cast(stg, wdn, wdn.t[:, j:j + 2, :], src_dn[:, j:j + 2, :], idx)
                idx += 1
            hsrc = self.hT if src_is_input else self.h
            K.barrier()
            tis = [ti for ti in range(len(TILES)) if not (ti == 0 and skip_ctx)]

            def load_x(ti):
                t0, n = TILES[ti]
                K.dma(K.sp, xin.t[:, :, :n], hsrc[:, t0:t0 + n].rearrange("(k p) n -> p k n", p=128), writes=[xin])

            def norm(ti):
                t0, n = TILES[ti]
                v = 1 if ti == 0 else 0
                self.norm_mod(nb, xin, n,
                              lambda k: self.gs.t[:, l, v, s_idx, k:k + 1],
                              lambda k: self.sh.t[:, l, v, s_idx, k:k + 1], u)

            load_x(tis[0])
            norm(tis[0])
            xrc = 0
            for ii, ti in enumerate(tis):
                t0, n = TILES[ti]
                v = 1 if ti == 0 else 0
                nxt = tis[ii + 1] if ii + 1 < len(tis) else None
                if nxt is not None:
                    load_x(nxt)
                for j in range(NJ):
                    pg = K.psum[1 + (j % 2)]
                    pu = K.psum[3 + (j % 2)]
                    for k in range(8):
                        K.mm(pg, pg.t[:, :n], wgu.t[:, k, j * 128:(j + 1) * 128], u.t[:, k, :n],
                             start=(k == 0), stop=(k == 7), reads=[wgu, u])
                    for k in range(8):
                        K.mm(pu, pu.t[:, :n], wgu.t[:, k, DFF + j * 128:DFF + (j + 1) * 128], u.t[:, k, :n],
                             start=(k == 0), stop=(k == 7), reads=[wgu, u])
                    s = sg[j % 2]
                    K.op(K.act, lambda s=s, pg=pg: nc.scalar.activation(out=s.t[:, :n], in_=pg.t[:, :n], func=AF.Silu),
                         reads=[pg], writes=[s])
                    K.op(K.dve, lambda s=s, pu=pu, j=j: nc.vector.tensor_tensor(out=hmid.t[:, j, :n], in0=s.t[:, :n],
                                                                               in1=pu.t[:, :n], op=ALU.mult),
                         reads=[s, pu], writes=[hmid] if j == 0 else [], nosync=[hmid] if j else [])
                if nxt is not None:
                    norm(nxt)
                for c in range(8):
                    po = K.psum[5 + (c % 2)]
                    x_ = xr[xrc % 3]
                    xrc += 1
                    hrow = hsrc[c * 128:(c + 1) * 128, t0:t0 + n]
                    K.dma(K.sp, x_.t[:, :n], hrow, writes=[x_])
                    for j in range(NJ):
                        K.mm(po, po.t[:, :n], wdn.t[:, j, c * 128:(c + 1) * 128], hmid.t[:, j, :n],
                             start=(j == 0), stop=(j == NJ - 1), reads=[wdn, hmid])
                    K.op(K.dve, lambda c=c, po=po, x_=x_: nc.vector.scalar_tensor_tensor(
                        out=x_.t[:, :n], in0=po.t[:, :n], scalar=self.gt.t[:, l, v, s_idx, c:c + 1],
                        in1=x_.t[:, :n], op0=ALU.mult, op1=ALU.add),
                        reads=[po, self.gt, x_], writes=[x_])
                    K.dma(K.sp, self.h[c * 128:(c + 1) * 128, t0:t0 + n], x_.t[:, :n], reads=[x_])

    def phase_proj(self, l):
        nc, K = self.nc, self.K
        with ExitStack() as es:
            win = K.sb(es, "win", [128, 8, NWIN], BF16)
            wuq = K.sb(es, "wuq", [128, 2, 768], BF16)
            wukv = K.sb(es, "wukv", [128, 512], BF16)
            qkn = K.sb(es, "qkn", [128, 3], F32)
            xinb = [K.sb(es, f"xin{i}", [128, 8, 512], F32) for i in range(2)]
            stg = [Tk(xinb[0].t[:, 4 * i:4 * i + 4, :].rearrange("p a b -> p (a b)")) for i in range(2)]
            ubuf = [K.sb(es, f"u{i}", [128, 8, 512], BF16) for i in range(2)]
            rtab = [K.sb(es, f"rtab{i}", [128, 4, 512], F32) for i in range(2)]
            cq_sb = K.sb(es, "cq_sb", [128, 2, 512], F32)
            ckv_sb = K.sb(es, "ckv_sb", [128, 1, 512], F32)
            cqn = K.sb(es, "cqn", [128, 2, 512], BF16)
            ckvn = K.sb(es, "ckvn", [128, 1, 512], BF16)
            stages = [K.sb(es, f"pst{i}", [128, 512], BF16) for i in range(6)]
            vst = [K.sb(es, f"vst{i}", [128, 260], BF16) for i in range(4)]
            ra = [K.sb(es, f"ra{i}", [128, 512], F32) for i in range(2)]
            rb = [K.sb(es, f"rb{i}", [128, 512], F32) for i in range(2)]
            nb = self.norm_bufs(es, "p", 0)
            for v_ in vst:
                K.op(K.dve, lambda v_=v_: nc.vector.memset(v_.t[:], 1.0), writes=[v_])
            K.dma(K.sp, qkn.t[:], self.qkn[l], writes=[qkn])
            idx = 0
            src = self.w_in[l].rearrange("(k p) n -> p k n", p=128)
            for k in range(8):
                for c0 in range(0, NWIN, 2048):
                    c1 = min(c0 + 2048, NWIN)
                    K.load_cast(stg, win, win.t[:, k, c0:c1], src[:, k, c0:c1], idx)
                    idx += 1
            K.load_cast(stg, wuq, wuq.t[:, :, :], self.w_uq[l].rearrange("(k p) n -> p k n", p=128), idx)
            idx += 1
            K.load_cast(stg, wukv, wukv.t[:, :], self.w_ukv[l], idx)
            idx += 1
            K.barrier()
            st = {"bank": 0, "stage": 0, "vst": 0, "ev": 0, "rr": 0}

            def bank():
                b = K.psum[1 + st["bank"] % 7]
                st["bank"] += 1
                return b

            def stage():
                t = stages[st["stage"] % len(stages)]
                st["stage"] += 1
                return t

            def vstage():
                t = vst[st["vst"] % len(vst)]
                st["vst"] += 1
                return t

            def evac(dst_tk, dst_ap, ps_tk, src_ap, nosync=False, eng=None):
                st["ev"] += 1
                w = [] if nosync else [dst_tk]
                ns = [dst_tk] if nosync else []
                use_act = (st["ev"] % 2 == 0) if eng is None else (eng == "act")
                if use_act:
                    K.op(K.act, lambda: nc.scalar.activation(out=dst_ap, in_=src_ap, func=AF.Copy), reads=[ps_tk], writes=w, nosync=ns)
                else:
                    K.op(K.dve, lambda: nc.vector.tensor_copy(out=dst_ap, in_=src_ap), reads=[ps_tk], writes=w, nosync=ns)

            def p_load(ti):
                t0, n = TILES[ti]
                K.dma(K.sp, xinb[ti % 2].t[:, :, :n], self.h[:, t0:t0 + n].rearrange("(k p) n -> p k n", p=128),
                      writes=[xinb[ti % 2]])
                K.dma(K.sp, rtab[ti % 2].t[:, :, :n], self.rope[:, :, t0:t0 + n].rearrange("a p n -> p a n"),
                      writes=[rtab[ti % 2]])

            def p_norm(ti):
                t0, n = TILES[ti]
                v = 1 if ti == 0 else 0
                self.norm_mod(nb, xinb[ti % 2], n, lambda k: self.gs.t[:, l, v, 1, k:k + 1],
                              lambda k: self.sh.t[:, l, v, 1, k:k + 1], ubuf[ti % 2])
                K.dma(K.sp, self.U[:, t0:t0 + n].rearrange("(k p) n -> p k n", p=128), ubuf[ti % 2].t[:, :, :n],
                      reads=[ubuf[ti % 2]])

            NTI = min(len(TILES), DBG_TILES)
            p_load(0)
            if NTI > 1:
                p_load(1)
            p_norm(0)
            for ti, (t0, n) in enumerate(TILES):
                if ti >= NTI:
                    break
                u = ubuf[ti % 2]
                rt_ = rtab[ti % 2]

                def proj(col0, M):
                    b = bank()
                    for k in range(8):
                        K.mm(b, b.t[0:M, :n], win.t[:, k, col0:col0 + M], u.t[:, k, :n], start=(k == 0), stop=(k == 7),
                             reads=[win, u])
                    return b

                def rope_out(bn, bs, p0, p1, tab, dst_tk, dst_ap_fn):
                    i = st["rr"] % 2
                    st["rr"] += 1
                    a_, b_ = ra[i], rb[i]
                    K.op(K.dve, lambda: nc.vector.tensor_tensor(out=a_.t[p0:p1, :n], in0=bn.t[p0:p1, :n],
                                                                in1=rt_.t[p0:p1, tab, :n], op=ALU.mult),
                         reads=[bn, rt_], writes=[a_])
                    K.op(K.dve, lambda: nc.vector.tensor_tensor(out=b_.t[p0:p1, :n], in0=bs.t[p0:p1, :n],
                                                                in1=rt_.t[p0:p1, tab + 1, :n], op=ALU.mult),
                         reads=[bs, rt_], writes=[b_])
                    K.op(K.pool, lambda: nc.gpsimd.tensor_tensor(out=dst_ap_fn(p0, p1), in0=a_.t[p0:p1, :n],
                                                                 in1=b_.t[p0:p1, :n], op=ALU.add),
                         reads=[a_, b_], writes=[dst_tk])

                if DBG_STEP < 2:
                    continue
                for k in range(2):
                    b = proj(k * 128, 128)
                    evac(cq_sb, cq_sb.t[:, k, :n], b, b.t[:, :n], nosync=(k == 1))
                self.norm_mod(nb, cq_sb, n, lambda k: qkn.t[:, k:k + 1], None, cqn, nk=2)
                b = proj(256, 128)
                evac(ckv_sb, ckv_sb.t[:, 0, :n], b, b.t[:, :n])
                self.norm_mod(nb, ckv_sb, n, lambda k: qkn.t[:, 2:3], None, ckvn, nk=1)
                if ti + 1 < NTI:
                    p_norm(ti + 1)
                if DBG_STEP < 3:
                    continue
                bn, bs = proj(384, 128), proj(2464, 128)
                sg_ = stage()
                rope_out(bn, bs, 0, 32, 0, sg_, lambda p0, p1: sg_.t[p0:p1, :n])
                for h in range(4):
                    K.dma(K.sp, self.KA[h, 64:96, t0:t0 + n], sg_.t[0:32, :n], reads=[sg_])
                if DBG_STEP < 4:
                    continue
                for (c_n, c_s, tab, dst) in [
                    (416, 2496, 0, self.QB[0]), (544, 2624, 0, self.QB[1]),
                    (672, 2752, 0, self.KB_[0]), (800, 2880, 0, self.KB_[1]),
                    (1952, 3008, 2, self.QD[0]), (2080, 3136, 2, self.QD[1]),
                ]:
                    bn, bs = proj(c_n, 128), proj(c_s, 128)
                    sg_ = stage()
                    rope_out(bn, bs, 0, 128, tab, sg_, lambda p0, p1, sg_=sg_: sg_.t[p0:p1, :n])
                    K.dma(K.sp, dst[:, t0:t0 + n], sg_.t[:, :n], reads=[sg_])
                if DBG_STEP < 5:
                    continue
                bn, bs = proj(2208, 128), proj(3264, 128)
                sg_ = stage()
                rope_out(bn, bs, 0, 128, 2, sg_, lambda p0, p1, sg_=sg_: sg_.t[p0:p1, :n])
                for g in range(2):
                    for dup in range(2):
                        K.dma(K.sp, self.KD[g, 64 * dup:64 * dup + 64, t0:t0 + n], sg_.t[64 * g:64 * g + 64, :n], reads=[sg_])
                if DBG_STEP < 6:
                    continue
                for (c_n, dst) in [(1184, self.QC[0]), (1312, self.QC[1]), (1440, self.KC[0]), (1568, self.KC[1])]:
                    b = proj(c_n, 128)
                    sg_ = stage()
                    evac(sg_, sg_.t[:, :n], b, b.t[:, :n])
                    K.dma(K.sp, dst[:, t0:t0 + n], sg_.t[:, :n], reads=[sg_])
                if DBG_STEP < 7:
                    continue
                for h in range(4):
                    b1, b2 = bank(), bank()
                    for k in range(2):
                        K.mm(b1, b1.t[0:96, :n], wuq.t[:, k, h * 192:h * 192 + 96], cqn.t[:, k, :n], start=(k == 0), stop=(k == 1),
                             reads=[wuq, cqn])
                    for k in range(2):
                        K.mm(b2, b2.t[0:96, :n], wuq.t[:, k, h * 192 + 96:h * 192 + 192], cqn.t[:, k, :n], start=(k == 0),
                             stop=(k == 1), reads=[wuq, cqn])
                    sg_ = stage()
                    evac(sg_, sg_.t[0:64, :n], b1, b1.t[0:64, :n], eng="dve")
                    rope_out(b1, b2, 64, 96, 0, sg_, lambda p0, p1, sg_=sg_: sg_.t[p0:p1, :n])
                    K.dma(K.sp, self.QA[h, :, t0:t0 + n], sg_.t[0:96, :n], reads=[sg_])
                if DBG_STEP < 8:
                    continue
                for i in range(2):
                    b = bank()
                    K.mm(b, b.t[:, :n], wukv.t[:, i * 128:(i + 1) * 128], ckvn.t[:, 0, :n], start=True, stop=True,
                         reads=[wukv, ckvn])
                    sg_ = stage()
                    evac(sg_, sg_.t[:, :n], b, b.t[:, :n])
                    for hh in range(2):
                        K.dma(K.sp, self.KA[2 * i + hh, 0:64, t0:t0 + n], sg_.t[64 * hh:64 * hh + 64, :n], reads=[sg_])
                if DBG_STEP < 9:
                    continue
                for sblk in range(n // 128):
                    tok = slice(sblk * 128, (sblk + 1) * 128)
                    r0 = t0 + sblk * 128
                    b = bank()
                    K.mm(b, b.t[:, 0:256], ckvn.t[:, 0, tok], wukv.t[:, 256:512], start=True, stop=True, reads=[wukv, ckvn])
                    vs = vstage()
                    if DBG_STEP >= 10:
                        evac(vs, vs.t[:, :].rearrange("p (h e) -> p h e", e=65)[:, :, 0:64],
                             b, b.t[:, 0:256].rearrange("p (h e) -> p h e", e=64))
                    if DBG_STEP >= 11:
                        K.dma(K.sp, self.VA[r0:r0 + 128, :], vs.t[:, :], reads=[vs])
                    if DBG_STEP < 12:
                        continue
                    bA, bB = bank(), bank()
                    for k in range(8):
                        K.mm(bA, bA.t[:, 0:512], u.t[:, k, tok], win.t[:, k, 3392:3904], start=(k == 0), stop=(k == 7),
                             reads=[win, u])
                    for k in range(8):
                        K.mm(bB, bB.t[:, 0:128], u.t[:, k, tok], win.t[:, k, 3904:4032], start=(k == 0), stop=(k == 7),
                             reads=[win, u])
                    if sblk == 0 and ti + 2 < NTI:
                        p_load(ti + 2)
                    for (bb, c0, nh, dst) in [(bA, 0, 4, self.VB), (bA, 256, 4, self.VC), (bB, 0, 2, self.VD)]:
                        vs = vstage()
                        if DBG_STEP < 13:
                            continue
                        evac(vs, vs.t[:, 0:nh * 65].rearrange("p (h e) -> p h e", e=65)[:, :, 0:64],
                             bb, bb.t[:, c0:c0 + nh * 64].rearrange("p (h e) -> p h e", e=64),
                             eng=("dve" if bb is bA else "act"))
                        if DBG_STEP < 14:
                            continue
                        K.dma(K.sp, dst[r0:r0 + 128, :], vs.t[:, 0:nh * 65], reads=[vs])


    def phase_attn(self, l, mx, with_ctx):
        nc, K = self.nc, self.K
        mi = "ABCD".index(mx)
        lam_init = 0.8 - 0.6 * math.exp(-0.3 * l)
        with ExitStack() as es:
            nvh = 2 if mx == "D" else 4
            vt = K.sb(es, "vt", [128, 66, nvh * 65], BF16)
            Vsrc = {"A": self.VA, "B": self.VB, "C": self.VC, "D": self.VD}[mx]
            vsrc = Vsrc.rearrange("(c p) f -> p c f", p=128)
            for c0 in range(0, 66, 11):
                K.dma(K.sp, vt.t[:, c0:c0 + 11, :], vsrc[:, c0:c0 + 11, :], nosync=[vt])
            if mx == "A":
                kt = [K.sb(es, f"kt{h}", [96, NT], BF16) for h in range(4)]
                for h in range(4):
                    K.dma(K.sp, kt[h].t[:, :], self.KA[h], writes=[kt[h]])
                qsb = [K.sb(es, f"q{i}", [96, 4, 512], BF16) for i in range(2)]
                Qsrc = self.QA
            else:
                Ksrc = {"B": self.KB_, "C": self.KC, "D": self.KD}[mx]
                kt = [K.sb(es, f"kt{h}", [128, NT], BF16) for h in range(2)]
                for h in range(2):
                    K.dma(K.sp, kt[h].t[:, :], Ksrc[h], writes=[kt[h]])
                qsb = [K.sb(es, f"q{i}", [128, 2, 512], BF16) for i in range(2)]
                Qsrc = {"B": self.QB, "C": self.QC, "D": self.QD}[mx]
            ptiles = [K.sb(es, f"pt{i}", [128, 1024], BF16) for i in range(4)]
            rd = [K.sb(es, f"rd{i}", [128, 512], F32) for i in range(2)]
            rbs = [K.sb(es, f"rbs{i}", [128, 512], F32) for i in range(2)]
            yst = [K.sb(es, f"yst{i}", [128, 512], BF16) for i in range(3)]
            for r_ in rd:
                K.op(K.dve, lambda r_=r_: nc.vector.memset(r_.t[:], 0.0), writes=[r_])
            if mx != "A":
                nrng = 4 if mx == "B" else 2
                qm = [[K.sb(es, f"qm{a}_{b_}", [128, 512], BF16) for b_ in range(2)] for a in range(nrng)]
                for a in range(nrng):
                    for b_ in range(2):
                        K.op(K.pool, lambda a=a, b_=b_: nc.gpsimd.memset(qm[a][b_].t[:], 0.0), writes=[qm[a][b_]])
                qmc = [0] * nrng

                def masked_q(q, c2, p0, W, n):
                    a = p0 // W
                    t = qm[a][qmc[a] % 2]
                    qmc[a] += 1
                    K.op(K.pool, lambda: nc.gpsimd.tensor_copy(out=t.t[p0:p0 + W, :n], in_=q.t[p0:p0 + W, c2, :n]),
                         reads=[q], writes=[t])
                    return t
            sgroups = [Tk(K.pfull[:, 1024 * g:1024 * (g + 1)]) for g in range(3)]
            ybanks = K.psum[6:8]

            def next_bb():
                g_ = sgroups[cnt["s"] % 3]
                cnt["s"] += 1
                return g_
            cnt = {"y": 0, "b": 0, "yst": 0, "tmp": 0, "s": 0, "p": 0}
            if mx == "C":
                biasf = K.sb(es, "biasf", [128, 8, 512], F32)
                biasb = [K.sb(es, f"biasb{i}", [128, 8, 512], BF16) for i in range(2)]
            if mx == "D":
                maskf = K.sb(es, "maskf", [128, 6, 512], F32)
                maskt = K.sb(es, "maskt", [128, 6, 512], BF16)
                K.dma(K.sp, maskf.t[:], self.win_mask[:, :, :], writes=[maskf])
                K.op(K.pool, lambda: nc.gpsimd.tensor_copy(out=maskt.t[:], in_=maskf.t[:]), reads=[maskf], writes=[maskt])
                esk = K.sb(es, "esk", [128, 4], F32)
                K.dma(K.sp, esk.t[64:65, :], self.sink_in[l], writes=[esk])
                K.op(K.act, lambda: nc.scalar.activation(out=esk.t[64:65, :], in_=esk.t[64:65, :], func=AF.Exp),
                     reads=[esk], writes=[esk])
            if mx == "B":
                lamt = K.sb(es, "lamt", [128, 128], F32)
                lprod = K.sb(es, "lprod", [128, 64], F32)
                lsum = K.sb(es, "lsum", [128, 2], F32)
                nlam = K.sb(es, "nlam", [128, 1], F32)
                subg = K.sb(es, "subg", [64, 1], F32)
                y1n = K.sb(es, "y1n", [64, 512], F32)
                t2 = K.sb(es, "t2", [64, 512], F32)
                yd = K.sb(es, "yd", [64, 512], F32)
                sq = K.sb(es, "bsq", [128, 512], BF16)
                K.op(K.dve, lambda: nc.vector.memset(sq.t[:], 0.0), writes=[sq])
                brt = K.sb(es, "brt", [64, 512], F32)
                brs = K.sb(es, "brs", [64, 512], F32)
                K.dma(K.sp, lamt.t[64:65, :], self.lam_in[l], writes=[lamt])
                K.dma(K.sp, subg.t[:, :], self.subln[l], writes=[subg])
                lv = lamt.t[64:65, :].rearrange("p (a b c) -> p a b c", a=2, b=2)
                K.op(K.dve, lambda: nc.vector.tensor_tensor(out=lprod.t[64:65, :].rearrange("p (a c) -> p a c", a=2),
                                                            in0=lv[:, :, 0, :], in1=lv[:, :, 1, :], op=ALU.mult),
                     reads=[lamt], writes=[lprod])
                K.op(K.dve, lambda: nc.vector.reduce_sum(out=lsum.t[64:65, :], in_=lprod.t[64:65, :].rearrange("p (a c) -> p a c", a=2),
                                                         axis=mybir.AxisListType.X), reads=[lprod], writes=[lsum])
                K.op(K.act, lambda: nc.scalar.activation(out=lsum.t[64:65, :], in_=lsum.t[64:65, :], func=AF.Exp),
                     reads=[lsum], writes=[lsum])
                K.op(K.dve, lambda: nc.vector.tensor_tensor(out=nlam.t[64:65, :], in0=lsum.t[64:65, 1:2], in1=lsum.t[64:65, 0:1],
                                                            op=ALU.subtract), reads=[lsum], writes=[nlam])
                K.op(K.dve, lambda: nc.vector.tensor_scalar(out=nlam.t[64:65, :], in0=nlam.t[64:65, :], scalar1=-lam_init,
                                                            scalar2=None, op0=ALU.add), reads=[nlam], writes=[nlam])
                if "lam_dbg" in self.debug:
                    dbg = self.nc.dram_tensor("lam_dbg", [1, 1], F32, kind="ExternalOutput").ap()
                    K.dma(K.sp, dbg[:, :], nlam.t[64:65, :], reads=[nlam])

            def gview(t, G, n):
                return t.t[:, 0:G * 512].rearrange("p (g x) -> p g x", x=512)[:, :, :n]

            def core(n, q_ap, q_tk, kT_fn, k_tk, v_fn, groups, scale, ybank, M, bias_fn=None, after=None):
                ng = len(groups)
                total = sum(len(g) for g in groups)
                hist = []
                done = 0
                LA = 2
                for i in range(ng + LA):
                    if i < ng:
                        gl = groups[i]
                        G = len(gl)
                        sg = sgroups[cnt["s"] % 3]
                        cnt["s"] += 1
                        bias = bias_fn(i) if bias_fn is not None else None
                        for a, c in enumerate(gl):
                            K.mm(sg, sg.t[:, a * 512:a * 512 + n], kT_fn(c), q_ap, True, bias is None, reads=[q_tk, k_tk])
                            if bias is not None:
                                K.mm(sg, sg.t[:, a * 512:a * 512 + n], self.ident.t[:, :], bias[1](a), False, True,
                                     reads=[bias[0], self.ident])
                        p = ptiles[cnt["p"] % 4]
                        cnt["p"] += 1
                        if True:
                            K.op(K.act, lambda sg=sg, p=p, G=G: nc.scalar.activation(out=gview(p, G, n), in_=gview(sg, G, n),
                                                                                     func=AF.Exp, scale=scale),
                                 reads=[sg], writes=[p])
                        hist.append((p, gl))
                    if i >= LA:
                        p, gl = hist[i - LA]
                        for a, c in enumerate(gl):
                            K.mm(ybank, ybank.t[0:M, :n], v_fn(c), p.t[:, a * 512:a * 512 + n], start=(done == 0),
                                 stop=(done == total - 1), reads=[vt, p])
                            done += 1
                    if i == min(6, ng) and after is not None:
                        after()
                        after = None
                if after is not None:
                    after()

            def finalize(n, ybank, out_tk, out_ap, den_add=None, mul=None):
                r_ = rd[cnt["b"] % 2]
                rb_ = rbs[cnt["b"] % 2]
                bb = next_bb()
                cnt["b"] += 1
                if den_add is not None:
                    K.op(K.dve, lambda: nc.vector.tensor_scalar(out=r_.t[64:65, :n], in0=ybank.t[64:65, :n], scalar1=den_add,
                                                                scalar2=None, op0=ALU.add), reads=[ybank], writes=[r_])
                    K.op(K.dve, lambda: nc.vector.reciprocal(out=r_.t[64:65, :n], in_=r_.t[64:65, :n]), reads=[r_], writes=[r_])
                else:
                    K.op(K.dve, lambda: nc.vector.reciprocal(out=r_.t[64:65, :n], in_=ybank.t[64:65, :n]), reads=[ybank], writes=[r_])
                if mul is not None:
                    K.op(K.dve, lambda: nc.vector.tensor_scalar(out=r_.t[64:65, :n], in0=r_.t[64:65, :n], scalar1=mul,
                                                                scalar2=None, op0=ALU.mult), reads=[r_], writes=[r_])
                K.mm(bb, bb.t[:, :n], self.ones_f.t[:, :], r_.t[:, :n], True, True, reads=[r_, self.ones_f])
                K.op(K.act, lambda: nc.scalar.activation(out=rb_.t[0:64, :n], in_=bb.t[0:64, :n], func=AF.Copy),
                     reads=[bb], writes=[rb_])
                K.op(K.dve, lambda: nc.vector.tensor_tensor(out=out_ap, in0=ybank.t[0:64, :n], in1=rb_.t[0:64, :n], op=ALU.mult),
                     reads=[ybank, rb_], writes=[out_tk])

            def next_y():
                b = ybanks[cnt["y"] % 2]
                cnt["y"] += 1
                return b

            def next_yst():
                t = yst[cnt["yst"] % 3]
                cnt["yst"] += 1
                return t

            def pairs(lst):
                return [lst[i:i + 2] for i in range(0, len(lst), 2)]

            def store_y(h, t0, n, ys):
                K.dma(K.sp, self.Y[mi, h * 64:(h + 1) * 64, t0:t0 + n], ys.t[0:64, :n], reads=[ys])

            tis = [ti for ti in range(len(TILES)) if not (ti == 0 and not with_ctx)]
            if mx == "C":
                order = [(ti, h) for h in range(4) for ti in tis]
            else:
                order = [(ti, h) for ti in tis for h in range(4)]
            units = []
            for (ti, h) in order:
                if mx == "B":
                    units.append({"ti": ti, "h": h, "m": 0})
                    units.append({"ti": ti, "h": h, "m": 1})
                else:
                    units.append({"ti": ti, "h": h, "m": 0})
            pending = [None]
            last_q = [None, None]
            qctr = [0]
            last_bias = [None, None]
            bctr = [0]
            allch = list(range(66))

            def prep(U_):
                ti, h, m_ = U_["ti"], U_["h"], U_["m"]
                t0, n = TILES[ti]
                if last_q[0] != ti:
                    q = qsb[qctr[0] % 2]
                    qctr[0] += 1
                    if mx == "A":
                        K.dma(K.sp, q.t[:, :, :n], Qsrc[:, :, t0:t0 + n].rearrange("h p t -> p h t"), writes=[q])
                    else:
                        K.dma(K.sp, q.t[:, :, :n], Qsrc[:, :, t0:t0 + n].rearrange("c p t -> p c t"), writes=[q])
                    last_q[0], last_q[1] = ti, q
                q = last_q[1]
                U_["q"] = q
                if mx == "B":
                    v_ = 2 * h + m_
                    c2, pm = v_ // 4, v_ % 4
                    U_["c2"] = c2
                    U_["qt"] = masked_q(q, c2, 32 * pm, 32, n)
                elif mx in "CD":
                    c2, p0 = h // 2, 64 * (h % 2)
                    U_["c2"] = c2
                    U_["qt"] = masked_q(q, c2, p0, 64, n)
                if mx == "C" and ti != 0:
                    i_lat = ti - 1
                    typ = 0 if i_lat == 0 else (2 if i_lat == 15 else 1)
                    if last_bias[0] != (typ, h):
                        bt = biasb[bctr[0] % 2]
                        bctr[0] += 1
                        K.dma(K.sp, biasf.t[:, :, :], self.na_bias[l, typ, h], writes=[biasf])
                        K.op(K.dve, lambda bt=bt: nc.vector.tensor_scalar(out=bt.t[:], in0=biasf.t[:], scalar1=1.0 / NA_SCALE,
                                                                         scalar2=None, op0=ALU.mult),
                             reads=[biasf], writes=[bt])
                        last_bias[0], last_bias[1] = (typ, h), bt
                    U_["bt"] = last_bias[1]

            def run(U_):
                ti, h, m_ = U_["ti"], U_["h"], U_["m"]
                t0, n = TILES[ti]
                i_lat = ti - 1
                q = U_["q"]
                if mx == "A":
                    groups = pairs([0, 1] if ti == 0 else allch)
                    yb = next_y()
                    core(n, q.t[0:96, h, :n], q, lambda c, h=h: kt[h].t[0:96, c * 128:(c + 1) * 128], kt[h],
                         lambda c, h=h: vt.t[:, c, h * 65:h * 65 + 65], groups, MLA_SCALE, yb, 65, after=pending[0])

                    def fin(n=n, yb=yb, h=h, t0=t0):
                        ys = next_yst()
                        finalize(n, yb, ys, ys.t[0:64, :n])
                        store_y(h, t0, n, ys)
                    pending[0] = fin
                elif mx == "B":
                    groups = pairs([0, 1] if ti == 0 else allch)
                    c2 = U_["c2"]
                    yb = next_y()
                    qt_ = U_["qt"]
                    core(n, qt_.t[:, :n], qt_,
                         lambda c, c2=c2: kt[c2].t[:, c * 128:(c + 1) * 128], kt[c2],
                         lambda c, h=h: vt.t[:, c, h * 65:h * 65 + 65], groups, DIFF_SCALE, yb, 65, after=pending[0])
                    if m_ == 0:
                        def fin(n=n, yb=yb):
                            finalize(n, yb, y1n, y1n.t[0:64, :n])
                    else:
                        def fin(n=n, yb=yb, h=h, t0=t0):
                            finalize(n, yb, t2, t2.t[0:64, :n], mul=nlam.t[64:65, 0:1])
                            K.op(K.pool, lambda: nc.gpsimd.tensor_tensor(out=yd.t[:, :n], in0=y1n.t[:, :n], in1=t2.t[:, :n],
                                                                         op=ALU.add), reads=[y1n, t2], writes=[yd])
                            K.op(K.act, lambda: nc.scalar.activation(out=sq.t[0:64, :n], in_=yd.t[:, :n], func=AF.Square),
                                 reads=[yd], writes=[sq])
                            bb = next_bb()
                            K.mm(bb, bb.t[:, :n], self.ones_bf.t[:, :], sq.t[:, :n], True, True, reads=[sq, self.ones_bf])
                            li = 1.0 - lam_init
                            K.op(K.act, lambda bb=bb: nc.scalar.activation(out=brt.t[:, :n], in_=bb.t[0:64, :n], func=AF.Sqrt,
                                                                          scale=1.0 / (64 * li * li),
                                                                          bias=self.eps_t.t[0:64, 2 + l:3 + l]),
                                 reads=[bb, self.eps_t], writes=[brt])
                            K.op(K.dve, lambda: nc.vector.reciprocal(out=brs.t[:, :n], in_=brt.t[:, :n]), reads=[brt], writes=[brs])
                            ys = next_yst()
                            K.op(K.dve, lambda ys=ys: nc.vector.scalar_tensor_tensor(out=ys.t[0:64, :n], in0=yd.t[:, :n],
                                                                                    scalar=subg.t[:, 0:1], in1=brs.t[:, :n],
                                                                                    op0=ALU.mult, op1=ALU.mult),
                                 reads=[yd, brs, subg], writes=[ys])
                            store_y(h, t0, n, ys)
                    pending[0] = fin
                elif mx == "C":
                    c2 = U_["c2"]
                    bias_fn = None
                    if ti == 0:
                        groups = pairs([0, 1])
                    else:
                        fc = min(max(4 * i_lat - 2, 0), 56)
                        groups = pairs([0, 1] + [2 + fc + kc for kc in range(8)])
                        bt = U_["bt"]
                        bias_fn = (lambda i, bt=bt, n=n: None if i < 1 else
                                   (bt, lambda a, i=i: bt.t[:, 2 * (i - 1) + a, :n]))
                    yb = next_y()
                    qt_ = U_["qt"]
                    core(n, qt_.t[:, :n], qt_,
                         lambda c, c2=c2: kt[c2].t[:, c * 128:(c + 1) * 128], kt[c2],
                         lambda c, h=h: vt.t[:, c, h * 65:h * 65 + 65], groups, NA_SCALE, yb, 65, bias_fn=bias_fn,
                         after=pending[0])

                    def fin(n=n, yb=yb, h=h, t0=t0):
                        ys = next_yst()
                        finalize(n, yb, ys, ys.t[0:64, :n])
                        store_y(h, t0, n, ys)
                    pending[0] = fin
                else:
                    g = h // 2
                    bias_fn = None
                    if ti == 0:
                        groups = pairs([0, 1])
                    else:
                        rel = [kc for kc in range(6) if 0 <= 4 * i_lat - 1 + kc < 64]
                        groups = [[0, 1]] + pairs([2 + 4 * i_lat - 1 + kc for kc in rel])
                        relp = pairs(rel)
                        bias_fn = (lambda i, relp=relp, n=n: None if i < 1 else
                                   (maskt, lambda a, i=i: maskt.t[:, relp[i - 1][a], :n]))
                    yb = next_y()
                    qt_ = U_["qt"]
                    core(n, qt_.t[:, :n], qt_,
                         lambda c, g=g: kt[g].t[:, c * 128:(c + 1) * 128], kt[g],
                         lambda c, g=g: vt.t[:, c, g * 65:g * 65 + 65], groups, GQA_SCALE, yb, 65, bias_fn=bias_fn,
                         after=pending[0])

                    def fin(n=n, yb=yb, h=h, t0=t0):
                        ys = next_yst()
                        finalize(n, yb, ys, ys.t[0:64, :n], den_add=esk.t[64:65, h:h + 1])
                        store_y(h, t0, n, ys)
                    pending[0] = fin

            prep(units[0])
            for ui, U_ in enumerate(units):
                if ui + 1 < len(units):
                    prep(units[ui + 1])
                run(U_)
            if pending[0] is not None:
                pending[0]()

    def phase_merge(self, l, with_ctx):
        nc, K = self.nc, self.K
        with ExitStack() as es:
            wg = K.sb(es, "wg", [128, 4, 8, D], BF16)
            wb = K.sb(es, "wb", [128, 4, 2, D], BF16)
            wo = K.sb(es, "wo", [128, 8, D], BF16)
            bg = K.sb(es, "bg", [128, 4, 8], F32)
            stg = [K.sb(es, f"mstg{i}", [128, 2048], F32) for i in range(2)]
            ub = [K.sb(es, f"u{i}", [128, 8, 512], BF16) for i in range(2)]
            ytb = [K.sb(es, f"yt{i}", [128, 4, 2, 512], BF16) for i in range(2)]
            mer = K.sb(es, "mer", [128, 8, 512], BF16)
            sgt = [K.sb(es, f"sgt{i}", [128, 512], F32) for i in range(2)]
            tmp = [K.sb(es, f"mtmp{i}", [128, 512], F32) for i in range(2)]
            macc = [K.sb(es, f"macc{i}", [128, 512], F32) for i in range(2)]
            xr = [K.sb(es, f"xr{i}", [128, 512], F32) for i in range(3)]
            K.dma(K.sp, bg.t[:], self.b_gate[l], writes=[bg])
            idx = 0
            for i in range(4):
                src = self.w_gate[l, i].rearrange("(k p) n -> p k n", p=128)
                for k in range(0, 8, 2):
                    K.load_cast(stg, wg, wg.t[:, i, k:k + 2, :], src[:, k:k + 2, :], idx)
                    idx += 1
                srcb = self.w_branch[l, i].rearrange("(k p) n -> p k n", p=128)
                K.load_cast(stg, wb, wb.t[:, i, :, :], srcb, idx)
                idx += 1
            srco = self.w_out[l].rearrange("(k p) n -> p k n", p=128)
            for k in range(0, 8, 2):
                K.load_cast(stg, wo, wo.t[:, k:k + 2, :], srco[:, k:k + 2, :], idx)
                idx += 1
            K.barrier()
            tis = [ti for ti in range(len(TILES)) if not (ti == 0 and not with_ctx)]

            def load_in(ii):
                t0, n = TILES[tis[ii]]
                u, yt = ub[ii % 2], ytb[ii % 2]
                K.dma(K.sp, u.t[:, :, :n], self.U[:, t0:t0 + n].rearrange("(k p) n -> p k n", p=128), writes=[u])
                for i in range(4):
                    K.dma(K.sp, yt.t[:, i, :, :n], self.Y[i, :, t0:t0 + n].rearrange("(k p) n -> p k n", p=128),
                          writes=[yt] if i == 0 else [], nosync=[yt] if i else [])

            load_in(0)
            ctr = 0
            xrc = 0
            for ii, ti in enumerate(tis):
                t0, n = TILES[ti]
                v = 1 if ti == 0 else 0
                u, yt = ub[ii % 2], ytb[ii % 2]
                if ii + 1 < len(tis):
                    load_in(ii + 1)
                for c in range(8):
                    cs = slice(c * 128, (c + 1) * 128)
                    ma = macc[c % 2]
                    for i in range(4):
                        pg = K.psum[(ctr % 2)]
                        pb = K.psum[2 + (ctr % 2)]
                        ctr += 1
                        for k in range(8):
                            K.mm(pg, pg.t[:, :n], wg.t[:, i, k, cs], u.t[:, k, :n], start=(k == 0), stop=(k == 7), reads=[wg, u])
                        for k in range(2):
                            K.mm(pb, pb.t[:, :n], wb.t[:, i, k, cs], yt.t[:, i, k, :n], start=(k == 0), stop=(k == 1), reads=[wb, yt])
                        sg_ = sgt[ctr % 2]
                        K.op(K.act, lambda pg=pg, sg_=sg_, i=i, c=c: nc.scalar.activation(
                            out=sg_.t[:, :n], in_=pg.t[:, :n], func=AF.Sigmoid, bias=bg.t[:, i, c:c + 1]),
                            reads=[pg, bg], writes=[sg_])
                        if i == 0:
                            K.op(K.dve, lambda pb=pb, sg_=sg_, ma=ma: nc.vector.tensor_tensor(
                                out=ma.t[:, :n], in0=sg_.t[:, :n], in1=pb.t[:, :n], op=ALU.mult),
                                reads=[sg_, pb], writes=[ma])
                        else:
                            tp = tmp[ctr % 2]
                            K.op(K.dve, lambda pb=pb, sg_=sg_, tp=tp: nc.vector.tensor_tensor(
                                out=tp.t[:, :n], in0=sg_.t[:, :n], in1=pb.t[:, :n], op=ALU.mult),
                                reads=[sg_, pb], writes=[tp])
                            if i < 3:
                                K.op(K.pool, lambda tp=tp, ma=ma: nc.gpsimd.tensor_tensor(
                                    out=ma.t[:, :n], in0=ma.t[:, :n], in1=tp.t[:, :n], op=ALU.add),
                                    reads=[tp, ma], writes=[ma])
                            else:
                                K.op(K.pool, lambda tp=tp, ma=ma, c=c: nc.gpsimd.tensor_tensor(
                                    out=mer.t[:, c, :n], in0=ma.t[:, :n], in1=tp.t[:, :n], op=ALU.add),
                                    reads=[tp, ma], writes=[mer] if c == 0 else [], nosync=[mer] if c else [])
                for c2 in range(8):
                    po = K.psum[4 + (c2 % 2)]
                    x_ = xr[xrc % 3]
                    xrc += 1
                    K.dma(K.sp, x_.t[:, :n], self.h[c2 * 128:(c2 + 1) * 128, t0:t0 + n], writes=[x_])
                    for c in range(8):
                        K.mm(po, po.t[:, :n], wo.t[:, c, c2 * 128:(c2 + 1) * 128], mer.t[:, c, :n], start=(c == 0), stop=(c == 7),
                             reads=[wo, mer])
                    K.op(K.dve, lambda c2=c2, po=po, x_=x_: nc.vector.scalar_tensor_tensor(
                        out=x_.t[:, :n], in0=po.t[:, :n], scalar=self.gt.t[:, l, v, 1, c2:c2 + 1],
                        in1=x_.t[:, :n], op0=ALU.mult, op1=ALU.add),
                        reads=[po, self.gt, x_], writes=[x_])
                    K.dma(K.sp, self.h[c2 * 128:(c2 + 1) * 128, t0:t0 + n], x_.t[:, :n], reads=[x_])

    def phase_final(self):
        nc, K = self.nc, self.K
        with ExitStack() as es:
            xin = [K.sb(es, f"fx{i}", [128, 8, 512], F32) for i in range(2)]
            fn = K.sb(es, "fn", [128, 8], F32)
            nb = self.norm_bufs(es, "n", 0)
            sq, xn, rt, rstd, ps_ss = nb
            K.dma(K.sp, fn.t[:], self.fnorm[:, :], writes=[fn])
            for ti, (t0, n) in enumerate(TILES):
                if ti == 0:
                    continue
                x = xin[ti % 2]
                K.dma(K.sp, x.t[:, :, :n], self.h[:, t0:t0 + n].rearrange("(k p) n -> p k n", p=128),
                      reads=[self.h_tk[ti]], writes=[x])
                for k in range(8):
                    s = sq[k % 2]
                    K.op(K.act, lambda k=k, s=s: nc.scalar.activation(out=s.t[:, :n], in_=x.t[:, k, :n], func=AF.Square),
                         reads=[x], writes=[s])
                    K.mm(ps_ss, ps_ss.t[:, :n], self.ones_bf.t[:, :], s.t[:, :n], start=(k == 0), stop=(k == 7),
                         reads=[s, self.ones_bf])
                K.op(K.act, lambda: nc.scalar.activation(out=rt.t[:, :n], in_=ps_ss.t[:, :n], func=AF.Sqrt,
                                                         scale=1.0 / D, bias=self.eps_t.t[:, 0:1]),
                     reads=[ps_ss, self.eps_t], writes=[rt])
                K.op(K.dve, lambda: nc.vector.reciprocal(out=rstd.t[:, :n], in_=rt.t[:, :n]), reads=[rt], writes=[rstd])
                for k in range(8):
                    K.op(K.dve, lambda k=k: nc.vector.scalar_tensor_tensor(
                        out=x.t[:, k, :n], in0=x.t[:, k, :n], scalar=fn.t[:, k:k + 1], in1=rstd.t[:, :n],
                        op0=ALU.mult, op1=ALU.mult), reads=[x, rstd, fn], writes=[x])
                K.dma(K.sp, self.out[:, t0 - L:t0 - L + n].rearrange("(k p) n -> p k n", p=128), x.t[:, :, :n],
                      reads=[x])


def _rope_tables():
    f = np.float32
    t = np.arange(S)
    rows = (t // GRID_W).astype(f)
    cols = (t % GRID_W).astype(f)

    def tabs(dim):
        half = dim // 2
        freqs = np.power(f(10000.0), -np.arange(0, half, 2, dtype=f) / f(half)).astype(f)
        ar = (rows[:, None] * freqs).astype(f)
        ac = (cols[:, None] * freqs).astype(f)
        q = dim // 4
        cos = np.concatenate([np.cos(ar), np.cos(ar), np.cos(ac), np.cos(ac)], axis=1)
        sin = np.concatenate([-np.sin(ar), np.sin(ar), -np.sin(ac), np.sin(ac)], axis=1)
        cos = np.concatenate([np.ones((L, dim), f), cos.astype(f)], axis=0)
        sin = np.concatenate([np.zeros((L, dim), f), sin.astype(f)], axis=0)
        rep = 128 // dim
        return np.tile(cos.T, (rep, 1)), np.tile(sin.T, (rep, 1))

    c32, s32 = tabs(32)
    c64, s64 = tabs(64)
    return np.ascontiguousarray(np.stack([c32, s32, c64, s64], axis=0).astype(f))


def _swap_idx(dim, nvec):
    q = dim // 4
    base = np.arange(dim)
    partner = np.where((base % (2 * q)) < q, base + q, base - q)
    return np.concatenate([v * dim + partner for v in range(nvec)])


def _na_bias(rpb):
    f = np.float32
    out = np.full((3, 4, 128, 8, 512), NEG, f)
    qr = np.arange(8)[:, None].repeat(64, 1).reshape(-1)
    qc = np.tile(np.arange(64), 8)
    p = np.arange(128)
    for typ, (r0, row0) in enumerate([(0, 0), (8, 4), (120, 112)]):
        r = r0 + qr
        rs = np.clip(r - 4, 0, 120)
        cs = np.clip(qc - 8, 0, 48)
        for kc in range(8):
            kr = row0 + 2 * kc + p // 64
            kcol = p % 64
            ok = ((kr[:, None] >= rs[None, :]) & (kr[:, None] < rs[None, :] + 8)
                  & (kcol[:, None] >= cs[None, :]) & (kcol[:, None] < cs[None, :] + 16))
            dr = np.clip(kr[:, None] - r[None, :] + 7, 0, 14)
            dc = np.clip(kcol[:, None] - qc[None, :] + 15, 0, 30)
            for h in range(4):
                vals = rpb[h][dr, dc]
                out[typ, h, :, kc, :] = np.where(ok, vals, f(NEG))
    return out


def _win_mask():
    f = np.float32
    p = np.arange(128)[:, None]
    q = np.arange(512)[None, :]
    m = np.zeros((128, 6, 512), f)
    for kc in range(6):
        j = 128 * kc - 128 + p
        m[:, kc, :] = np.where(np.abs(q - j) <= 128, f(0.0), f(NEG))
    return m


def prep_shared(inp):
    f = np.float32
    m = {}
    m["w_ada"] = inp["w_ada"]
    m["b_ada"] = np.ascontiguousarray(inp["b_ada"].reshape(DEPTH, 72, 128).transpose(0, 2, 1))
    nr = np.stack([inp["norm_ffn1"], inp["norm_mix"], inp["norm_ffn2"]], axis=1)
    m["norms"] = np.ascontiguousarray(nr.reshape(DEPTH, 3, 8, 128).transpose(0, 3, 1, 2))
    m["fnorm"] = np.ascontiguousarray(inp["final_norm"].reshape(8, 128).T)
    for k in ("ffn1_w_gu", "ffn2_w_gu", "ffn1_w_down", "ffn2_w_down", "w_gate", "w_branch", "w_out"):
        m[k] = inp[k]
    w = inp["w_in"]
    sw32 = lambda c0, nv: w[:, :, c0 + _swap_idx(32, nv)]
    sw64 = lambda c0, nv: w[:, :, c0 + _swap_idx(64, nv)]
    m["w_in_ext"] = np.ascontiguousarray(np.concatenate(
        [w, sw32(384, 1), sw32(416, 8), sw32(672, 8), sw64(1952, 4), sw64(2208, 2),
         w[:, :, 928:1184], w[:, :, 1696:1952], w[:, :, 2336:2464]], axis=2))
    wq = inp["mla_w_uq"].reshape(DEPTH, 256, 4, 96)
    wq_sw = np.concatenate([wq[..., :64], wq[..., 64 + _swap_idx(32, 1)]], axis=-1)
    m["w_uq_ext"] = np.ascontiguousarray(np.concatenate([wq, wq_sw], axis=-1).reshape(DEPTH, 256, 768))
    wkv = inp["mla_w_ukv"].reshape(DEPTH, 128, 4, 128)
    m["w_ukv_ext"] = np.ascontiguousarray(np.concatenate(
        [wkv[..., :64].reshape(DEPTH, 128, 256), wkv[..., 64:].reshape(DEPTH, 128, 256)], axis=-1))
    qn = inp["mla_q_norm"].reshape(DEPTH, 2, 128).transpose(0, 2, 1)
    m["qkn"] = np.ascontiguousarray(np.concatenate([qn, inp["mla_kv_norm"][:, :, None]], axis=-1))
    m["rope_tab"] = _rope_tables()
    m["diff_lam"] = np.ascontiguousarray(inp["diff_lam"].reshape(DEPTH, 1, 128))
    m["diff_subln"] = np.ascontiguousarray(inp["diff_subln"].reshape(DEPTH, 64, 1))
    m["gqa_sink"] = np.ascontiguousarray(inp["gqa_sink"].reshape(DEPTH, 1, 4))
    m["na_bias"] = np.stack([_na_bias(inp["na_rpb"][l]) for l in range(DEPTH)], axis=0)
    m["win_mask"] = _win_mask()
    m["ident"] = np.eye(128, dtype=f)
    m["b_gate"] = np.ascontiguousarray(inp["b_gate"].reshape(DEPTH, 4, 8, 128).transpose(0, 3, 1, 2))
    return {k: np.ascontiguousarray(v.astype(f)) for k, v in m.items()}


def prep_core(inp, b):
    f = np.float32
    m = {}
    m["hT"] = np.ascontiguousarray(np.concatenate([inp["ctx"][b], inp["x"][b]], axis=0).T.astype(f))
    cc = np.stack([inp["c"][b], inp["c_ctx"]], axis=-1)
    m["c_in"] = np.ascontiguousarray(cc.reshape(8, 128, 2).transpose(1, 0, 2).astype(f))
    return m


def prep_inputs(inp, b):
    m = prep_shared(inp)
    m.update(prep_core(inp, b))
    return m


_CACHE = {}


def kernel(**inputs):
    inp = {k: np.asarray(v) for k, v in inputs.items()}
    if "nc" not in _CACHE:
        p = Prog()
        _CACHE["nc"] = p.build()
        _CACHE["names"] = list(p.inputs.keys())
    nc = _CACHE["nc"]
    in_maps = []
    shared = prep_shared(inp)
    for b in range(8):
        m = dict(shared)
        m.update(prep_core(inp, b))
        in_maps.append({k: m[k] for k in _CACHE["names"]})
    res = run_bass_kernel_spmd(nc, in_maps, core_ids=list(range(8)))
    out = np.stack([np.ascontiguousarray(res.results[b]["outT"].T) for b in range(8)], axis=0)
    return out.astype(np.float32)
```
